# Optimizing a Trainium2 kernel written in Bass

```python
import math
import jax, jax.numpy as jnp
from jax import lax
import numpy as np


D_MODEL = 1024
BATCH = 8
SEQ = 2048
DEPTH = 2
DEC_BATCH = 8
DEC_SEQ = 32
PAST_LEN = 4096

CHUNK = 64
QBLK = 128
D_MIX = D_MODEL
A_WIDTH = D_MIX // 2
A_HEAD_DIM = 64
A_HEADS = A_WIDTH // (2 * A_HEAD_DIM)
A_VDIM = 2 * A_HEAD_DIM
ATTN_SCALE = A_HEAD_DIM ** -0.5
B_WIDTH = D_MIX // 4
POOL_WINDOWS = (2, 4, 8, 16)
N_POOL_GROUPS = len(POOL_WINDOWS)
POOL_GROUP = B_WIDTH // N_POOL_GROUPS
POOL_HIST = max(POOL_WINDOWS) - 1
C_WIDTH = D_MIX // 4
CONV_W = 3
D_FF = 2816
ROPE_THETA = 10000.0
EPS = 1e-6
NEG_INF = -1e30
D_IN = 3 * A_WIDTH + B_WIDTH + 3 * C_WIDTH
SPLITS = [A_WIDTH, 2 * A_WIDTH, 3 * A_WIDTH, 3 * A_WIDTH + B_WIDTH,
          3 * A_WIDTH + B_WIDTH + C_WIDTH, 3 * A_WIDTH + B_WIDTH + 2 * C_WIDTH]

kernel_name = 'hybrid_stream_diffattn_pool_conv_step'

F32 = jnp.float32


def rmsnorm(x, g):
    xf = x.astype(F32)
    y = xf * lax.rsqrt(jnp.mean(xf * xf, axis=-1, keepdims=True) + EPS)
    return (y * g.astype(F32)).astype(x.dtype)


def swiglu(x, wg, wu, wd):
    return (jax.nn.silu(x @ wg) * (x @ wu)) @ wd


def rope(x, pos):
    half = A_HEAD_DIM // 2
    inv = jnp.power(ROPE_THETA, -jnp.arange(half, dtype=F32) / half)
    ang = pos.astype(F32)[:, None] * inv[None, :]
    cos = jnp.cos(ang)[None, :, None, None, :]
    sin = jnp.sin(ang)[None, :, None, None, :]
    xf = x.astype(F32)
    x1, x2 = xf[..., :half], xf[..., half:]
    return jnp.concatenate([x1 * cos - x2 * sin, x2 * cos + x1 * sin], axis=-1).astype(x.dtype)


def diff_attend(q, k, v, mask, lam):
    s = jnp.einsum('bqhcd,bkhcd->bhcqk', q.astype(F32), k) * ATTN_SCALE
    if mask is not None:
        s = jnp.where(mask, s, NEG_INF)
    p = jax.nn.softmax(s, axis=-1)
    a = p[:, :, 0] - lam * p[:, :, 1]
    return jnp.einsum('bhqk,bkhe->bqhe', a, v)


def pool_mix(u_ext, pos, w_pool, scale):
    bsz, tot, _ = u_ext.shape
    t = tot - POOL_HIST
    uf = u_ext.astype(F32)
    cs = jnp.concatenate([jnp.zeros((bsz, 1, B_WIDTH), F32), jnp.cumsum(uf, axis=1)], axis=1)
    cur = uf[:, POOL_HIST:]
    outs = []
    for gi, w in enumerate(POOL_WINDOWS):
        lo, hi = gi * POOL_GROUP, (gi + 1) * POOL_GROUP
        wsum = cs[:, POOL_HIST + 1:, lo:hi] - cs[:, POOL_HIST + 1 - w:POOL_HIST + 1 - w + t, lo:hi]
        cnt = jnp.minimum(pos + 1, w).astype(F32)[None, :, None]
        outs.append(wsum / cnt - cur[..., lo:hi])
    d = jnp.stack(outs, axis=2)
    y = jnp.einsum('btgc,gce->btge', d, w_pool.astype(F32)).reshape(bsz, t, B_WIDTH)
    return (y * scale.astype(F32)).astype(u_ext.dtype)


def short_conv(v_ext, w):
    t = v_ext.shape[1] - (CONV_W - 1)
    y = w[0] * v_ext[:, 0:t]
    for j in range(1, CONV_W):
        y = y + w[j] * v_ext[:, j:j + t]
    return y


def run_trunk(x, pos, cache_k, cache_v, state_pool, state_conv, p):
    bsz, t, _ = x.shape
    new_k, new_v, new_pool, new_conv = [], [], [], []
    for l in range(DEPTH):
        h = rmsnorm(x, p['norm_ffn1'][l])
        x = x + 0.5 * swiglu(h, p['ffn1_gate'][l], p['ffn1_up'][l], p['ffn1_down'][l])
        h = rmsnorm(x, p['norm_mix'][l])
        z = h @ p['w_in'][l]
        q, k, v, u, hc, bg, cg = jnp.split(z, SPLITS, axis=-1)
        q = rope(q.reshape(bsz, t, A_HEADS, 2, A_HEAD_DIM), pos)
        k = rope(k.reshape(bsz, t, A_HEADS, 2, A_HEAD_DIM), pos)
        v = v.reshape(bsz, t, A_HEADS, A_VDIM)
        lam_init = 0.8 - 0.6 * math.exp(-0.3 * l)
        lq1, lk1 = p['lambda_q1'][l].astype(F32), p['lambda_k1'][l].astype(F32)
        lq2, lk2 = p['lambda_q2'][l].astype(F32), p['lambda_k2'][l].astype(F32)
        lam = jnp.exp(jnp.sum(lq1 * lk1)) - jnp.exp(jnp.sum(lq2 * lk2)) + lam_init
        cv = cg * hc
        if cache_k is None:
            kf, vf = k.astype(F32), v.astype(F32)
            nblk = t // QBLK
            qb = q.reshape(bsz, nblk, QBLK, A_HEADS, 2, A_HEAD_DIM).swapaxes(0, 1)
            key_chunk = jnp.arange(t) // CHUNK

            def blk(args):
                qi, bi = args
                q_chunk = (bi * QBLK + jnp.arange(QBLK)) // CHUNK
                mask = key_chunk[None, :] <= q_chunk[:, None]
                return diff_attend(qi, kf, vf, mask, lam)

            o = lax.map(blk, (qb, jnp.arange(nblk)))
            o = o.swapaxes(0, 1).reshape(bsz, t, A_HEADS, A_VDIM)
            pool_ext = jnp.concatenate([jnp.zeros((bsz, POOL_HIST, B_WIDTH), u.dtype), u], axis=1)
            conv_ext = jnp.concatenate([jnp.zeros((bsz, CONV_W - 1, C_WIDTH), cv.dtype), cv], axis=1)
        else:
            past = cache_k.shape[2]
            ck = cache_k[l].reshape(bsz, past, A_HEADS, 2, A_HEAD_DIM)
            k_all = jnp.concatenate([ck.astype(F32), k.astype(F32)], axis=1)
            v_all = jnp.concatenate([cache_v[l].astype(F32), v.astype(F32)], axis=1)
            o = diff_attend(q, k_all, v_all, None, lam)
            pool_ext = jnp.concatenate([state_pool[l].astype(u.dtype), u], axis=1)
            conv_ext = jnp.concatenate([state_conv[l].astype(cv.dtype), cv], axis=1)
        o = rmsnorm(o, p['subln'][l]) * (1.0 - lam_init)
        y_attn = o.reshape(bsz, t, A_WIDTH).astype(x.dtype)
        y_pool = pool_mix(pool_ext, pos, p['pool_w'][l], p['pool_scale'][l])
        y_conv = bg * short_conv(conv_ext, p['conv_w'][l])
        mix = jnp.concatenate([y_attn, y_pool, y_conv], axis=-1) @ p['w_out'][l]
        x = x + mix
        h = rmsnorm(x, p['norm_ffn2'][l])
        x = x + 0.5 * swiglu(h, p['ffn2_gate'][l], p['ffn2_up'][l], p['ffn2_down'][l])
        new_k.append(k.reshape(bsz, t, A_HEADS, 2 * A_HEAD_DIM))
        new_v.append(v)
        new_pool.append(pool_ext[:, -POOL_HIST:])
        new_conv.append(conv_ext[:, -(CONV_W - 1):])
    x = rmsnorm(x, p['final_norm'])
    return x, jnp.stack(new_k), jnp.stack(new_v), jnp.stack(new_pool), jnp.stack(new_conv)


def setup_inputs(seed: int = 0) -> dict:
    key = jax.random.key(seed)
    ks = jax.random.split(key, 32)

    def nrm(k, shape, s):
        return jax.random.normal(k, shape, F32) * s

    def gain(k, shape):
        return 1.0 + 0.1 * jax.random.normal(k, shape, F32)

    return {
        'x_prompt': nrm(ks[0], (BATCH, SEQ, D_MODEL), 1.0),
        'x_sample': nrm(ks[1], (DEC_BATCH, DEC_SEQ, D_MODEL), 1.0),
        'cache_k': nrm(ks[2], (DEPTH, DEC_BATCH, PAST_LEN, A_HEADS, 2 * A_HEAD_DIM), 1.0),
        'cache_v': nrm(ks[3], (DEPTH, DEC_BATCH, PAST_LEN, A_HEADS, A_VDIM), 1.0),
        'state_pool': nrm(ks[4], (DEPTH, DEC_BATCH, POOL_HIST, B_WIDTH), 1.0),
        'state_conv': nrm(ks[5], (DEPTH, DEC_BATCH, CONV_W - 1, C_WIDTH), 1.0),
        'norm_ffn1': gain(ks[6], (DEPTH, D_MODEL)),
        'ffn1_gate': nrm(ks[7], (DEPTH, D_MODEL, D_FF), D_MODEL ** -0.5),
        'ffn1_up': nrm(ks[8], (DEPTH, D_MODEL, D_FF), D_MODEL ** -0.5),
        'ffn1_down': nrm(ks[9], (DEPTH, D_FF, D_MODEL), D_FF ** -0.5),
        'norm_mix': gain(ks[10], (DEPTH, D_MODEL)),
        'w_in': nrm(ks[11], (DEPTH, D_MODEL, D_IN), D_MODEL ** -0.5),
        'lambda_q1': nrm(ks[12], (DEPTH, A_HEAD_DIM), 0.1),
        'lambda_k1': nrm(ks[13], (DEPTH, A_HEAD_DIM), 0.1),
        'lambda_q2': nrm(ks[14], (DEPTH, A_HEAD_DIM), 0.1),
        'lambda_k2': nrm(ks[15], (DEPTH, A_HEAD_DIM), 0.1),
        'subln': gain(ks[16], (DEPTH, A_VDIM)),
        'pool_w': nrm(ks[17], (DEPTH, N_POOL_GROUPS, POOL_GROUP, POOL_GROUP), POOL_GROUP ** -0.5),
        'pool_scale': gain(ks[18], (DEPTH, B_WIDTH)),
        'conv_w': nrm(ks[19], (DEPTH, CONV_W, C_WIDTH), CONV_W ** -0.5),
        'w_out': nrm(ks[20], (DEPTH, D_MIX, D_MODEL), D_MIX ** -0.5),
        'norm_ffn2': gain(ks[21], (DEPTH, D_MODEL)),
        'ffn2_gate': nrm(ks[22], (DEPTH, D_MODEL, D_FF), D_MODEL ** -0.5),
        'ffn2_up': nrm(ks[23], (DEPTH, D_MODEL, D_FF), D_MODEL ** -0.5),
        'ffn2_down': nrm(ks[24], (DEPTH, D_FF, D_MODEL), D_FF ** -0.5),
        'final_norm': gain(ks[25], (D_MODEL,)),
    }


def reference(x_prompt, x_sample, cache_k, cache_v, state_pool, state_conv,
              norm_ffn1, ffn1_gate, ffn1_up, ffn1_down, norm_mix, w_in,
              lambda_q1, lambda_k1, lambda_q2, lambda_k2, subln, pool_w, pool_scale, conv_w,
              w_out, norm_ffn2, ffn2_gate, ffn2_up, ffn2_down, final_norm):
    p = {
        'norm_ffn1': norm_ffn1, 'ffn1_gate': ffn1_gate, 'ffn1_up': ffn1_up, 'ffn1_down': ffn1_down,
        'norm_mix': norm_mix, 'w_in': w_in,
        'lambda_q1': lambda_q1, 'lambda_k1': lambda_k1, 'lambda_q2': lambda_q2, 'lambda_k2': lambda_k2,
        'subln': subln, 'pool_w': pool_w, 'pool_scale': pool_scale, 'conv_w': conv_w, 'w_out': w_out,
        'norm_ffn2': norm_ffn2, 'ffn2_gate': ffn2_gate, 'ffn2_up': ffn2_up, 'ffn2_down': ffn2_down,
        'final_norm': final_norm,
    }
    pos_prompt = jnp.arange(x_prompt.shape[1], dtype=jnp.int32)
    pos_sample = PAST_LEN + jnp.arange(x_sample.shape[1], dtype=jnp.int32)
    y_prompt, k_prompt, v_prompt, pool_prompt, conv_prompt = run_trunk(
        x_prompt, pos_prompt, None, None, None, None, p)
    y_sample, k_sample, v_sample, pool_sample, conv_sample = run_trunk(
        x_sample, pos_sample, cache_k, cache_v, state_pool, state_conv, p)
    return (y_prompt, y_sample, k_prompt, v_prompt, pool_prompt, conv_prompt,
            k_sample, v_sample, pool_sample, conv_sample)
```

```python
import math
import os
from contextlib import ExitStack

import numpy as np
import concourse.bass as bass
import concourse.mybir as mybir
from concourse.bass_utils import run_bass_kernel_spmd

F32 = mybir.dt.float32
BF16 = mybir.dt.bfloat16
ALU = mybir.AluOpType
AF = mybir.ActivationFunctionType
AX = mybir.AxisListType

D = 1024
SEQ = 2048
DEC = 32
NT = SEQ + DEC
PAST = 4096
DEPTH = 2
DFF = 2816
DIN = 2560
NH = 4
EPS = 1e-6
SCALE = 64 ** -0.5
NSUB = 17

PP_G = 0
PP_PS = PP_G + 56
PP_CW = PP_PS + 4
PP_SUB = PP_CW + 12
PP_LAM = PP_SUB + 256
PP_COS = PP_LAM + 512
PP_SIN = PP_COS + 544
PP_IC = PP_SIN + 544
NPP = PP_IC + 32


class Res:
    __slots__ = ("name", "writer", "readers", "dsem", "dcount", "nobar")

    def __init__(self, name, nobar=False):
        self.name = name
        self.writer = None
        self.readers = []
        self.dsem = None
        self.dcount = 0
        self.nobar = nobar


class Eng:
    def __init__(self, fw, raw, name, selfwait=True):
        self.fw = fw
        self.raw = raw
        self.name = name
        self.sem = fw.new_sem("e_" + name)
        self.count = 0
        self.seen = {}
        self.selfwait = selfwait

    def wait(self, tok):
        if tok is None:
            return
        sem, val = tok
        if sem is self.sem and not self.selfwait:
            return
        k = id(sem)
        if self.seen.get(k, 0) >= val:
            return
        self.raw.wait_ge(sem, val)
        self.seen[k] = val


class FW:
    def __init__(self, nc, es):
        self.nc = nc
        self.es = es
        self.pe = Eng(self, nc.tensor, "pe", selfwait=False)
        self.act = Eng(self, nc.scalar, "act")
        self.dve = Eng(self, nc.vector, "dve")
        self.pool = Eng(self, nc.gpsimd, "pool")
        self.sp = Eng(self, nc.sync, "sp", selfwait=False)
        self.engs = [self.pe, self.act, self.dve, self.pool, self.sp]
        self.dma_res = []

    def new_sem(self, name):
        self.nsem = getattr(self, "nsem", 0) + 1
        return self.es.enter_context(self.nc.semaphore(f"{name}_{self.nsem}"))

    def deps(self, eng, reads, writes):
        best = {}

        def add(tok):
            if tok is None:
                return
            k = id(tok[0])
            if k not in best or best[k][1] < tok[1]:
                best[k] = tok
        for r in reads:
            add(r.writer)
        for w in writes:
            add(w.writer)
            for t in w.readers:
                add(t)
        for tok in best.values():
            eng.wait(tok)

    def commit(self, tok, reads, writes):
        for r in reads:
            r.readers.append(tok)
            if len(r.readers) > 16:
                best = {}
                for s, v in r.readers:
                    if id(s) not in best or best[id(s)][1] < v:
                        best[id(s)] = (s, v)
                r.readers = list(best.values())
        for w in writes:
            w.writer = tok
            w.readers = []

    def op(self, eng, fn, reads=(), writes=(), signal=True):
        self.deps(eng, reads, writes)
        ins = fn()
        if signal:
            eng.count += 1
            ins.then_inc(eng.sem, 1)
            tok = (eng.sem, eng.count)
            self.commit(tok, reads, writes)
            return tok
        return None

    def dma(self, eng, out, in_, slot, reads=(), writes=(), nowait=False, **kw):
        if slot.dsem is None:
            slot.dsem = self.new_sem("d_" + slot.name)
            self.dma_res.append(slot)
        if not nowait:
            self.deps(eng, reads, writes)
        ins = eng.raw.dma_start(out=out, in_=in_, **kw)
        slot.dcount += 16
        ins.then_inc(slot.dsem, 16)
        tok = (slot.dsem, slot.dcount)
        self.commit(tok, reads, writes)
        return tok

    def barrier(self):
        toks = [(e.sem, e.count) for e in self.engs if e.count > 0]
        toks += [(r.dsem, r.dcount) for r in self.dma_res if r.dcount > 0 and not r.nobar]
        for e in self.engs:
            for t in toks:
                if t[0] is e.sem and not e.selfwait:
                    continue
                e.wait(t)

    def finish(self):
        toks = [(e.sem, e.count) for e in self.engs if e.count > 0]
        toks += [(r.dsem, r.dcount) for r in self.dma_res if r.dcount > 0]
        for t in toks:
            self.sp.wait(t)


class _Stop(Exception):
    pass


def build_program():
    nc = bass.Bass("TRN2", target_bir_lowering=False)
    STAGE = float(os.environ.get("KSTAGE", "99"))

    stopped = {"v": False}

    def stage(x):
        if STAGE <= x:
            stopped["v"] = True
        return stopped["v"]

    def din(name, shape):
        return nc.dram_tensor(name, list(shape), F32, kind="ExternalInput").ap()

    def dout(name, shape):
        return nc.dram_tensor(name, list(shape), F32, kind="ExternalOutput").ap()

    xp_d = din("xp", (SEQ, D))
    xs_d = din("xs", (DEC, D))
    ck_d = din("ck", (DEPTH, PAST, 512))
    cv_d = din("cv", (DEPTH, PAST, 512))
    stp_d = din("stp", (DEPTH, 15, 256))
    stc_d = din("stc", (DEPTH, 2, 256))
    pp_d = din("pp", (128, NPP))
    pw_d = din("pwbd", (DEPTH, 2, 128, 128))
    wg_d = [din("f1g", (DEPTH, D, DFF)), din("f2g", (DEPTH, D, DFF))]
    wu_d = [din("f1u", (DEPTH, D, DFF)), din("f2u", (DEPTH, D, DFF))]
    wd_d = [din("f1d", (DEPTH, DFF, D)), din("f2d", (DEPTH, DFF, D))]
    win_d = din("win", (DEPTH, D, DIN))
    wout_d = din("wout", (DEPTH, D, D))

    yp_d = dout("yp", (SEQ, D))
    ys_d = dout("ys", (DEC, D))
    kp_d = dout("kp", (DEPTH, SEQ, 512))
    vp_d = dout("vp", (DEPTH, SEQ, 512))
    plp_d = dout("plp", (DEPTH, 15, 256))
    cvp_d = dout("cvp", (DEPTH, 2, 256))
    ks_d = dout("ks", (DEPTH, DEC, 512))
    vs_d = dout("vs", (DEPTH, DEC, 512))
    pls_d = dout("pls", (DEPTH, 15, 256))
    cvs_d = dout("cvs", (DEPTH, 2, 256))

    with ExitStack() as es:
        fw = FW(nc, es)
        pe, act, dve, pool, sp = fw.pe, fw.act, fw.dve, fw.pool, fw.sp

        uid = {"n": 0}

        def sb(st, name, shape, dt):
            uid["n"] += 1
            return st.enter_context(nc.sbuf_tensor(f"s{uid['n']}_{name}", list(shape), dt))

        xT = sb(es, "xT", (128, 8, NT), F32)
        R_x = [[Res(f"x{dc}_{tl}") for tl in range(NSUB)] for dc in range(8)]

        def RX(t0, n, dcs=range(8)):
            s0, s1 = t0 // 128, (t0 + n - 1) // 128
            return [R_x[dc][s] for dc in dcs for s in range(s0, s1 + 1)]

        pp = sb(es, "pp", (128, NPP), F32)
        R_pp = Res("pp")
        pwb = sb(es, "pwb", (128, DEPTH, 2, 128), BF16)
        R_pw = Res("pwb")
        idf = sb(es, "idf", (128, 128), F32)
        idb = sb(es, "idb", (128, 128), BF16)
        onesb = sb(es, "onesb", (128, 128), BF16)
        R_c = Res("consts")
        lamt = sb(es, "lamt", (128, DEPTH, 4), F32)
        subg = sb(es, "subg", (128, DEPTH, 128), F32)
        msk = sb(es, "msk", (128, 2, 4), F32)
        invw = sb(es, "invw", (128, 2), F32)
        cneg = sb(es, "cneg", (128, 1), F32)
        mrow = sb(es, "mrow", (128, 128), BF16)
        mcol = sb(es, "mcol", (128, 256), BF16)
        gu = [sb(es, f"gu{i}", (128, 2, 8, 512), BF16) for i in range(2)]
        dn = sb(es, "dn", (128, 2, 4, 1024), BF16)
        R_gu = [Res(f"gu{i}", nobar=True) for i in range(2)]
        R_dn = [Res(f"dn{i}", nobar=True) for i in range(2)]
        pb = [es.enter_context(nc.psum_tensor(f"pb{i}", [128, 512], F32)) for i in range(8)]
        R_pb = [Res(f"pb{i}") for i in range(8)]

        def ppc(c0, n=1):
            return pp[:, c0:c0 + n]

        def mm_group(bank_ap, pairs, reads, writes):
            fw.deps(pe, reads, writes)
            n = len(pairs)
            for i, (l, r) in enumerate(pairs):
                ins = nc.tensor.matmul(bank_ap, l, r, start=(i == 0), stop=(i == n - 1))
            pe.count += 1
            ins.then_inc(pe.sem, 1)
            tok = (pe.sem, pe.count)
            fw.commit(tok, reads, writes)
            return tok

        rr = {"n": 0}

        fw.dma(sp, pp[:], pp_d, R_pp, writes=[R_pp])
        fw.dma(pool, pwb[:], pw_d.rearrange("l c p e -> p l c e"), R_pw, writes=[R_pw])
        fw.op(pool, lambda: nc.gpsimd.memset(idf[:], 1.0), writes=[R_c])
        fw.op(pool, lambda: nc.gpsimd.affine_select(idf[:], idf[:], pattern=[[-1, 128]], compare_op=ALU.is_equal,
                                                   fill=0.0, base=0, channel_multiplier=1), reads=[R_c], writes=[R_c])
        fw.op(dve, lambda: nc.vector.tensor_copy(idb[:], idf[:]), reads=[R_c], writes=[R_c])
        fw.op(dve, lambda: nc.vector.memset(onesb[:], 1.0), writes=[R_c])
        fw.op(dve, lambda: nc.vector.memset(msk[:], 0.0), writes=[R_c])
        fw.op(dve, lambda: nc.vector.memset(msk[:, :, 0:1], 1.0), reads=[R_c], writes=[R_c])
        fw.op(dve, lambda: nc.vector.memset(msk[64:128, 0, 1:2], 1.0), reads=[R_c], writes=[R_c])
        fw.op(dve, lambda: nc.vector.memset(msk[:, 1, 1:3], 1.0), reads=[R_c], writes=[R_c])
        fw.op(dve, lambda: nc.vector.memset(msk[64:128, 1, 3:4], 1.0), reads=[R_c], writes=[R_c])
        fw.op(dve, lambda: nc.vector.memset(cneg[:], -0.5), reads=[R_c], writes=[R_c])
        fw.op(dve, lambda: nc.vector.memset(mrow[:], 0.0), reads=[R_c], writes=[R_c])
        fw.op(dve, lambda: nc.vector.memset(mrow[:, 64:128], 1.0), reads=[R_c], writes=[R_c])
        fw.op(dve, lambda: nc.vector.memset(mcol[:], 0.0), reads=[R_c], writes=[R_c])
        fw.op(dve, lambda: nc.vector.memset(mcol[:, 0:64], -30000.0), reads=[R_c], writes=[R_c])
        fw.op(dve, lambda: nc.vector.memset(invw[0:64, 0:1], 0.5), reads=[R_c], writes=[R_c])
        fw.op(dve, lambda: nc.vector.memset(invw[64:128, 0:1], 0.25), reads=[R_c], writes=[R_c])
        fw.op(dve, lambda: nc.vector.memset(invw[0:64, 1:2], 0.125), reads=[R_c], writes=[R_c])
        fw.op(dve, lambda: nc.vector.memset(invw[64:128, 1:2], 0.0625), reads=[R_c], writes=[R_c])
        with ExitStack() as ph:
            ltmp = sb(ph, "ltmp", (128, 64), F32)
            R_lt = Res("ltmp")
            for l in range(DEPTH):
                lam_init = 0.8 - 0.6 * math.exp(-0.3 * l)
                for j in range(2):
                    c0 = PP_LAM + l * 256 + j * 128
                    fw.op(dve, lambda: nc.vector.tensor_tensor(ltmp[:], ppc(c0, 64), ppc(c0 + 64, 64), ALU.mult),
                          reads=[R_pp, R_lt], writes=[R_lt])
                    fw.op(dve, lambda: nc.vector.reduce_sum(lamt[:, l, 2 + j:3 + j], ltmp[:], axis=AX.X),
                          reads=[R_lt, R_c], writes=[R_c])
                fw.op(act, lambda: nc.scalar.activation(lamt[:, l, 2:4], lamt[:, l, 2:4], AF.Exp), reads=[R_c], writes=[R_c])
                fw.op(dve, lambda: nc.vector.tensor_tensor(lamt[:, l, 0:1], lamt[:, l, 2:3], lamt[:, l, 3:4], ALU.subtract),
                      reads=[R_c], writes=[R_c])
                fw.op(dve, lambda: nc.vector.tensor_scalar(lamt[:, l, 0:1], lamt[:, l, 0:1], lam_init, None, ALU.add),
                      reads=[R_c], writes=[R_c])
                fw.op(dve, lambda: nc.vector.tensor_scalar(lamt[:, l, 1:2], lamt[:, l, 0:1], -1.0, None, ALU.mult),
                      reads=[R_c], writes=[R_c])
                fw.op(dve, lambda: nc.vector.tensor_scalar(subg[:, l, :], ppc(PP_SUB + l * 128, 128), 1.0 - lam_init, None, ALU.mult),
                      reads=[R_pp, R_c], writes=[R_c])
            fw.barrier()

        def rms_tile(st, gi, t0, n, dst_fn, dst_res, bank, sq_eng="act"):
            sq, R_sq, rstd, R_rs = st["sq"], st["R_sq"], st["rstd"], st["R_rs"]
            xres = RX(t0, n)
            for dc in range(8):
                i = rr["n"] % 4
                rr["n"] += 1
                if sq_eng == "pool" or (sq_eng == "mix" and dc % 2 == 1):
                    fw.op(pool, lambda: nc.gpsimd.tensor_tensor(sq[:, i, 0:n], xT[:, dc, t0:t0 + n], xT[:, dc, t0:t0 + n], ALU.mult),
                          reads=RX(t0, n, [dc]), writes=[R_sq[i]])
                else:
                    fw.op(act, lambda: nc.scalar.activation(sq[:, i, 0:n], xT[:, dc, t0:t0 + n], AF.Square),
                          reads=RX(t0, n, [dc]), writes=[R_sq[i]])
                fw.op(pe, lambda: nc.tensor.matmul(pb[bank][:, 0:n], onesb[:], sq[:, i, 0:n], start=(dc == 0), stop=(dc == 7)),
                      reads=[R_sq[i], R_c], writes=[R_pb[bank]])
            j = st["rsi"] % st["nrs"]
            st["rsi"] += 1
            fw.op(act, lambda: nc.scalar.activation(rstd[:, j, 0:n], pb[bank][:, 0:n], AF.Ln, bias=st["epsb"][:, 0:1], scale=1.0 / D),
                  reads=[R_pb[bank]], writes=[R_rs[j]])
            fw.op(act, lambda: nc.scalar.activation(rstd[:, j, 0:n], rstd[:, j, 0:n], AF.Exp, scale=-0.5),
                  reads=[R_rs[j]], writes=[R_rs[j]])
            for dc in range(8):
                fw.op(dve, lambda: nc.vector.scalar_tensor_tensor(dst_fn(dc), xT[:, dc, t0:t0 + n], ppc(PP_G + gi * 8 + dc),
                                                                  rstd[:, j, 0:n], ALU.mult, ALU.mult),
                      reads=[R_rs[j], R_pp] + RX(t0, n, [dc]), writes=[dst_res])

        def norm_bufs(st_, nmax=512, nrs=2):
            d = {}
            d["sq"] = sb(st_, "sq", (128, 4, nmax), BF16)
            d["R_sq"] = [Res(f"sq{i}") for i in range(4)]
            d["rstd"] = sb(st_, "rstd", (128, nrs, nmax), F32)
            d["R_rs"] = [Res(f"rs{i}") for i in range(nrs)]
            d["nrs"] = nrs
            d["epsb"] = sb(st_, "epsb", (128, 1), F32)
            d["rsi"] = 0
            fw.op(dve, lambda: nc.vector.memset(d["epsb"][:], EPS), writes=[d["R_rs"][0]])
            return d

        TILES = [(0, 512), (512, 512), (1024, 512), (1536, 512), (2048, 32)]

        def load_cols(dst_ap, w_ap, l, c0, ncol, slot_res, nowait=False):
            src = w_ap[l, :, c0:c0 + ncol].rearrange("(dc p) c -> p dc c", p=128)
            fw.dma(pool, dst_ap, src, slot_res, writes=[slot_res], nowait=nowait)

        def load_rows(dst_ap, w_ap, l, r0, nr, slot_res):
            src = w_ap[l, r0:r0 + nr, :].rearrange("(fc p) n -> p fc n", p=128)
            fw.dma(pool, dst_ap, src, slot_res, writes=[slot_res])

        PARTS = [(0, 4), (512, 4), (1024, 4), (1536, 4), (2048, 4), (2560, 2)]

        def ffn_load(l, w, p):
            c0, nf = PARTS[p]
            s = p % 2
            load_cols(gu[s][:, 0, :, 0:nf * 128], wg_d[w], l, c0, nf * 128, R_gu[s])
            load_cols(gu[s][:, 1, :, 0:nf * 128], wu_d[w], l, c0, nf * 128, R_gu[s], nowait=True)
            load_rows(dn[:, s, 0:nf, :], wd_d[w], l, c0, nf * 128, R_dn[s])

        ffn_load(0, 0, 0)
        ffn_load(0, 0, 1)

        def xload_gen(st_):
            xst = sb(st_, "xst", (128, 2, D), F32)
            R_xst = [Res("xst0"), Res("xst1")]
            for s in range(NSUB):
                n = 128 if s < 16 else DEC
                src = xp_d[s * 128:(s + 1) * 128, :] if s < 16 else xs_d
                k = s % 2
                fw.dma(sp, xst[0:n, k, :], src, R_xst[k], writes=[R_xst[k]])
                for hb in range(2):
                    bank = (7, 5)[hb]
                    fw.deps(pe, [R_xst[k], R_c], [R_pb[bank]])
                    for q in range(4):
                        dc = hb * 4 + q
                        ins = nc.tensor.transpose(pb[bank][:, q * 128:q * 128 + n], xst[0:n, k, dc * 128:(dc + 1) * 128], idf[0:n, 0:n])
                    pe.count += 1
                    ins.then_inc(pe.sem, 1)
                    fw.commit((pe.sem, pe.count), [R_xst[k], R_c], [R_pb[bank]])
                    src_ap = pb[bank][:].rearrange("p (q t) -> p q t", q=4)[:, :, 0:n]
                    dst_ap = xT[:, hb * 4:hb * 4 + 4, s * 128:s * 128 + n]
                    if hb == 0:
                        fw.op(act, lambda: nc.scalar.copy(dst_ap, src_ap), reads=[R_pb[bank]],
                              writes=[R_x[dc][s] for dc in range(hb * 4, hb * 4 + 4)])
                    else:
                        fw.op(dve, lambda: nc.vector.tensor_copy(dst_ap, src_ap), reads=[R_pb[bank]],
                              writes=[R_x[dc][s] for dc in range(hb * 4, hb * 4 + 4)])
                yield s

        FT = [(i * 416, 416) for i in range(5)]

        def ffn(l, w, gi, preloaded, prefetch=None, xload=False):
            TILES = FT
            with ExitStack() as ph:
                hT = sb(ph, "hT", (128, 8, NT), BF16)
                R_h = [Res(f"h{t}") for t in range(5)]
                aT = sb(ph, "aT", (128, 4, NT), BF16)
                R_a = [[Res(f"a{fc}_{t}") for t in range(5)] for fc in range(4)]
                sg = sb(ph, "sg", (128, 2, 512), F32)
                R_sg = [Res("sg0"), Res("sg1")]
                nb = norm_bufs(ph)
                if preloaded < 1:
                    ffn_load(l, w, 0)
                if preloaded < 2:
                    ffn_load(l, w, 1)

                def ffn_rms(ti_):
                    t0_, n_ = TILES[ti_]
                    rms_tile(nb, gi, t0_, n_, lambda dc: hT[:, dc, t0_:t0_ + n_], R_h[ti_], 6, sq_eng="mix")

                xg = xload_gen(ph) if xload else None

                def need_tokens(upto):
                    if xg is not None:
                        while xstate["s"] < min(NSUB - 1, (upto - 1) // 128):
                            xstate["s"] = next(xg)

                xstate = {"s": -1}
                need_tokens(TILES[0][0] + TILES[0][1])
                ffn_rms(0)
                cnt = 0
                for p, (c0, nf) in enumerate(PARTS):
                    s = p % 2
                    for ti, (t0, n) in enumerate(TILES):
                        if p == 0 and ti + 1 < len(TILES):
                            need_tokens(TILES[ti + 1][0] + TILES[ti + 1][1])
                            ffn_rms(ti + 1)
                        for fc in range(nf):
                            gb = cnt % 2
                            ub = 2 + cnt % 2
                            cnt += 1
                            mm_group(pb[gb][:, 0:n], [(gu[s][:, 0, dc, fc * 128:(fc + 1) * 128], hT[:, dc, t0:t0 + n]) for dc in range(8)],
                                     [R_gu[s], R_h[ti]], [R_pb[gb]])
                            mm_group(pb[ub][:, 0:n], [(gu[s][:, 1, dc, fc * 128:(fc + 1) * 128], hT[:, dc, t0:t0 + n]) for dc in range(8)],
                                     [R_gu[s], R_h[ti]], [R_pb[ub]])
                            k = gb
                            fw.op(act, lambda: nc.scalar.activation(sg[:, k, 0:n], pb[gb][:, 0:n], AF.Silu),
                                  reads=[R_pb[gb]], writes=[R_sg[k]])
                            fw.op(dve, lambda: nc.vector.tensor_tensor(aT[:, fc, t0:t0 + n], sg[:, k, 0:n], pb[ub][:, 0:n], ALU.mult),
                                  reads=[R_sg[k], R_pb[ub]], writes=[R_a[fc][ti]])
                    dcnt = 0
                    for ti, (t0, n) in enumerate(TILES):
                        for dco in range(8):
                            db = 4 + dcnt % 2
                            dcnt += 1
                            mm_group(pb[db][:, 0:n], [(dn[:, s, fc, dco * 128:(dco + 1) * 128], aT[:, fc, t0:t0 + n]) for fc in range(nf)],
                                     [R_dn[s]] + [R_a[fc][ti] for fc in range(nf)], [R_pb[db]])
                            xr = RX(t0, n, [dco])
                            fw.op(dve, lambda: nc.vector.scalar_tensor_tensor(xT[:, dco, t0:t0 + n], pb[db][:, 0:n], 0.5,
                                                                              xT[:, dco, t0:t0 + n], ALU.mult, ALU.add),
                                  reads=[R_pb[db]] + xr, writes=xr)
                    if p + 2 < len(PARTS):
                        ffn_load(l, w, p + 2)
                    elif p == len(PARTS) - 2 and prefetch is not None:
                        prefetch()
                fw.barrier()

        def rope(src_bank, n, s, dst_ap, dst_res, tmp, R_tmp):
            sv = pb[src_bank][0:n, :].rearrange("p (g h d) -> p g h d", g=8, h=2)
            dv = dst_ap.rearrange("p (g h d) -> p g h d", g=8, h=2)
            cosb = pp[0:n, PP_COS + s * 32:PP_COS + s * 32 + 32].unsqueeze(1).to_broadcast([n, 8, 32])
            sinb = pp[0:n, PP_SIN + s * 32:PP_SIN + s * 32 + 32].unsqueeze(1).to_broadcast([n, 8, 32])
            tv = [tmp[0:n, i, 0:256].rearrange("p (g d) -> p g d", g=8) for i in range(4)]
            rd = [R_pb[src_bank], R_pp]
            fw.op(dve, lambda: nc.vector.tensor_tensor(tv[0], sv[:, :, 0, :], cosb, ALU.mult), reads=rd, writes=[R_tmp[0]])
            fw.op(dve, lambda: nc.vector.tensor_tensor(tv[1], sv[:, :, 1, :], sinb, ALU.mult), reads=rd, writes=[R_tmp[1]])
            fw.op(dve, lambda: nc.vector.tensor_tensor(tv[2], sv[:, :, 1, :], cosb, ALU.mult), reads=rd, writes=[R_tmp[2]])
            fw.op(dve, lambda: nc.vector.tensor_tensor(tv[3], sv[:, :, 0, :], sinb, ALU.mult), reads=rd, writes=[R_tmp[3]])
            fw.op(pool, lambda: nc.gpsimd.tensor_tensor(dv[:, :, 0, :], tv[0], tv[1], ALU.subtract),
                  reads=[R_tmp[0], R_tmp[1]], writes=[dst_res])
            fw.op(pool, lambda: nc.gpsimd.tensor_tensor(dv[:, :, 1, :], tv[2], tv[3], ALU.add),
                  reads=[R_tmp[2], R_tmp[3]], writes=[dst_res])

        def transpose4(src_fn, n, dst_ap, src_res, dst_res, bank, eng):
            pbb = pb[bank][:].bitcast(BF16)
            fw.deps(pe, [src_res, R_c], [R_pb[bank]])
            for h in range(4):
                ins = nc.tensor.transpose(pbb[:, h * 128:h * 128 + n], src_fn(h), idb[0:n, 0:n])
            pe.count += 1
            ins.then_inc(pe.sem, 1)
            fw.commit((pe.sem, pe.count), [src_res, R_c], [R_pb[bank]])
            src_ap = pbb[:, 0:512].rearrange("p (h t) -> p h t", h=4)[:, :, 0:n]
            if eng is act:
                fw.op(act, lambda: nc.scalar.copy(dst_ap, src_ap), reads=[R_pb[bank]], writes=[dst_res])
            else:
                fw.op(dve, lambda: nc.vector.tensor_copy(dst_ap, src_ap), reads=[R_pb[bank]], writes=[dst_res])

        def mixer_preload(l):
            load_cols(gu[0][:, 0], win_d, l, 512, 512, R_gu[0])
            load_cols(gu[0][:, 1], win_d, l, 1024, 512, R_gu[0], nowait=True)

        def mixer(l):
            gi = l * 3 + 1
            with ExitStack() as mph:
                kT = sb(mph, "kT", (128, NH, NT), BF16)
                Vx = sb(mph, "Vx", (128, NSUB, NH, 129), BF16)
                R_kT = [Res(f"kT{s}") for s in range(NSUB)]
                R_V = [Res(f"V{s}") for s in range(NSUB)]
                with ExitStack() as ph:
                    nb = norm_bufs(ph)
                    hTt = sb(ph, "hTt", (128, 2, 8, 512), BF16)
                    R_ht = [Res("ht0"), Res("ht1")]
                    kst = sb(ph, "kst", (128, 2, 512), F32)
                    R_kst = [Res("kst0"), Res("kst1")]
                    vst = sb(ph, "vst", (128, 2, 512), F32)
                    R_vst = [Res("vst0"), Res("vst1")]
                    kbb = sb(ph, "kbb", (128, 2, 512), BF16)
                    R_kb = [Res("kb0"), Res("kb1")]
                    rtmp = sb(ph, "rtmp", (128, 8, 256), F32)
                    R_rt = [Res(f"rt{i}") for i in range(8)]
                    wk = gu[0][:, 0]
                    wv = gu[0][:, 1]
                    load_cols(gu[1][:, 0], win_d, l, 2048, 512, R_gu[1])
                    load_rows(dn[:].rearrange("p a f n -> p (a f) n"), wout_d, l, 0, D, R_dn[0])
                    fw.op(pool, lambda: nc.gpsimd.memset(Vx[:, :, :, 128:129], 1.0), writes=R_V)
                    pend = []

                    def flush_one():
                        m_, c_, s_ = pend.pop(0)
                        transpose4(lambda h: kbb[0:m_, c_, h * 128:(h + 1) * 128], m_, kT[:, :, s_ * 128:s_ * 128 + m_],
                                   R_kb[c_], R_kT[s_], 4 + c_, act)

                    def pe_warm(nmm):
                        fw.deps(pe, [R_gu[0], R_c], [R_pb[7]])
                        for _ in range(nmm):
                            ins = nc.tensor.matmul(pb[7][:, :], onesb[:], wk[:, 0, :], start=True, stop=True)
                        pe.count += 1
                        ins.then_inc(pe.sem, 1)
                        fw.commit((pe.sem, pe.count), [R_gu[0], R_c], [R_pb[7]])

                    def m1_rms(ti_):
                        t0_, n_ = TILES[ti_]
                        hs_ = ti_ % 2
                        if n_ >= 512:
                            pe_warm(20)
                        rms_tile(nb, gi, t0_, n_, lambda dc: hTt[:, hs_, dc, 0:n_], R_ht[hs_], 6, sq_eng="mix")

                    m1_rms(0)
                    for ti, (t0, n) in enumerate(TILES):
                        hs = ti % 2
                        if ti + 1 < len(TILES):
                            m1_rms(ti + 1)
                        for sub in range((n + 127) // 128):
                            s = t0 // 128 + sub
                            m = min(128, n - sub * 128)
                            c = s % 2
                            kb_, vb_ = c, 2 + c
                            mm_group(pb[kb_][0:m, :], [(hTt[:, hs, dc, sub * 128:sub * 128 + m], wk[:, dc, :]) for dc in range(8)],
                                     [R_ht[hs], R_gu[0]], [R_pb[kb_]])
                            mm_group(pb[vb_][0:m, :], [(hTt[:, hs, dc, sub * 128:sub * 128 + m], wv[:, dc, :]) for dc in range(8)],
                                     [R_ht[hs], R_gu[0]], [R_pb[vb_]])
                            rope(kb_, m, s, kst[0:m, c, :], R_kst[c], rtmp[:, 4 * c:4 * c + 4], R_rt[4 * c:4 * c + 4])
                            kdst = kp_d[l, s * 128:s * 128 + m, :] if s < 16 else ks_d[l]
                            fw.dma(sp, kdst, kst[0:m, c, :], R_kst[c], reads=[R_kst[c]])
                            fw.op(act, lambda: nc.scalar.copy(kbb[0:m, c, :], kst[0:m, c, :]), reads=[R_kst[c]], writes=[R_kb[c]])
                            fw.op(act, lambda: nc.scalar.copy(vst[0:m, c, :], pb[vb_][0:m, :]), reads=[R_pb[vb_]], writes=[R_vst[c]])
                            vdst = vp_d[l, s * 128:s * 128 + m, :] if s < 16 else vs_d[l]
                            fw.dma(sp, vdst, vst[0:m, c, :], R_vst[c], reads=[R_vst[c]])
                            fw.op(act, lambda: nc.scalar.copy(Vx[0:m, s, :, 0:128], vst[0:m, c, :].rearrange("p (h e) -> p h e", h=4)),
                                  reads=[R_vst[c]], writes=[R_V[s]])
                            pend.append((m, c, s))
                            if len(pend) > 1:
                                flush_one()
                    while pend:
                        flush_one()
                    fw.barrier()
                if stage(2 + 10 * l):
                    return
                wq = gu[0][:, 0]
                wuh = gu[0][:, 1]
                wbc = gu[1][:, 0]
                wo = dn[:].rearrange("p a f n -> p (a f) n")
                load_cols(wq, win_d, l, 0, 512, R_gu[0])
                load_cols(wuh, win_d, l, 1536, 512, R_gu[0], nowait=True)
                R_wo = [R_dn[0]]
                M2T = [(i * 256, 256) for i in range(8)] + [(2048, 32)]
                with ExitStack() as ph:
                    nb = norm_bufs(ph, 256, 1)
                    hT2v = gu[1][:, 1]
                    R_ht = [Res("h20"), Res("h21")]
                    qbb = sb(ph, "qbb", (128, 2, 512), BF16)
                    R_qb = [Res("qb0"), Res("qb1")]
                    rtmp = sb(ph, "rtmp2", (128, 4, 258), F32)
                    R_rt = [Res(f"rt2{i}") for i in range(4)]
                    qT = sb(ph, "qT", (128, 2, NH, 256), BF16)
                    R_qT = [[Res(f"qT{a_}{b_}") for b_ in range(2)] for a_ in range(2)]
                    PT = sb(ph, "PT", (128, 2, 2, 2, 256), BF16)
                    R_PT = [Res(f"PT{i}") for i in range(2)]
                    mixT = sb(ph, "mixT", (128, 8, 256), BF16)
                    R_mx = [Res(f"mx{c}") for c in range(8)]
                    ubuf = sb(ph, "ubuf", (128, 3, 2, 272), F32)
                    R_ub = [Res(f"ub{i}") for i in range(3)]
                    cvb = sb(ph, "cvb", (128, 2, 258), F32)
                    R_cvb = Res("cvb")
                    cvt = sb(ph, "cvt", (128, 2, 256), F32)
                    R_cvt = [Res("cvt0"), Res("cvt1")]
                    dbf = sb(ph, "dbf", (128, 2, 256), BF16)
                    R_dbf = [Res("dbf0"), Res("dbf1")]
                    oev = sb(ph, "oev", (128, 2, 2, 129), F32)
                    R_oev = [Res("oev0"), Res("oev1")]
                    osb = sb(ph, "osb", (128, 2, 128), F32)
                    R_os = [Res("os0"), Res("os1")]
                    pend_fin = []
                    sml = sb(ph, "sml", (128, 2, 8), F32)
                    R_sml = [Res("sml0"), Res("sml1")]
                    yat = sb(ph, "yat", (128, 2, 512), BF16)
                    R_yat = [Res("yat0"), Res("yat1")]
                    kcs = sb(ph, "kcs", (128, 2, 2, 512), BF16)
                    R_kcs = [Res("kcs0"), Res("kcs1")]
                    kcT = sb(ph, "kcT", (128, 2, NH, 256), BF16)
                    R_kcT = [Res("kcT0"), Res("kcT1")]
                    vcs = sb(ph, "vcs", (128, 2, 2, 512), BF16)
                    R_vcs = [Res("vcs0"), Res("vcs1")]
                    PTs = PT[:].rearrange("p s e c q -> p s (e c q)")[:, :, 0:512].rearrange("p s (c h j q) -> p s c h j q", c=2, h=NH, j=2)
                    R_PTs = [R_PT[0], R_PT[1]]
                    oacc = rtmp[0:DEC].rearrange("p a b -> p (a b)").rearrange("p (g e) -> p g e", g=8)
                    R_oacc = R_rt

                    def hTt_ap(hs, dc, a_, b_):
                        return hT2v[:, dc, hs * 256 + a_:hs * 256 + b_]

                    fw.op(pool, lambda: nc.gpsimd.memset(ubuf[:], 0.0), writes=R_ub)
                    fw.op(pool, lambda: nc.gpsimd.memset(cvb[:], 0.0), writes=[R_cvb])
                    NCB = 16

                    def load_ck(cb):
                        sl = cb % 2
                        fw.dma(pool, kcs[:, sl], ck_d[l, cb * 256:(cb + 1) * 256, :].rearrange("(j p) f -> p j f", p=128),
                               R_kcs[sl], writes=[R_kcs[sl]])

                    def load_cv(cb):
                        sl = cb % 2
                        fw.dma(pool, vcs[:, sl], cv_d[l, cb * 256:(cb + 1) * 256, :].rearrange("(j p) f -> p j f", p=128),
                               R_vcs[sl], writes=[R_vcs[sl]])

                    if l == 0 and os.environ.get("KDEBUG"):
                        print("M2 sbuf bytes remaining:", nc.sbuf_bytes_remaining)
                    load_ck(0)
                    load_cv(0)
                    load_ck(1)
                    load_cv(1)
                    misc = {"n": 0}

                    def mbank():
                        if misc.get("mode") == "in":
                            return 7
                        b = (7, 0, 1, 6)[misc["n"] % 4]
                        misc["n"] += 1
                        return b

                    ptc = {"n": 0, "e": 0}

                    class BV:
                        def __init__(self, ap, res):
                            self.ap = ap
                            self.res = res if isinstance(res, list) else [res]

                        def __getitem__(self, k):
                            return self.ap[k]

                    def flush_fin():
                        while pend_fin:
                            pend_fin.pop(0)()

                    def epilogue(m, h, src1, src2, yslot):
                        k = ptc["e"] % 2
                        ptc["e"] += 1
                        r1, r2 = src1.res, src2.res
                        ob_ = osb[0:m, k, :]
                        fw.op(dve, lambda: nc.vector.reciprocal(sml[0:m, k, 0:1], src1[0:m, 128:129]), reads=r1, writes=[R_sml[k]])
                        fw.op(dve, lambda: nc.vector.reciprocal(sml[0:m, k, 1:2], src2[0:m, 128:129]), reads=r2 + [R_sml[k]], writes=[R_sml[k]])
                        fw.op(dve, lambda: nc.vector.tensor_tensor(sml[0:m, k, 1:2], sml[0:m, k, 1:2], lamt[0:m, l, 1:2], ALU.mult),
                              reads=[R_sml[k], R_c], writes=[R_sml[k]])
                        fw.op(dve, lambda: nc.vector.tensor_scalar(ob_, src1[0:m, 0:128], sml[0:m, k, 0:1], None, ALU.mult),
                              reads=r1 + [R_sml[k]], writes=[R_os[k]])
                        fw.op(dve, lambda: nc.vector.scalar_tensor_tensor(ob_, src2[0:m, 0:128], sml[0:m, k, 1:2], ob_,
                                                                          ALU.mult, ALU.add),
                              reads=r2 + [R_sml[k], R_os[k]], writes=[R_os[k]])
                        fw.op(dve, lambda: nc.vector.tensor_tensor(src1[0:m, 0:128], ob_, ob_, ALU.mult),
                              reads=[R_os[k]] + r1, writes=r1)
                        fw.op(dve, lambda: nc.vector.reduce_sum(sml[0:m, k, 2:3], src1[0:m, 0:128], axis=AX.X),
                              reads=r1 + [R_sml[k]], writes=[R_sml[k]])
                        flush_fin()
                        fw.op(dve, lambda: nc.vector.tensor_scalar(sml[0:m, k, 3:4], sml[0:m, k, 2:3], 1.0 / 128, EPS, ALU.mult, ALU.add),
                              reads=[R_sml[k]], writes=[R_sml[k]])
                        fw.op(pool, lambda: nc.gpsimd.tensor_tensor(sml[0:m, k, 3:4], sml[0:m, k, 3:4], cneg[0:m, 0:1], ALU.pow),
                              reads=[R_sml[k], R_c], writes=[R_sml[k]])

                        def fin():
                            fw.op(dve, lambda: nc.vector.scalar_tensor_tensor(yat[0:m, yslot, h * 128:(h + 1) * 128], ob_, sml[0:m, k, 3:4],
                                                                              subg[0:m, l, :], ALU.mult, ALU.mult),
                                  reads=[R_os[k], R_sml[k], R_c], writes=[R_yat[yslot]])
                        pend_fin.append(fin)

                    def tinfo(ti):
                        t0, n = M2T[ti]
                        return t0, n, ti % 2, (t0 >= SEQ), (n + 127) // 128

                    def qchain(ti):
                        t0, n, hs, is_s, nsub = tinfo(ti)
                        sq, R_sq, rstd, R_rs = nb["sq"], nb["R_sq"], nb["rstd"], nb["R_rs"]
                        bank = mbank()

                        def square(dc):
                            fw.op(act, lambda: nc.scalar.activation(sq[:, dc % 4, 0:n], xT[:, dc, t0:t0 + n], AF.Square),
                                  reads=RX(t0, n, [dc]), writes=[R_sq[dc % 4]])

                        def ssmm(dc):
                            fw.op(pe, lambda: nc.tensor.matmul(pb[bank][:, 0:n], onesb[:], sq[:, dc % 4, 0:n], start=(dc == 0), stop=(dc == 7)),
                                  reads=[R_sq[dc % 4], R_c], writes=[R_pb[bank]])
                        for dc in range(4):
                            square(dc)
                        yield
                        for dc in range(4):
                            ssmm(dc)
                            square(4 + dc)
                        for dc in range(4, 8):
                            ssmm(dc)
                        j = 0
                        fw.op(act, lambda: nc.scalar.activation(rstd[:, j, 0:n], pb[bank][:, 0:n], AF.Ln, bias=nb["epsb"][:, 0:1], scale=1.0 / D),
                              reads=[R_pb[bank]], writes=[R_rs[j]])
                        fw.op(act, lambda: nc.scalar.activation(rstd[:, j, 0:n], rstd[:, j, 0:n], AF.Exp, scale=-0.5),
                              reads=[R_rs[j]], writes=[R_rs[j]])
                        for dc in range(8):
                            fw.op(dve, lambda: nc.vector.scalar_tensor_tensor(hTt_ap(hs, dc, 0, n), xT[:, dc, t0:t0 + n], ppc(PP_G + gi * 8 + dc),
                                                                              rstd[:, j, 0:n], ALU.mult, ALU.mult),
                                  reads=[R_rs[j], R_pp] + RX(t0, n, [dc]), writes=[R_ht[hs]])
                        yield
                        for sub in range(nsub):
                            s_ = t0 // 128 + sub
                            m = min(128, n - sub * 128)
                            c = s_ % 2
                            qb_ = mbank()
                            mm_group(pb[qb_][0:m, :], [(hTt_ap(hs, dc, sub * 128, sub * 128 + m), wq[:, dc, :]) for dc in range(8)],
                                     [R_ht[hs], R_gu[0]], [R_pb[qb_]])
                            rope(qb_, m, s_, qbb[0:m, c, :], R_qb[c], rtmp, R_rt)
                            yield
                        for sub in range(nsub):
                            s_ = t0 // 128 + sub
                            m = min(128, n - sub * 128)
                            c = s_ % 2
                            transpose4(lambda h: qbb[0:m, c, h * 128:(h + 1) * 128], m, qT[:, hs, :, sub * 128:sub * 128 + m],
                                       R_qb[c], R_qT[hs][sub], mbank(), act)
                            yield

                    def fproj(ti, wt, col, bank):
                        t0, n, hs, is_s, nsub = tinfo(ti)
                        mm_group(pb[bank][:, 0:n], [(wt[:, dc, col * 128:(col + 1) * 128], hTt_ap(hs, dc, 0, n)) for dc in range(8)],
                                 [R_ht[hs], R_gu[0], R_gu[1]], [R_pb[bank]])

                    def early(ti):
                        t0, n, hs, is_s, nsub = tinfo(ti)
                        if ti == 0:
                            pass
                        elif is_s:
                            with nc.allow_non_contiguous_dma(reason="tiny state halo loads"):
                                for ch in range(2):
                                    fw.dma(sp, ubuf[:, 0, ch, 1:16], stp_d[l, :, ch * 128:(ch + 1) * 128].rearrange("t p -> p t"), R_ub[0],
                                           writes=[R_ub[0]], nowait=(ch > 0))
                                    fw.dma(sp, cvb[:, ch, 0:2], stc_d[l, :, ch * 128:(ch + 1) * 128].rearrange("t p -> p t"), R_cvb,
                                           writes=[R_cvb], nowait=(ch > 0))
                        else:
                            fw.op(pool, lambda: nc.gpsimd.tensor_copy(ubuf[:, 0, :, 0:16], ubuf[:, 0, :, 256:272]), reads=[R_ub[0]], writes=[R_ub[0]])
                            fw.op(pool, lambda: nc.gpsimd.tensor_copy(cvb[:, :, 0:2], cvb[:, :, 256:258]), reads=[R_cvb], writes=[R_cvb])
                        for ch in range(2):
                            b = mbank()
                            fproj(ti, wuh, ch, b)
                            fw.op(dve, lambda: nc.vector.tensor_copy(ubuf[:, 0, ch, 16:16 + n], pb[b][:, 0:n]), reads=[R_pb[b]], writes=[R_ub[0]])
                        for ch in range(2):
                            b = mbank()
                            fproj(ti, wuh, 2 + ch, b)
                            fw.op(dve, lambda: nc.vector.tensor_copy(cvt[:, ch, 0:n], pb[b][:, 0:n]), reads=[R_pb[b]], writes=[R_cvt[ch]])
                        for ch in range(2):
                            b = mbank()
                            fproj(ti, wbc, 2 + ch, b)
                            fw.op(dve, lambda: nc.vector.tensor_tensor(cvb[:, ch, 2:2 + n], pb[b][:, 0:n], cvt[:, ch, 0:n], ALU.mult),
                                  reads=[R_pb[b], R_cvt[ch]], writes=[R_cvb])
                        if t0 + n == SEQ or is_s:
                            pd = pls_d if is_s else plp_d
                            cd = cvs_d if is_s else cvp_d
                            with nc.allow_non_contiguous_dma(reason="tiny state outputs"):
                                for ch in range(2):
                                    fw.dma(sp, pd[l, :, ch * 128:(ch + 1) * 128].rearrange("t p -> p t"), ubuf[:, 0, ch, 16 + n - 15:16 + n],
                                           R_ub[0], reads=[R_ub[0]])
                                    fw.dma(sp, cd[l, :, ch * 128:(ch + 1) * 128].rearrange("t p -> p t"), cvb[:, ch, n:n + 2],
                                           R_cvb, reads=[R_cvb])
                        for ch in range(2):
                            cw = PP_CW + l * 6 + ch * 3
                            fw.op(dve, lambda: nc.vector.tensor_scalar(cvt[:, ch, 0:n], cvb[:, ch, 0:n], ppc(cw), None, ALU.mult),
                                  reads=[R_cvb, R_pp], writes=[R_cvt[ch]])
                            fw.op(dve, lambda: nc.vector.scalar_tensor_tensor(cvt[:, ch, 0:n], cvb[:, ch, 1:1 + n], ppc(cw + 1), cvt[:, ch, 0:n],
                                                                              ALU.mult, ALU.add),
                                  reads=[R_cvb, R_pp, R_cvt[ch]], writes=[R_cvt[ch]])
                            fw.op(dve, lambda: nc.vector.scalar_tensor_tensor(cvt[:, ch, 0:n], cvb[:, ch, 2:2 + n], ppc(cw + 2), cvt[:, ch, 0:n],
                                                                              ALU.mult, ALU.add),
                                  reads=[R_cvb, R_pp, R_cvt[ch]], writes=[R_cvt[ch]])
                        W = 16 + n
                        cur = 0
                        for step in range(4):
                            sh = 1 << step
                            nxt = 1 + (step % 2)
                            lo = 2 * sh
                            for ch in range(2):
                                if ch == 0 and step >= 2:
                                    continue
                                fw.op(dve, lambda: nc.vector.scalar_tensor_tensor(ubuf[:, nxt, ch, lo:W], ubuf[:, cur, ch, lo - sh:W - sh],
                                                                                  msk[:, ch, step:step + 1], ubuf[:, cur, ch, lo:W],
                                                                                  ALU.mult, ALU.add),
                                      reads=[R_ub[cur], R_c], writes=[R_ub[nxt]])
                            cur = nxt
                        for ch in range(2):
                            fin = 2
                            fw.op(dve, lambda: nc.vector.scalar_tensor_tensor(dbf[:, ch, 0:n], ubuf[:, fin, ch, 16:16 + n], invw[:, ch:ch + 1],
                                                                              ubuf[:, 0, ch, 16:16 + n], ALU.mult, ALU.subtract),
                                  reads=[R_ub[fin], R_ub[0], R_c], writes=[R_dbf[ch]])
                            if ti == 0:
                                fw.op(pool, lambda: nc.gpsimd.tensor_tensor(ubuf[:, 1, ch, 0:16], ubuf[:, fin, ch, 16:32], ppc(PP_IC + ch * 16, 16), ALU.mult),
                                      reads=[R_ub[fin], R_pp, R_ub[1]], writes=[R_ub[1]])
                                fw.op(pool, lambda: nc.gpsimd.tensor_tensor(dbf[:, ch, 0:16], ubuf[:, 1, ch, 0:16], ubuf[:, 0, ch, 16:32], ALU.subtract),
                                      reads=[R_ub[1], R_ub[0], R_dbf[ch]], writes=[R_dbf[ch]])

                    def attention_prompt(ti, steps):
                        t0, n, hs, is_s, nsub = tinfo(ti)
                        g = ti
                        iters = [(h, kp) for h in range(NH) for kp in range(g + 1)]

                        def emit_S(i):
                            h, kp = iters[i]
                            banks = ((0, 1, 6)[(2 * i) % 3], (0, 1, 6)[(2 * i + 1) % 3])
                            slot = i % 2
                            rds = [R_kT[2 * kp], R_kT[2 * kp + 1], R_qT[hs][0], R_qT[hs][1], R_c]
                            for c in range(2):
                                sbk = banks[c]
                                fw.deps(pe, rds, [R_pb[sbk]])
                                for e in range(2):
                                    kt = 2 * kp + e
                                    c0 = 128 if kt == 2 * g + 1 else 0
                                    diag = (kt >= 2 * g)
                                    oap = pb[sbk][:, e * 256 + c0:e * 256 + 256]
                                    ins = nc.tensor.matmul(oap, kT[c * 64:(c + 1) * 64, h, kt * 128:(kt + 1) * 128],
                                                           qT[c * 64:(c + 1) * 64, hs, h, c0:256], start=True, stop=not diag)
                                    if diag:
                                        ins = nc.tensor.matmul(oap, mrow[c * 64:c * 64 + 1, :], mcol[c * 64:c * 64 + 1, 0:256 - c0],
                                                               start=False, stop=True)
                                pe.count += 1
                                ins.then_inc(pe.sem, 1)
                                fw.commit((pe.sem, pe.count), rds, [R_pb[sbk]])
                            for c in range(2):
                                sbk = banks[c]
                                fw.op(act, lambda: nc.scalar.activation(PT[:, slot, :, c, :], pb[sbk][:].rearrange("p (e q) -> p e q", e=2),
                                                                        AF.Exp, scale=SCALE),
                                      reads=[R_pb[sbk]], writes=[R_PT[slot]])

                        def emit_PV(i):
                            h, kp = iters[i]
                            slot = i % 2
                            closed = []
                            for e in range(2):
                                kt = 2 * kp + e
                                for qs in range(2):
                                    if kt > 2 * g + qs:
                                        continue
                                    for c in range(2):
                                        ob = 2 + qs * 2 + c
                                        first = (kt == 0)
                                        last = (kt == 2 * g + qs)
                                        fw.deps(pe, [R_PT[slot], R_V[kt]], [R_pb[ob]] if first else [])
                                        ins = nc.tensor.matmul(pb[ob][:, 0:129], PT[:, slot, e, c, qs * 128:(qs + 1) * 128], Vx[:, kt, h, :],
                                                               start=first, stop=last)
                                        pe.count += 1
                                        ins.then_inc(pe.sem, 1)
                                        fw.commit((pe.sem, pe.count), [R_PT[slot], R_V[kt]], [R_pb[ob]])
                                    if kt == 2 * g + qs:
                                        closed.append(qs)
                            ks = []
                            for qs in closed:
                                k = ptc["n"] % 2
                                ptc["n"] += 1
                                ks.append(k)
                                for c in range(2):
                                    ob = 2 + qs * 2 + c
                                    fw.op(dve, lambda: nc.vector.tensor_copy(oev[:, k, c, :], pb[ob][:, 0:129]), reads=[R_pb[ob]], writes=[R_oev[k]])
                            for qs, k in zip(closed, ks):
                                epilogue(128, h, BV(oev[:, k, 0, :], R_oev[k]), BV(oev[:, k, 1, :], R_oev[k]), qs)

                        nsteps = 6
                        stride = max(1, len(iters) // (nsteps + 1))
                        misc["mode"] = "in"
                        emit_S(0)
                        for i in range(len(iters)):
                            if i + 1 < len(iters):
                                emit_S(i + 1)
                            emit_PV(i)
                            if steps is not None and i % stride == stride - 1:
                                next(steps, None)
                        if steps is not None:
                            for _ in steps:
                                pass
                        misc["mode"] = "out"
                        wb = mbank()
                        fw.deps(pe, [R_gu[0], R_c], [R_pb[wb]])
                        for _ in range(12):
                            ins = nc.tensor.matmul(pb[wb][:, :], onesb[:], wq[:, 0, :], start=True, stop=True)
                        pe.count += 1
                        ins.then_inc(pe.sem, 1)
                        fw.commit((pe.sem, pe.count), [R_gu[0], R_c], [R_pb[wb]])
                        flush_fin()
                        for qs in range(2):
                            transpose4(lambda h: yat[:, qs, h * 128:(h + 1) * 128], 128, mixT[:, 0:4, qs * 128:(qs + 1) * 128],
                                       R_yat[qs], R_mx[qs], mbank(), dve)
                        return [R_mx[0], R_mx[1]]

                    def attention_sample(ti):
                        t0, n, hs, is_s, nsub = tinfo(ti)
                        fw.op(pool, lambda: nc.gpsimd.memset(oacc, 0.0), writes=R_oacc)

                        def blk(b_):
                            cache = b_ < NCB
                            return cache, b_ % 2, (2 if cache else 1), (128 if cache else DEC)

                        def emit_T(b_):
                            cache, sl, nj, kp_ = blk(b_)
                            if not cache:
                                return
                            pbb = pb[5][:].bitcast(BF16)
                            fw.deps(pe, [R_kcs[sl], R_c], [R_pb[5]])
                            for j in range(2):
                                for h in range(NH):
                                    ins = nc.tensor.transpose(pbb[:, (h * 2 + j) * 128:(h * 2 + j + 1) * 128], kcs[:, sl, j, h * 128:(h + 1) * 128], idb[:])
                            pe.count += 1
                            ins.then_inc(pe.sem, 1)
                            fw.commit((pe.sem, pe.count), [R_kcs[sl], R_c], [R_pb[5]])
                            fw.op(dve, lambda: nc.vector.tensor_copy(kcT[:, sl], pbb[:].rearrange("p (h k) -> p h k", h=NH)),
                                  reads=[R_pb[5]], writes=[R_kcT[sl]])

                        def emit_S(b_):
                            cache, sl, nj, kp_ = blk(b_)
                            kres = [R_kcT[sl]] if cache else [R_kT[16]]
                            banks = (0, 1) if b_ % 2 == 0 else (6, 7)
                            for c in range(2):
                                sbk = banks[c]
                                Sv = pb[sbk][:, 0:256].rearrange("p (h j q) -> p h j q", h=4, j=2)
                                fw.deps(pe, kres + [R_qT[hs][0]], [R_pb[sbk]])
                                for h in range(NH):
                                    for j in range(nj):
                                        if cache:
                                            lk = kcT[c * 64:(c + 1) * 64, sl, h, j * 128:(j + 1) * 128]
                                        else:
                                            lk = kT[c * 64:(c + 1) * 64, h, SEQ:SEQ + DEC]
                                        ins = nc.tensor.matmul(Sv[0:kp_, h, j, :], lk, qT[c * 64:(c + 1) * 64, hs, h, 0:DEC],
                                                               start=True, stop=True)
                                pe.count += 1
                                ins.then_inc(pe.sem, 1)
                                fw.commit((pe.sem, pe.count), kres + [R_qT[hs][0]], [R_pb[sbk]])
                            for c in range(2):
                                sbk = banks[c]
                                fw.op(act, lambda: nc.scalar.activation(PTs[0:kp_, sl, c].rearrange("p h j q -> p (h j q)"), pb[sbk][0:kp_, 0:256],
                                                                        AF.Exp, scale=SCALE),
                                      reads=[R_pb[sbk]], writes=[R_PTs[sl]])

                        def emit_PV(b_):
                            cache, sl, nj, kp_ = blk(b_)
                            vres = [R_vcs[sl]] if cache else [R_V[16]]
                            for bi, grp in enumerate([(0, 1, 2), (3, 4, 5), (6, 7)]):
                                ob = 2 + bi
                                rds = vres + [R_PTs[sl], R_c]
                                fw.deps(pe, rds, [R_pb[ob]])
                                for gi_, hc_ in enumerate(grp):
                                    h, c = hc_ // 2, hc_ % 2
                                    if cache:
                                        for j in range(nj):
                                            ins = nc.tensor.matmul(pb[ob][0:DEC, gi_ * 129:gi_ * 129 + 128], PTs[0:kp_, sl, c, h, j, :],
                                                                   vcs[0:kp_, sl, j, h * 128:(h + 1) * 128], start=(j == 0), stop=(j == nj - 1))
                                        for j in range(nj):
                                            ins = nc.tensor.matmul(pb[ob][0:DEC, gi_ * 129 + 128:gi_ * 129 + 129], PTs[0:kp_, sl, c, h, j, :],
                                                                   onesb[0:kp_, 0:1], start=(j == 0), stop=(j == nj - 1))
                                    else:
                                        ins = nc.tensor.matmul(pb[ob][0:DEC, gi_ * 129:(gi_ + 1) * 129], PTs[0:kp_, sl, c, h, 0, :],
                                                               Vx[0:kp_, 16, h, :], start=True, stop=True)
                                pe.count += 1
                                ins.then_inc(pe.sem, 1)
                                fw.commit((pe.sem, pe.count), rds, [R_pb[ob]])
                                ng = len(grp)
                                oav = oacc[:, grp[0]:grp[0] + ng, :].rearrange("p g e -> p (g e)")
                                fw.op(dve, lambda: nc.vector.tensor_tensor(oav, oav, pb[ob][0:DEC, 0:ng * 129], ALU.add),
                                      reads=[R_pb[ob]] + R_oacc, writes=R_oacc)

                        NB = NCB + 1
                        emit_T(0)
                        emit_S(0)
                        for b_ in range(NB):
                            if b_ + 1 < NB:
                                emit_T(b_ + 1)
                                if b_ + 2 < NCB:
                                    load_ck(b_ + 2)
                                emit_S(b_ + 1)
                            emit_PV(b_)
                            if b_ + 2 < NCB:
                                load_cv(b_ + 2)
                        for h in range(NH):
                            epilogue(DEC, h, BV(oacc[:, 2 * h, :], R_oacc), BV(oacc[:, 2 * h + 1, :], R_oacc), 0)
                        flush_fin()
                        transpose4(lambda h: yat[0:DEC, 0, h * 128:(h + 1) * 128], DEC, mixT[:, 0:4, 0:DEC],
                                   R_yat[0], R_mx[0], mbank(), dve)
                        return [R_mx[0]]

                    def tail(ti, attn_res):
                        t0, n, hs, is_s, nsub = tinfo(ti)
                        for ch in range(2):
                            b = mbank()
                            fproj(ti, wbc, ch, b)
                            fw.op(dve, lambda: nc.vector.tensor_tensor(mixT[:, 6 + ch, 0:n], pb[b][:, 0:n], cvt[:, ch, 0:n], ALU.mult),
                                  reads=[R_pb[b], R_cvt[ch]], writes=[R_mx[6 + ch]])
                        for ch in range(2):
                            b = mbank()
                            fw.op(pe, lambda: nc.tensor.matmul(pb[b][:, 0:n], pwb[:, l, ch, :], dbf[:, ch, 0:n], start=True, stop=True),
                                  reads=[R_pw, R_dbf[ch]], writes=[R_pb[b]])
                            fw.op(dve, lambda: nc.vector.tensor_scalar(mixT[:, 4 + ch, 0:n], pb[b][:, 0:n], ppc(PP_PS + l * 2 + ch), None, ALU.mult),
                                  reads=[R_pb[b], R_pp], writes=[R_mx[4 + ch]])
                        for dco in range(8):
                            b = mbank()
                            mm_group(pb[b][:, 0:n], [(wo[:, mc, dco * 128:(dco + 1) * 128], mixT[:, mc, 0:n]) for mc in range(8)],
                                     R_wo + attn_res + R_mx[4:8], [R_pb[b]])
                            xr = RX(t0, n, [dco])
                            fw.op(dve, lambda: nc.vector.tensor_tensor(xT[:, dco, t0:t0 + n], pb[b][:, 0:n], xT[:, dco, t0:t0 + n], ALU.add),
                                  reads=[R_pb[b]] + xr, writes=xr)

                    for _ in qchain(0):
                        pass
                    for ti in range(len(M2T)):
                        early(ti)
                        steps = qchain(ti + 1) if ti + 1 < len(M2T) else None
                        if ti < 8:
                            ares = attention_prompt(ti, steps)
                        else:
                            ares = attention_sample(ti)
                        tail(ti, ares)
                    fw.barrier()

        stage(0)
        for l in range(DEPTH):
            if stopped["v"]:
                break
            ffn(l, 0, l * 3 + 0, 2 if l == 0 else 1, prefetch=lambda: mixer_preload(l), xload=(l == 0))
            if stage(1 + 10 * l):
                break
            mixer(l)
            if stage(5 + 10 * l):
                break
            ffn(l, 1, l * 3 + 2, 0, prefetch=(lambda: ffn_load(l + 1, 0, 0)) if l + 1 < DEPTH else None)
            if stage(6 + 10 * l):
                break
        if stopped["v"]:
            fw.finish()
            return nc

        with ExitStack() as ph:
            nb = norm_bufs(ph)
            yf = sb(ph, "yf", (128, 2, 8, 512), F32)
            R_yf = [Res("yf0"), Res("yf1")]
            yst = sb(ph, "yst", (128, 2, D), F32)
            R_yst = [Res("yst0"), Res("yst1")]
            FTL = [(0, 512), (512, 512), (1024, 512), (1536, 512), (2048, 32)]

            def fin_out(ti, sub):
                t0, nt = FTL[ti]
                ks = ti % 2
                s = t0 // 128 + sub
                n = min(128, nt - sub * 128)
                k = s % 2
                for hb in range(2):
                    bank = (2 * s + hb) % 4
                    fw.deps(pe, [R_yf[ks], R_c], [R_pb[bank]])
                    for q in range(4):
                        dc = hb * 4 + q
                        ins = nc.tensor.transpose(pb[bank][0:n, q * 128:(q + 1) * 128], yf[:, ks, dc, sub * 128:sub * 128 + n], idf[:])
                    pe.count += 1
                    ins.then_inc(pe.sem, 1)
                    fw.commit((pe.sem, pe.count), [R_yf[ks], R_c], [R_pb[bank]])
                    if hb == 0:
                        fw.op(act, lambda: nc.scalar.copy(yst[0:n, k, 0:512], pb[bank][0:n, :]), reads=[R_pb[bank]], writes=[R_yst[k]])
                    else:
                        fw.op(dve, lambda: nc.vector.tensor_copy(yst[0:n, k, 512:1024], pb[bank][0:n, :]), reads=[R_pb[bank]], writes=[R_yst[k]])
                dst = yp_d[s * 128:(s + 1) * 128, :] if s < 16 else ys_d
                fw.dma(sp, dst, yst[0:n, k, :], R_yst[k], reads=[R_yst[k]])

            def fin_rms(ti):
                t0, nt = FTL[ti]
                ks = ti % 2
                rms_tile(nb, 6, t0, nt, lambda dc: yf[:, ks, dc, 0:nt], R_yf[ks], 6 + ti % 2, sq_eng="mix")

            fin_rms(0)
            for ti, (t0, nt) in enumerate(FTL):
                if ti + 1 < len(FTL):
                    fin_rms(ti + 1)
                for sub in range((nt + 127) // 128):
                    fin_out(ti, sub)
            fw.finish()
    return nc


def _param_pack(p):
    pp = np.zeros((128, NPP), np.float32)
    gains = [p["norm_ffn1"][0], p["norm_mix"][0], p["norm_ffn2"][0],
             p["norm_ffn1"][1], p["norm_mix"][1], p["norm_ffn2"][1], p["final_norm"]]
    for gi, g in enumerate(gains):
        pp[:, PP_G + gi * 8:PP_G + gi * 8 + 8] = np.asarray(g, np.float32).reshape(8, 128).T
    for l in range(DEPTH):
        pp[:, PP_PS + l * 2:PP_PS + l * 2 + 2] = np.asarray(p["pool_scale"][l], np.float32).reshape(2, 128).T
        cw = np.asarray(p["conv_w"][l], np.float32)
        for ch in range(2):
            pp[:, PP_CW + l * 6 + ch * 3:PP_CW + l * 6 + ch * 3 + 3] = cw[:, ch * 128:(ch + 1) * 128].T
        pp[:, PP_SUB + l * 128:PP_SUB + (l + 1) * 128] = np.asarray(p["subln"][l], np.float32)[None, :]
        for j, nm in enumerate(["lambda_q1", "lambda_k1", "lambda_q2", "lambda_k2"]):
            pp[:, PP_LAM + l * 256 + j * 64:PP_LAM + l * 256 + (j + 1) * 64] = np.asarray(p[nm][l], np.float32)[None, :]
    half = 32
    inv = np.power(np.float32(10000.0), -np.arange(half, dtype=np.float32) / np.float32(half)).astype(np.float32)
    pos = np.concatenate([np.arange(SEQ), PAST + np.arange(DEC)]).astype(np.float32)
    ang = (pos[:, None] * inv[None, :]).astype(np.float32)
    cos = np.cos(ang).astype(np.float32)
    sin = np.sin(ang).astype(np.float32)
    for s in range(NSUB):
        n = 128 if s < 16 else DEC
        pp[0:n, PP_COS + s * 32:PP_COS + (s + 1) * 32] = cos[s * 128:s * 128 + n]
        pp[0:n, PP_SIN + s * 32:PP_SIN + (s + 1) * 32] = sin[s * 128:s * 128 + n]
    wins = [2, 4, 8, 16]
    for ch in range(2):
        for half_i in range(2):
            w = wins[ch * 2 + half_i]
            for t in range(16):
                pp[half_i * 64:(half_i + 1) * 64, PP_IC + ch * 16 + t] = 1.0 / min(t + 1, w)
    return pp


_NC_CACHE = {}


def kernel(**inputs):
    p = {k: np.asarray(v) for k, v in inputs.items()}
    if "nc" not in _NC_CACHE:
        _NC_CACHE["nc"] = build_program()
    nc = _NC_CACHE["nc"]
    pp = _param_pack(p)
    pw = np.asarray(p["pool_w"], np.float32)
    pwbd = np.zeros((DEPTH, 2, 128, 128), np.float32)
    for l in range(DEPTH):
        for g in range(4):
            ch, o = g // 2, (g % 2) * 64
            pwbd[l, ch, o:o + 64, o:o + 64] = pw[l, g]
    shared = {
        "pp": pp, "pwbd": pwbd,
        "f1g": np.ascontiguousarray(p["ffn1_gate"], np.float32), "f1u": np.ascontiguousarray(p["ffn1_up"], np.float32),
        "f1d": np.ascontiguousarray(p["ffn1_down"], np.float32),
        "f2g": np.ascontiguousarray(p["ffn2_gate"], np.float32), "f2u": np.ascontiguousarray(p["ffn2_up"], np.float32),
        "f2d": np.ascontiguousarray(p["ffn2_down"], np.float32),
        "win": np.ascontiguousarray(p["w_in"], np.float32), "wout": np.ascontiguousarray(p["w_out"], np.float32),
    }
    in_maps = []
    for b in range(8):
        m = dict(shared)
        m["xp"] = np.ascontiguousarray(p["x_prompt"][b], np.float32)
        m["xs"] = np.ascontiguousarray(p["x_sample"][b], np.float32)
        m["ck"] = np.ascontiguousarray(p["cache_k"][:, b].reshape(DEPTH, PAST, 512), np.float32)
        m["cv"] = np.ascontiguousarray(p["cache_v"][:, b].reshape(DEPTH, PAST, 512), np.float32)
        m["stp"] = np.ascontiguousarray(p["state_pool"][:, b], np.float32)
        m["stc"] = np.ascontiguousarray(p["state_conv"][:, b], np.float32)
        in_maps.append(m)
    ncores = int(os.environ.get("KCORES", "8"))
    res = run_bass_kernel_spmd(nc, in_maps[:ncores], core_ids=list(range(ncores)))
    R = list(res.results)
    while len(R) < 8:
        R.append(R[0])

    def gather(name, shape_tail, axis_b):
        arrs = [np.asarray(R[b][name], np.float32) for b in range(8)]
        return np.stack(arrs, axis=axis_b)

    y_prompt = gather("yp", None, 0)
    y_sample = gather("ys", None, 0)
    k_prompt = gather("kp", None, 1).reshape(DEPTH, 8, SEQ, NH, 128)
    v_prompt = gather("vp", None, 1).reshape(DEPTH, 8, SEQ, NH, 128)
    pool_prompt = gather("plp", None, 1)
    conv_prompt = gather("cvp", None, 1)
    k_sample = gather("ks", None, 1).reshape(DEPTH, 8, DEC, NH, 128)
    v_sample = gather("vs", None, 1).reshape(DEPTH, 8, DEC, NH, 128)
    pool_sample = gather("pls", None, 1)
    conv_sample = gather("cvs", None, 1)
    return (y_prompt, y_sample, k_prompt, v_prompt, pool_prompt, conv_prompt,
            k_sample, v_sample, pool_sample, conv_sample)
```

```python
import math
import os
from contextlib import ExitStack

import numpy as np
import concourse.bass as bass
import concourse.mybir as mybir
from concourse.bass_utils import run_bass_kernel_spmd

F32 = mybir.dt.float32
BF16 = mybir.dt.bfloat16
ALU = mybir.AluOpType
AF = mybir.ActivationFunctionType
AX = mybir.AxisListType

D = 1024
SEQ = 2048
DEC = 32
NT = SEQ + DEC
PAST = 4096
DEPTH = 2
DFF = 2816
DIN = 2560
NH = 4
EPS = 1e-6
SCALE = 64 ** -0.5
NSUB = 17

PP_G = 0
PP_PS = PP_G + 56
PP_CW = PP_PS + 4
PP_SUB = PP_CW + 12
PP_LAM = PP_SUB + 256
PP_COS = PP_LAM + 512
PP_SIN = PP_COS + 544
PP_IC = PP_SIN + 544
NPP = PP_IC + 32


class Res:
    __slots__ = ("name", "writer", "readers", "dsem", "dcount", "nobar")

    def __init__(self, name, nobar=False):
        self.name = name
        self.writer = None
        self.readers = []
        self.dsem = None
        self.dcount = 0
        self.nobar = nobar


class Eng:
    def __init__(self, fw, raw, name, selfwait=True):
        self.fw = fw
        self.raw = raw
        self.name = name
        self.sem = fw.new_sem("e_" + name)
        self.count = 0
        self.seen = {}
        self.selfwait = selfwait

    def wait(self, tok):
        if tok is None:
            return
        sem, val = tok
        if sem is self.sem and not self.selfwait:
            return
        k = id(sem)
        if self.seen.get(k, 0) >= val:
            return
        self.raw.wait_ge(sem, val)
        self.seen[k] = val


class FW:
    def __init__(self, nc, es):
        self.nc = nc
        self.es = es
        self.pe = Eng(self, nc.tensor, "pe", selfwait=False)
        self.act = Eng(self, nc.scalar, "act")
        self.dve = Eng(self, nc.vector, "dve")
        self.pool = Eng(self, nc.gpsimd, "pool")
        self.sp = Eng(self, nc.sync, "sp", selfwait=False)
        self.engs = [self.pe, self.act, self.dve, self.pool, self.sp]
        self.dma_res = []

    def new_sem(self, name):
        self.nsem = getattr(self, "nsem", 0) + 1
        return self.es.enter_context(self.nc.semaphore(f"{name}_{self.nsem}"))

    def deps(self, eng, reads, writes):
        best = {}

        def add(tok):
            if tok is None:
                return
            k = id(tok[0])
            if k not in best or best[k][1] < tok[1]:
                best[k] = tok
        for r in reads:
            add(r.writer)
        for w in writes:
            add(w.writer)
            for t in w.readers:
                add(t)
        for tok in best.values():
            eng.wait(tok)

    def commit(self, tok, reads, writes):
        for r in reads:
            r.readers.append(tok)
            if len(r.readers) > 16:
                best = {}
                for s, v in r.readers:
                    if id(s) not in best or best[id(s)][1] < v:
                        best[id(s)] = (s, v)
                r.readers = list(best.values())
        for w in writes:
            w.writer = tok
            w.readers = []

    def op(self, eng, fn, reads=(), writes=(), signal=True):
        self.deps(eng, reads, writes)
        ins = fn()
        if signal:
            eng.count += 1
            ins.then_inc(eng.sem, 1)
            tok = (eng.sem, eng.count)
            self.commit(tok, reads, writes)
            return tok
        return None

    def dma(self, eng, out, in_, slot, reads=(), writes=(), nowait=False, **kw):
        if slot.dsem is None:
            slot.dsem = self.new_sem("d_" + slot.name)
            self.dma_res.append(slot)
        if not nowait:
            self.deps(eng, reads, writes)
        ins = eng.raw.dma_start(out=out, in_=in_, **kw)
        slot.dcount += 16
        ins.then_inc(slot.dsem, 16)
        tok = (slot.dsem, slot.dcount)
        self.commit(tok, reads, writes)
        return tok

    def barrier(self):
        toks = [(e.sem, e.count) for e in self.engs if e.count > 0]
        toks += [(r.dsem, r.dcount) for r in self.dma_res if r.dcount > 0 and not r.nobar]
        for e in self.engs:
            for t in toks:
                if t[0] is e.sem and not e.selfwait:
                    continue
                e.wait(t)

    def finish(self):
        toks = [(e.sem, e.count) for e in self.engs if e.count > 0]
        toks += [(r.dsem, r.dcount) for r in self.dma_res if r.dcount > 0]
        for t in toks:
            self.sp.wait(t)


class _Stop(Exception):
    pass


def build_program():
    nc = bass.Bass("TRN2", target_bir_lowering=False)
    STAGE = float(os.environ.get("KSTAGE", "99"))

    stopped = {"v": False}

    def stage(x):
        if STAGE <= x:
            stopped["v"] = True
        return stopped["v"]

    def din(name, shape):
        return nc.dram_tensor(name, list(shape), F32, kind="ExternalInput").ap()

    def dout(name, shape):
        return nc.dram_tensor(name, list(shape), F32, kind="ExternalOutput").ap()

    xp_d = din("xp", (SEQ, D))
    xs_d = din("xs", (DEC, D))
    ck_d = din("ck", (DEPTH, PAST, 512))
    cv_d = din("cv", (DEPTH, PAST, 512))
    stp_d = din("stp", (DEPTH, 15, 256))
    stc_d = din("stc", (DEPTH, 2, 256))
    pp_d = din("pp", (128, NPP))
    pw_d = din("pwbd", (DEPTH, 2, 128, 128))
    wg_d = [din("f1g", (DEPTH, D, DFF)), din("f2g", (DEPTH, D, DFF))]
    wu_d = [din("f1u", (DEPTH, D, DFF)), din("f2u", (DEPTH, D, DFF))]
    wd_d = [din("f1d", (DEPTH, DFF, D)), din("f2d", (DEPTH, DFF, D))]
    win_d = din("win", (DEPTH, D, DIN))
    wout_d = din("wout", (DEPTH, D, D))

    yp_d = dout("yp", (SEQ, D))
    ys_d = dout("ys", (DEC, D))
    kp_d = dout("kp", (DEPTH, SEQ, 512))
    vp_d = dout("vp", (DEPTH, SEQ, 512))
    plp_d = dout("plp", (DEPTH, 15, 256))
    cvp_d = dout("cvp", (DEPTH, 2, 256))
    ks_d = dout("ks", (DEPTH, DEC, 512))
    vs_d = dout("vs", (DEPTH, DEC, 512))
    pls_d = dout("pls", (DEPTH, 15, 256))
    cvs_d = dout("cvs", (DEPTH, 2, 256))

    with ExitStack() as es:
        fw = FW(nc, es)
        pe, act, dve, pool, sp = fw.pe, fw.act, fw.dve, fw.pool, fw.sp

        uid = {"n": 0}

        def sb(st, name, shape, dt):
            uid["n"] += 1
            return st.enter_context(nc.sbuf_tensor(f"s{uid['n']}_{name}", list(shape), dt))

        xT = sb(es, "xT", (128, 8, NT), F32)
        R_x = [[Res(f"x{dc}_{tl}") for tl in range(NSUB)] for dc in range(8)]

        def RX(t0, n, dcs=range(8)):
            s0, s1 = t0 // 128, (t0 + n - 1) // 128
            return [R_x[dc][s] for dc in dcs for s in range(s0, s1 + 1)]

        pp = sb(es, "pp", (128, NPP), F32)
        R_pp = Res("pp")
        pwb = sb(es, "pwb", (128, DEPTH, 2, 128), BF16)
        R_pw = Res("pwb")
        idf = sb(es, "idf", (128, 128), F32)
        idb = sb(es, "idb", (128, 128), BF16)
        onesb = sb(es, "onesb", (128, 128), BF16)
        R_c = Res("consts")
        lamt = sb(es, "lamt", (128, DEPTH, 4), F32)
        subg = sb(es, "subg", (128, DEPTH, 128), F32)
        msk = sb(es, "msk", (128, 2, 4), F32)
        invw = sb(es, "invw", (128, 2), F32)
        cneg = sb(es, "cneg", (128, 1), F32)
        mrow = sb(es, "mrow", (128, 128), BF16)
        mcol = sb(es, "mcol", (128, 256), BF16)
        gu = [sb(es, f"gu{i}", (128, 2, 8, 512), BF16) for i in range(2)]
        dn = sb(es, "dn", (128, 2, 4, 1024), BF16)
        R_gu = [Res(f"gu{i}", nobar=True) for i in range(2)]
        R_dn = [Res(f"dn{i}", nobar=True) for i in range(2)]
        pb = [es.enter_context(nc.psum_tensor(f"pb{i}", [128, 512], F32)) for i in range(8)]
        R_pb = [Res(f"pb{i}") for i in range(8)]

        def ppc(c0, n=1):
            return pp[:, c0:c0 + n]

        def mm_group(bank_ap, pairs, reads, writes):
            fw.deps(pe, reads, writes)
            n = len(pairs)
            for i, (l, r) in enumerate(pairs):
                ins = nc.tensor.matmul(bank_ap, l, r, start=(i == 0), stop=(i == n - 1))
            pe.count += 1
            ins.then_inc(pe.sem, 1)
            tok = (pe.sem, pe.count)
            fw.commit(tok, reads, writes)
            return tok

        rr = {"n": 0}

        fw.dma(sp, pp[:], pp_d, R_pp, writes=[R_pp])
        fw.dma(pool, pwb[:], pw_d.rearrange("l c p e -> p l c e"), R_pw, writes=[R_pw])
        fw.op(pool, lambda: nc.gpsimd.memset(idf[:], 1.0), writes=[R_c])
        fw.op(pool, lambda: nc.gpsimd.affine_select(idf[:], idf[:], pattern=[[-1, 128]], compare_op=ALU.is_equal,
                                                   fill=0.0, base=0, channel_multiplier=1), reads=[R_c], writes=[R_c])
        fw.op(dve, lambda: nc.vector.tensor_copy(idb[:], idf[:]), reads=[R_c], writes=[R_c])
        fw.op(dve, lambda: nc.vector.memset(onesb[:], 1.0), writes=[R_c])
        fw.op(dve, lambda: nc.vector.memset(msk[:], 0.0), writes=[R_c])
        fw.op(dve, lambda: nc.vector.memset(msk[:, :, 0:1], 1.0), reads=[R_c], writes=[R_c])
        fw.op(dve, lambda: nc.vector.memset(msk[64:128, 0, 1:2], 1.0), reads=[R_c], writes=[R_c])
        fw.op(dve, lambda: nc.vector.memset(msk[:, 1, 1:3], 1.0), reads=[R_c], writes=[R_c])
        fw.op(dve, lambda: nc.vector.memset(msk[64:128, 1, 3:4], 1.0), reads=[R_c], writes=[R_c])
        fw.op(dve, lambda: nc.vector.memset(cneg[:], -0.5), reads=[R_c], writes=[R_c])
        fw.op(dve, lambda: nc.vector.memset(mrow[:], 0.0), reads=[R_c], writes=[R_c])
        fw.op(dve, lambda: nc.vector.memset(mrow[:, 64:128], 1.0), reads=[R_c], writes=[R_c])
        fw.op(dve, lambda: nc.vector.memset(mcol[:], 0.0), reads=[R_c], writes=[R_c])
        fw.op(dve, lambda: nc.vector.memset(mcol[:, 0:64], -30000.0), reads=[R_c], writes=[R_c])
        fw.op(dve, lambda: nc.vector.memset(invw[0:64, 0:1], 0.5), reads=[R_c], writes=[R_c])
        fw.op(dve, lambda: nc.vector.memset(invw[64:128, 0:1], 0.25), reads=[R_c], writes=[R_c])
        fw.op(dve, lambda: nc.vector.memset(invw[0:64, 1:2], 0.125), reads=[R_c], writes=[R_c])
        fw.op(dve, lambda: nc.vector.memset(invw[64:128, 1:2], 0.0625), reads=[R_c], writes=[R_c])
        with ExitStack() as ph:
            ltmp = sb(ph, "ltmp", (128, 64), F32)
            R_lt = Res("ltmp")
            for l in range(DEPTH):
                lam_init = 0.8 - 0.6 * math.exp(-0.3 * l)
                for j in range(2):
                    c0 = PP_LAM + l * 256 + j * 128
                    fw.op(dve, lambda: nc.vector.tensor_tensor(ltmp[:], ppc(c0, 64), ppc(c0 + 64, 64), ALU.mult),
                          reads=[R_pp, R_lt], writes=[R_lt])
                    fw.op(dve, lambda: nc.vector.reduce_sum(lamt[:, l, 2 + j:3 + j], ltmp[:], axis=AX.X),
                          reads=[R_lt, R_c], writes=[R_c])
                fw.op(act, lambda: nc.scalar.activation(lamt[:, l, 2:4], lamt[:, l, 2:4], AF.Exp), reads=[R_c], writes=[R_c])
                fw.op(dve, lambda: nc.vector.tensor_tensor(lamt[:, l, 0:1], lamt[:, l, 2:3], lamt[:, l, 3:4], ALU.subtract),
                      reads=[R_c], writes=[R_c])
                fw.op(dve, lambda: nc.vector.tensor_scalar(lamt[:, l, 0:1], lamt[:, l, 0:1], lam_init, None, ALU.add),
                      reads=[R_c], writes=[R_c])
                fw.op(dve, lambda: nc.vector.tensor_scalar(lamt[:, l, 1:2], lamt[:, l, 0:1], -1.0, None, ALU.mult),
                      reads=[R_c], writes=[R_c])
                fw.op(dve, lambda: nc.vector.tensor_scalar(subg[:, l, :], ppc(PP_SUB + l * 128, 128), 1.0 - lam_init, None, ALU.mult),
                      reads=[R_pp, R_c], writes=[R_c])
            fw.barrier()

        def rms_tile(st, gi, t0, n, dst_fn, dst_res, bank, sq_eng="act"):
            sq, R_sq, rstd, R_rs = st["sq"], st["R_sq"], st["rstd"], st["R_rs"]
            xres = RX(t0, n)
            for dc in range(8):
                i = rr["n"] % 4
                rr["n"] += 1
                if sq_eng == "pool" or (sq_eng == "mix" and dc % 2 == 1):
                    fw.op(pool, lambda: nc.gpsimd.tensor_tensor(sq[:, i, 0:n], xT[:, dc, t0:t0 + n], xT[:, dc, t0:t0 + n], ALU.mult),
                          reads=RX(t0, n, [dc]), writes=[R_sq[i]])
                else:
                    fw.op(act, lambda: nc.scalar.activation(sq[:, i, 0:n], xT[:, dc, t0:t0 + n], AF.Square),
                          reads=RX(t0, n, [dc]), writes=[R_sq[i]])
                fw.op(pe, lambda: nc.tensor.matmul(pb[bank][:, 0:n], onesb[:], sq[:, i, 0:n], start=(dc == 0), stop=(dc == 7)),
                      reads=[R_sq[i], R_c], writes=[R_pb[bank]])
            j = st["rsi"] % st["nrs"]
            st["rsi"] += 1
            fw.op(act, lambda: nc.scalar.activation(rstd[:, j, 0:n], pb[bank][:, 0:n], AF.Ln, bias=st["epsb"][:, 0:1], scale=1.0 / D),
                  reads=[R_pb[bank]], writes=[R_rs[j]])
            fw.op(act, lambda: nc.scalar.activation(rstd[:, j, 0:n], rstd[:, j, 0:n], AF.Exp, scale=-0.5),
                  reads=[R_rs[j]], writes=[R_rs[j]])
            for dc in range(8):
                fw.op(dve, lambda: nc.vector.scalar_tensor_tensor(dst_fn(dc), xT[:, dc, t0:t0 + n], ppc(PP_G + gi * 8 + dc),
                                                                  rstd[:, j, 0:n], ALU.mult, ALU.mult),
                      reads=[R_rs[j], R_pp] + RX(t0, n, [dc]), writes=[dst_res])

        def norm_bufs(st_, nmax=512, nrs=2):
            d = {}
            d["sq"] = sb(st_, "sq", (128, 4, nmax), BF16)
            d["R_sq"] = [Res(f"sq{i}") for i in range(4)]
            d["rstd"] = sb(st_, "rstd", (128, nrs, nmax), F32)
            d["R_rs"] = [Res(f"rs{i}") for i in range(nrs)]
            d["nrs"] = nrs
            d["epsb"] = sb(st_, "epsb", (128, 1), F32)
            d["rsi"] = 0
            fw.op(dve, lambda: nc.vector.memset(d["epsb"][:], EPS), writes=[d["R_rs"][0]])
            return d

        TILES = [(0, 512), (512, 512), (1024, 512), (1536, 512), (2048, 32)]

        def load_cols(dst_ap, w_ap, l, c0, ncol, slot_res, nowait=False):
            src = w_ap[l, :, c0:c0 + ncol].rearrange("(dc p) c -> p dc c", p=128)
            fw.dma(pool, dst_ap, src, slot_res, writes=[slot_res], nowait=nowait)

        def load_rows(dst_ap, w_ap, l, r0, nr, slot_res):
            src = w_ap[l, r0:r0 + nr, :].rearrange("(fc p) n -> p fc n", p=128)
            fw.dma(pool, dst_ap, src, slot_res, writes=[slot_res])

        PARTS = [(0, 4), (512, 4), (1024, 4), (1536, 4), (2048, 4), (2560, 2)]

        def ffn_load(l, w, p):
            c0, nf = PARTS[p]
            s = p % 2
            load_cols(gu[s][:, 0, :, 0:nf * 128], wg_d[w], l, c0, nf * 128, R_gu[s])
            load_cols(gu[s][:, 1, :, 0:nf * 128], wu_d[w], l, c0, nf * 128, R_gu[s], nowait=True)
            load_rows(dn[:, s, 0:nf, :], wd_d[w], l, c0, nf * 128, R_dn[s])

        ffn_load(0, 0, 0)
        ffn_load(0, 0, 1)

        def xload_gen(st_):
            xst = sb(st_, "xst", (128, 2, D), F32)
            R_xst = [Res("xst0"), Res("xst1")]
            for s in range(NSUB):
                n = 128 if s < 16 else DEC
                src = xp_d[s * 128:(s + 1) * 128, :] if s < 16 else xs_d
                k = s % 2
                fw.dma(sp, xst[0:n, k, :], src, R_xst[k], writes=[R_xst[k]])
                for hb in range(2):
                    bank = (7, 5)[hb]
                    fw.deps(pe, [R_xst[k], R_c], [R_pb[bank]])
                    for q in range(4):
                        dc = hb * 4 + q
                        ins = nc.tensor.transpose(pb[bank][:, q * 128:q * 128 + n], xst[0:n, k, dc * 128:(dc + 1) * 128], idf[0:n, 0:n])
                    pe.count += 1
                    ins.then_inc(pe.sem, 1)
                    fw.commit((pe.sem, pe.count), [R_xst[k], R_c], [R_pb[bank]])
                    src_ap = pb[bank][:].rearrange("p (q t) -> p q t", q=4)[:, :, 0:n]
                    dst_ap = xT[:, hb * 4:hb * 4 + 4, s * 128:s * 128 + n]
                    if hb == 0:
                        fw.op(act, lambda: nc.scalar.copy(dst_ap, src_ap), reads=[R_pb[bank]],
                              writes=[R_x[dc][s] for dc in range(hb * 4, hb * 4 + 4)])
                    else:
                        fw.op(dve, lambda: nc.vector.tensor_copy(dst_ap, src_ap), reads=[R_pb[bank]],
                              writes=[R_x[dc][s] for dc in range(hb * 4, hb * 4 + 4)])
                yield s

        FT = [(i * 416, 416) for i in range(5)]

        def ffn(l, w, gi, preloaded, prefetch=None, xload=False):
            TILES = FT
            with ExitStack() as ph:
                hT = sb(ph, "hT", (128, 8, NT), BF16)
                R_h = [Res(f"h{t}") for t in range(5)]
                aT = sb(ph, "aT", (128, 4, NT), BF16)
                R_a = [[Res(f"a{fc}_{t}") for t in range(5)] for fc in range(4)]
                sg = sb(ph, "sg", (128, 2, 512), F32)
                R_sg = [Res("sg0"), Res("sg1")]
                nb = norm_bufs(ph)
                if preloaded < 1:
                    ffn_load(l, w, 0)
                if preloaded < 2:
                    ffn_load(l, w, 1)

                def ffn_rms(ti_):
                    t0_, n_ = TILES[ti_]
                    rms_tile(nb, gi, t0_, n_, lambda dc: hT[:, dc, t0_:t0_ + n_], R_h[ti_], 6, sq_eng="mix")

                xg = xload_gen(ph) if xload else None

                def need_tokens(upto):
                    if xg is not None:
                        while xstate["s"] < min(NSUB - 1, (upto - 1) // 128):
                            xstate["s"] = next(xg)

                xstate = {"s": -1}
                need_tokens(TILES[0][0] + TILES[0][1])
                ffn_rms(0)
                cnt = 0
                for p, (c0, nf) in enumerate(PARTS):
                    s = p % 2
                    for ti, (t0, n) in enumerate(TILES):
                        if p == 0 and ti + 1 < len(TILES):
                            need_tokens(TILES[ti + 1][0] + TILES[ti + 1][1])
                            ffn_rms(ti + 1)
                        for fc in range(nf):
                            gb = cnt % 2
                            ub = 2 + cnt % 2
                            cnt += 1
                            mm_group(pb[gb][:, 0:n], [(gu[s][:, 0, dc, fc * 128:(fc + 1) * 128], hT[:, dc, t0:t0 + n]) for dc in range(8)],
                                     [R_gu[s], R_h[ti]], [R_pb[gb]])
                            mm_group(pb[ub][:, 0:n], [(gu[s][:, 1, dc, fc * 128:(fc + 1) * 128], hT[:, dc, t0:t0 + n]) for dc in range(8)],
                                     [R_gu[s], R_h[ti]], [R_pb[ub]])
                            k = gb
                            fw.op(act, lambda: nc.scalar.activation(sg[:, k, 0:n], pb[gb][:, 0:n], AF.Silu),
                                  reads=[R_pb[gb]], writes=[R_sg[k]])
                            fw.op(dve, lambda: nc.vector.tensor_tensor(aT[:, fc, t0:t0 + n], sg[:, k, 0:n], pb[ub][:, 0:n], ALU.mult),
                                  reads=[R_sg[k], R_pb[ub]], writes=[R_a[fc][ti]])
                    dcnt = 0
                    for ti, (t0, n) in enumerate(TILES):
                        for dco in range(8):
                            db = 4 + dcnt % 2
                            dcnt += 1
                            mm_group(pb[db][:, 0:n], [(dn[:, s, fc, dco * 128:(dco + 1) * 128], aT[:, fc, t0:t0 + n]) for fc in range(nf)],
                                     [R_dn[s]] + [R_a[fc][ti] for fc in range(nf)], [R_pb[db]])
                            xr = RX(t0, n, [dco])
                            fw.op(dve, lambda: nc.vector.scalar_tensor_tensor(xT[:, dco, t0:t0 + n], pb[db][:, 0:n], 0.5,
                                                                              xT[:, dco, t0:t0 + n], ALU.mult, ALU.add),
                                  reads=[R_pb[db]] + xr, writes=xr)
                    if p + 2 < len(PARTS):
                        ffn_load(l, w, p + 2)
                    elif p == len(PARTS) - 2 and prefetch is not None:
                        prefetch()
                fw.barrier()

        def rope(src_bank, n, s, dst_ap, dst_res, tmp, R_tmp):
            sv = pb[src_bank][0:n, :].rearrange("p (g h d) -> p g h d", g=8, h=2)
            dv = dst_ap.rearrange("p (g h d) -> p g h d", g=8, h=2)
            cosb = pp[0:n, PP_COS + s * 32:PP_COS + s * 32 + 32].unsqueeze(1).to_broadcast([n, 8, 32])
            sinb = pp[0:n, PP_SIN + s * 32:PP_SIN + s * 32 + 32].unsqueeze(1).to_broadcast([n, 8, 32])
            tv = [tmp[0:n, i, 0:256].rearrange("p (g d) -> p g d", g=8) for i in range(4)]
            rd = [R_pb[src_bank], R_pp]
            fw.op(dve, lambda: nc.vector.tensor_tensor(tv[0], sv[:, :, 0, :], cosb, ALU.mult), reads=rd, writes=[R_tmp[0]])
            fw.op(dve, lambda: nc.vector.tensor_tensor(tv[1], sv[:, :, 1, :], sinb, ALU.mult), reads=rd, writes=[R_tmp[1]])
            fw.op(dve, lambda: nc.vector.tensor_tensor(tv[2], sv[:, :, 1, :], cosb, ALU.mult), reads=rd, writes=[R_tmp[2]])
            fw.op(dve, lambda: nc.vector.tensor_tensor(tv[3], sv[:, :, 0, :], sinb, ALU.mult), reads=rd, writes=[R_tmp[3]])
            fw.op(pool, lambda: nc.gpsimd.tensor_tensor(dv[:, :, 0, :], tv[0], tv[1], ALU.subtract),
                  reads=[R_tmp[0], R_tmp[1]], writes=[dst_res])
            fw.op(pool, lambda: nc.gpsimd.tensor_tensor(dv[:, :, 1, :], tv[2], tv[3], ALU.add),
                  reads=[R_tmp[2], R_tmp[3]], writes=[dst_res])

        def transpose4(src_fn, n, dst_ap, src_res, dst_res, bank, eng):
            pbb = pb[bank][:].bitcast(BF16)
            fw.deps(pe, [src_res, R_c], [R_pb[bank]])
            for h in range(4):
                ins = nc.tensor.transpose(pbb[:, h * 128:h * 128 + n], src_fn(h), idb[0:n, 0:n])
            pe.count += 1
            ins.then_inc(pe.sem, 1)
            fw.commit((pe.sem, pe.count), [src_res, R_c], [R_pb[bank]])
            src_ap = pbb[:, 0:512].rearrange("p (h t) -> p h t", h=4)[:, :, 0:n]
            if eng is act:
                fw.op(act, lambda: nc.scalar.copy(dst_ap, src_ap), reads=[R_pb[bank]], writes=[dst_res])
            else:
                fw.op(dve, lambda: nc.vector.tensor_copy(dst_ap, src_ap), reads=[R_pb[bank]], writes=[dst_res])

        def mixer_preload(l):
            load_cols(gu[0][:, 0], win_d, l, 512, 512, R_gu[0])
            load_cols(gu[0][:, 1], win_d, l, 1024, 512, R_gu[0], nowait=True)

        def mixer(l):
            gi = l * 3 + 1
            with ExitStack() as mph:
                kT = sb(mph, "kT", (128, NH, NT), BF16)
                Vx = sb(mph, "Vx", (128, NSUB, NH, 129), BF16)
                R_kT = [Res(f"kT{s}") for s in range(NSUB)]
                R_V = [Res(f"V{s}") for s in range(NSUB)]
                with ExitStack() as ph:
                    nb = norm_bufs(ph)
                    hTt = sb(ph, "hTt", (128, 2, 8, 512), BF16)
                    R_ht = [Res("ht0"), Res("ht1")]
                    kst = sb(ph, "kst", (128, 2, 512), F32)
                    R_kst = [Res("kst0"), Res("kst1")]
                    vst = sb(ph, "vst", (128, 2, 512), F32)
                    R_vst = [Res("vst0"), Res("vst1")]
                    kbb = sb(ph, "kbb", (128, 2, 512), BF16)
                    R_kb = [Res("kb0"), Res("kb1")]
                    rtmp = sb(ph, "rtmp", (128, 8, 256), F32)
                    R_rt = [Res(f"rt{i}") for i in range(8)]
                    wk = gu[0][:, 0]
                    wv = gu[0][:, 1]
                    load_cols(gu[1][:, 0], win_d, l, 2048, 512, R_gu[1])
                    load_rows(dn[:].rearrange("p a f n -> p (a f) n"), wout_d, l, 0, D, R_dn[0])
                    fw.op(pool, lambda: nc.gpsimd.memset(Vx[:, :, :, 128:129], 1.0), writes=R_V)
                    pend = []

                    def flush_one():
                        m_, c_, s_ = pend.pop(0)
                        transpose4(lambda h: kbb[0:m_, c_, h * 128:(h + 1) * 128], m_, kT[:, :, s_ * 128:s_ * 128 + m_],
                                   R_kb[c_], R_kT[s_], 4 + c_, act)

                    def pe_warm(nmm):
                        fw.deps(pe, [R_gu[0], R_c], [R_pb[7]])
                        for _ in range(nmm):
                            ins = nc.tensor.matmul(pb[7][:, :], onesb[:], wk[:, 0, :], start=True, stop=True)
                        pe.count += 1
                        ins.then_inc(pe.sem, 1)
                        fw.commit((pe.sem, pe.count), [R_gu[0], R_c], [R_pb[7]])

                    def m1_rms(ti_):
                        t0_, n_ = TILES[ti_]
                        hs_ = ti_ % 2
                        if n_ >= 512:
                            pe_warm(20)
                        rms_tile(nb, gi, t0_, n_, lambda dc: hTt[:, hs_, dc, 0:n_], R_ht[hs_], 6, sq_eng="mix")

                    m1_rms(0)
                    for ti, (t0, n) in enumerate(TILES):
                        hs = ti % 2
                        if ti + 1 < len(TILES):
                            m1_rms(ti + 1)
                        for sub in range((n + 127) // 128):
                            s = t0 // 128 + sub
                            m = min(128, n - sub * 128)
                            c = s % 2
                            kb_, vb_ = c, 2 + c
                            mm_group(pb[kb_][0:m, :], [(hTt[:, hs, dc, sub * 128:sub * 128 + m], wk[:, dc, :]) for dc in range(8)],
                                     [R_ht[hs], R_gu[0]], [R_pb[kb_]])
                            mm_group(pb[vb_][0:m, :], [(hTt[:, hs, dc, sub * 128:sub * 128 + m], wv[:, dc, :]) for dc in range(8)],
                                     [R_ht[hs], R_gu[0]], [R_pb[vb_]])
                            rope(kb_, m, s, kst[0:m, c, :], R_kst[c], rtmp[:, 4 * c:4 * c + 4], R_rt[4 * c:4 * c + 4])
                            kdst = kp_d[l, s * 128:s * 128 + m, :] if s < 16 else ks_d[l]
                            fw.dma(sp, kdst, kst[0:m, c, :], R_kst[c], reads=[R_kst[c]])
                            fw.op(act, lambda: nc.scalar.copy(kbb[0:m, c, :], kst[0:m, c, :]), reads=[R_kst[c]], writes=[R_kb[c]])
                            fw.op(act, lambda: nc.scalar.copy(vst[0:m, c, :], pb[vb_][0:m, :]), reads=[R_pb[vb_]], writes=[R_vst[c]])
                            vdst = vp_d[l, s * 128:s * 128 + m, :] if s < 16 else vs_d[l]
                            fw.dma(sp, vdst, vst[0:m, c, :], R_vst[c], reads=[R_vst[c]])
                            fw.op(act, lambda: nc.scalar.copy(Vx[0:m, s, :, 0:128], vst[0:m, c, :].rearrange("p (h e) -> p h e", h=4)),
                                  reads=[R_vst[c]], writes=[R_V[s]])
                            pend.append((m, c, s))
                            if len(pend) > 1:
                                flush_one()
                    while pend:
                        flush_one()
                    fw.barrier()
                if stage(2 + 10 * l):
                    return
                wq = gu[0][:, 0]
                wuh = gu[0][:, 1]
                wbc = gu[1][:, 0]
                wo = dn[:].rearrange("p a f n -> p (a f) n")
                load_cols(wq, win_d, l, 0, 512, R_gu[0])
                load_cols(wuh, win_d, l, 1536, 512, R_gu[0], nowait=True)
                R_wo = [R_dn[0]]
                M2T = [(i * 256, 256) for i in range(8)] + [(2048, 32)]
                with ExitStack() as ph:
                    nb = norm_bufs(ph, 256, 1)
                    hT2v = gu[1][:, 1]
                    R_ht = [Res("h20"), Res("h21")]
                    qbb = sb(ph, "qbb", (128, 2, 512), BF16)
                    R_qb = [Res("qb0"), Res("qb1")]
                    rtmp = sb(ph, "rtmp2", (128, 4, 258), F32)
                    R_rt = [Res(f"rt2{i}") for i in range(4)]
                    qT = sb(ph, "qT", (128, 2, NH, 256), BF16)
                    R_qT = [[Res(f"qT{a_}{b_}") for b_ in range(2)] for a_ in range(2)]
                    PT = sb(ph, "PT", (128, 2, 2, 2, 256), BF16)
                    R_PT = [Res(f"PT{i}") for i in range(2)]
                    mixT = sb(ph, "mixT", (128, 8, 256), BF16)
                    R_mx = [Res(f"mx{c}") for c in range(8)]
                    ubuf = sb(ph, "ubuf", (128, 3, 2, 272), F32)
                    R_ub = [Res(f"ub{i}") for i in range(3)]
                    cvb = sb(ph, "cvb", (128, 2, 258), F32)
                    R_cvb = Res("cvb")
                    cvt = sb(ph, "cvt", (128, 2, 256), F32)
                    R_cvt = [Res("cvt0"), Res("cvt1")]
                    dbf = sb(ph, "dbf", (128, 2, 256), BF16)
                    R_dbf = [Res("dbf0"), Res("dbf1")]
                    oev = sb(ph, "oev", (128, 2, 2, 129), F32)
                    R_oev = [Res("oev0"), Res("oev1")]
                    osb = sb(ph, "osb", (128, 2, 128), F32)
                    R_os = [Res("os0"), Res("os1")]
                    pend_fin = []
                    sml = sb(ph, "sml", (128, 2, 8), F32)
                    R_sml = [Res("sml0"), Res("sml1")]
                    yat = sb(ph, "yat", (128, 2, 512), BF16)
                    R_yat = [Res("yat0"), Res("yat1")]
                    kcs = sb(ph, "kcs", (128, 2, 2, 512), BF16)
                    R_kcs = [Res("kcs0"), Res("kcs1")]
                    kcT = sb(ph, "kcT", (128, 2, NH, 256), BF16)
                    R_kcT = [Res("kcT0"), Res("kcT1")]
                    vcs = sb(ph, "vcs", (128, 2, 2, 512), BF16)
                    R_vcs = [Res("vcs0"), Res("vcs1")]
                    PTs = PT[:].rearrange("p s e c q -> p s (e c q)")[:, :, 0:512].rearrange("p s (c h j q) -> p s c h j q", c=2, h=NH, j=2)
                    R_PTs = [R_PT[0], R_PT[1]]
                    oacc = rtmp[0:DEC].rearrange("p a b -> p (a b)").rearrange("p (g e) -> p g e", g=8)
                    R_oacc = R_rt

                    def hTt_ap(hs, dc, a_, b_):
                        return hT2v[:, dc, hs * 256 + a_:hs * 256 + b_]

                    fw.op(pool, lambda: nc.gpsimd.memset(ubuf[:], 0.0), writes=R_ub)
                    fw.op(pool, lambda: nc.gpsimd.memset(cvb[:], 0.0), writes=[R_cvb])
                    NCB = 16

                    def load_ck(cb):
                        sl = cb % 2
                        fw.dma(pool, kcs[:, sl], ck_d[l, cb * 256:(cb + 1) * 256, :].rearrange("(j p) f -> p j f", p=128),
                               R_kcs[sl], writes=[R_kcs[sl]])

                    def load_cv(cb):
                        sl = cb % 2
                        fw.dma(pool, vcs[:, sl], cv_d[l, cb * 256:(cb + 1) * 256, :].rearrange("(j p) f -> p j f", p=128),
                               R_vcs[sl], writes=[R_vcs[sl]])

                    if l == 0 and os.environ.get("KDEBUG"):
                        print("M2 sbuf bytes remaining:", nc.sbuf_bytes_remaining)
                    load_ck(0)
                    load_cv(0)
                    load_ck(1)
                    load_cv(1)
                    misc = {"n": 0}

                    def mbank():
                        if misc.get("mode") == "in":
                            return 7
                        b = (7, 0, 1, 6)[misc["n"] % 4]
                        misc["n"] += 1
                        return b

                    ptc = {"n": 0, "e": 0}

                    class BV:
                        def __init__(self, ap, res):
                            self.ap = ap
                            self.res = res if isinstance(res, list) else [res]

                        def __getitem__(self, k):
                            return self.ap[k]

                    def flush_fin():
                        while pend_fin:
                            pend_fin.pop(0)()

                    def epilogue(m, h, src1, src2, yslot):
                        k = ptc["e"] % 2
                        ptc["e"] += 1
                        r1, r2 = src1.res, src2.res
                        ob_ = osb[0:m, k, :]
                        fw.op(dve, lambda: nc.vector.reciprocal(sml[0:m, k, 0:1], src1[0:m, 128:129]), reads=r1, writes=[R_sml[k]])
                        fw.op(dve, lambda: nc.vector.reciprocal(sml[0:m, k, 1:2], src2[0:m, 128:129]), reads=r2 + [R_sml[k]], writes=[R_sml[k]])
                        fw.op(dve, lambda: nc.vector.tensor_tensor(sml[0:m, k, 1:2], sml[0:m, k, 1:2], lamt[0:m, l, 1:2], ALU.mult),
                              reads=[R_sml[k], R_c], writes=[R_sml[k]])
                        fw.op(dve, lambda: nc.vector.tensor_scalar(ob_, src1[0:m, 0:128], sml[0:m, k, 0:1], None, ALU.mult),
                              reads=r1 + [R_sml[k]], writes=[R_os[k]])
                        fw.op(dve, lambda: nc.vector.scalar_tensor_tensor(ob_, src2[0:m, 0:128], sml[0:m, k, 1:2], ob_,
                                                                          ALU.mult, ALU.add),
                              reads=r2 + [R_sml[k], R_os[k]], writes=[R_os[k]])
                        fw.op(dve, lambda: nc.vector.tensor_tensor(src1[0:m, 0:128], ob_, ob_, ALU.mult),
                              reads=[R_os[k]] + r1, writes=r1)
                        fw.op(dve, lambda: nc.vector.reduce_sum(sml[0:m, k, 2:3], src1[0:m, 0:128], axis=AX.X),
                              reads=r1 + [R_sml[k]], writes=[R_sml[k]])
                        flush_fin()
                        fw.op(dve, lambda: nc.vector.tensor_scalar(sml[0:m, k, 3:4], sml[0:m, k, 2:3], 1.0 / 128, EPS, ALU.mult, ALU.add),
                              reads=[R_sml[k]], writes=[R_sml[k]])
                        fw.op(pool, lambda: nc.gpsimd.tensor_tensor(sml[0:m, k, 3:4], sml[0:m, k, 3:4], cneg[0:m, 0:1], ALU.pow),
                              reads=[R_sml[k], R_c], writes=[R_sml[k]])

                        def fin():
                            fw.op(dve, lambda: nc.vector.scalar_tensor_tensor(yat[0:m, yslot, h * 128:(h + 1) * 128], ob_, sml[0:m, k, 3:4],
                                                                              subg[0:m, l, :], ALU.mult, ALU.mult),
                                  reads=[R_os[k], R_sml[k], R_c], writes=[R_yat[yslot]])
                        pend_fin.append(fin)

                    def tinfo(ti):
                        t0, n = M2T[ti]
                        return t0, n, ti % 2, (t0 >= SEQ), (n + 127) // 128

                    def qchain(ti):
                        t0, n, hs, is_s, nsub = tinfo(ti)
                        sq, R_sq, rstd, R_rs = nb["sq"], nb["R_sq"], nb["rstd"], nb["R_rs"]
                        bank = mbank()

                        def square(dc):
                            fw.op(act, lambda: nc.scalar.activation(sq[:, dc % 4, 0:n], xT[:, dc, t0:t0 + n], AF.Square),
                                  reads=RX(t0, n, [dc]), writes=[R_sq[dc % 4]])

                        def ssmm(dc):
                            fw.op(pe, lambda: nc.tensor.matmul(pb[bank][:, 0:n], onesb[:], sq[:, dc % 4, 0:n], start=(dc == 0), stop=(dc == 7)),
                                  reads=[R_sq[dc % 4], R_c], writes=[R_pb[bank]])
                        for dc in range(4):
                            square(dc)
                        yield
                        for dc in range(4):
                            ssmm(dc)
                            square(4 + dc)
                        for dc in range(4, 8):
                            ssmm(dc)
                        j = 0
                        fw.op(act, lambda: nc.scalar.activation(rstd[:, j, 0:n], pb[bank][:, 0:n], AF.Ln, bias=nb["epsb"][:, 0:1], scale=1.0 / D),
                              reads=[R_pb[bank]], writes=[R_rs[j]])
                        fw.op(act, lambda: nc.scalar.activation(rstd[:, j, 0:n], rstd[:, j, 0:n], AF.Exp, scale=-0.5),
                              reads=[R_rs[j]], writes=[R_rs[j]])
                        for dc in range(8):
                            fw.op(dve, lambda: nc.vector.scalar_tensor_tensor(hTt_ap(hs, dc, 0, n), xT[:, dc, t0:t0 + n], ppc(PP_G + gi * 8 + dc),
                                                                              rstd[:, j, 0:n], ALU.mult, ALU.mult),
                                  reads=[R_rs[j], R_pp] + RX(t0, n, [dc]), writes=[R_ht[hs]])
                        yield
                        for sub in range(nsub):
                            s_ = t0 // 128 + sub
                            m = min(128, n - sub * 128)
                            c = s_ % 2
                            qb_ = mbank()
                            mm_group(pb[qb_][0:m, :], [(hTt_ap(hs, dc, sub * 128, sub * 128 + m), wq[:, dc, :]) for dc in range(8)],
                                     [R_ht[hs], R_gu[0]], [R_pb[qb_]])
                            rope(qb_, m, s_, qbb[0:m, c, :], R_qb[c], rtmp, R_rt)
                            yield
                        for sub in range(nsub):
                            s_ = t0 // 128 + sub
                            m = min(128, n - sub * 128)
                            c = s_ % 2
                            transpose4(lambda h: qbb[0:m, c, h * 128:(h + 1) * 128], m, qT[:, hs, :, sub * 128:sub * 128 + m],
                                       R_qb[c], R_qT[hs][sub], mbank(), act)
                            yield

                    def fproj(ti, wt, col, bank):
                        t0, n, hs, is_s, nsub = tinfo(ti)
                        mm_group(pb[bank][:, 0:n], [(wt[:, dc, col * 128:(col + 1) * 128], hTt_ap(hs, dc, 0, n)) for dc in range(8)],
                                 [R_ht[hs], R_gu[0], R_gu[1]], [R_pb[bank]])

                    def early(ti):
                        t0, n, hs, is_s, nsub = tinfo(ti)
                        if ti == 0:
                            pass
                        elif is_s:
                            with nc.allow_non_contiguous_dma(reason="tiny state halo loads"):
                                for ch in range(2):
                                    fw.dma(sp, ubuf[:, 0, ch, 1:16], stp_d[l, :, ch * 128:(ch + 1) * 128].rearrange("t p -> p t"), R_ub[0],
                                           writes=[R_ub[0]], nowait=(ch > 0))
                                    fw.dma(sp, cvb[:, ch, 0:2], stc_d[l, :, ch * 128:(ch + 1) * 128].rearrange("t p -> p t"), R_cvb,
                                           writes=[R_cvb], nowait=(ch > 0))
                        else:
                            fw.op(pool, lambda: nc.gpsimd.tensor_copy(ubuf[:, 0, :, 0:16], ubuf[:, 0, :, 256:272]), reads=[R_ub[0]], writes=[R_ub[0]])
                            fw.op(pool, lambda: nc.gpsimd.tensor_copy(cvb[:, :, 0:2], cvb[:, :, 256:258]), reads=[R_cvb], writes=[R_cvb])
                        for ch in range(2):
                            b = mbank()
                            fproj(ti, wuh, ch, b)
                            fw.op(dve, lambda: nc.vector.tensor_copy(ubuf[:, 0, ch, 16:16 + n], pb[b][:, 0:n]), reads=[R_pb[b]], writes=[R_ub[0]])
                        for ch in range(2):
                            b = mbank()
                            fproj(ti, wuh, 2 + ch, b)
                            fw.op(dve, lambda: nc.vector.tensor_copy(cvt[:, ch, 0:n], pb[b][:, 0:n]), reads=[R_pb[b]], writes=[R_cvt[ch]])
                        for ch in range(2):
                            b = mbank()
                            fproj(ti, wbc, 2 + ch, b)
                            fw.op(dve, lambda: nc.vector.tensor_tensor(cvb[:, ch, 2:2 + n], pb[b][:, 0:n], cvt[:, ch, 0:n], ALU.mult),
                                  reads=[R_pb[b], R_cvt[ch]], writes=[R_cvb])
                        if t0 + n == SEQ or is_s:
                            pd = pls_d if is_s else plp_d
                            cd = cvs_d if is_s else cvp_d
                            with nc.allow_non_contiguous_dma(reason="tiny state outputs"):
                                for ch in range(2):
                                    fw.dma(sp, pd[l, :, ch * 128:(ch + 1) * 128].rearrange("t p -> p t"), ubuf[:, 0, ch, 16 + n - 15:16 + n],
                                           R_ub[0], reads=[R_ub[0]])
                                    fw.dma(sp, cd[l, :, ch * 128:(ch + 1) * 128].rearrange("t p -> p t"), cvb[:, ch, n:n + 2],
                                           R_cvb, reads=[R_cvb])
                        for ch in range(2):
                            cw = PP_CW + l * 6 + ch * 3
                            fw.op(dve, lambda: nc.vector.tensor_scalar(cvt[:, ch, 0:n], cvb[:, ch, 0:n], ppc(cw), None, ALU.mult),
                                  reads=[R_cvb, R_pp], writes=[R_cvt[ch]])
                            fw.op(dve, lambda: nc.vector.scalar_tensor_tensor(cvt[:, ch, 0:n], cvb[:, ch, 1:1 + n], ppc(cw + 1), cvt[:, ch, 0:n],
                                                                              ALU.mult, ALU.add),
                                  reads=[R_cvb, R_pp, R_cvt[ch]], writes=[R_cvt[ch]])
                            fw.op(dve, lambda: nc.vector.scalar_tensor_tensor(cvt[:, ch, 0:n], cvb[:, ch, 2:2 + n], ppc(cw + 2), cvt[:, ch, 0:n],
                                                                              ALU.mult, ALU.add),
                                  reads=[R_cvb, R_pp, R_cvt[ch]], writes=[R_cvt[ch]])
                        W = 16 + n
                        cur = 0
                        for step in range(4):
                            sh = 1 << step
                            nxt = 1 + (step % 2)
                            lo = 2 * sh
                            for ch in range(2):
                                if ch == 0 and step >= 2:
                                    continue
                                fw.op(dve, lambda: nc.vector.scalar_tensor_tensor(ubuf[:, nxt, ch, lo:W], ubuf[:, cur, ch, lo - sh:W - sh],
                                                                                  msk[:, ch, step:step + 1], ubuf[:, cur, ch, lo:W],
                                                                                  ALU.mult, ALU.add),
                                      reads=[R_ub[cur], R_c], writes=[R_ub[nxt]])
                            cur = nxt
                        for ch in range(2):
                            fin = 2
                            fw.op(dve, lambda: nc.vector.scalar_tensor_tensor(dbf[:, ch, 0:n], ubuf[:, fin, ch, 16:16 + n], invw[:, ch:ch + 1],
                                                                              ubuf[:, 0, ch, 16:16 + n], ALU.mult, ALU.subtract),
                                  reads=[R_ub[fin], R_ub[0], R_c], writes=[R_dbf[ch]])
                            if ti == 0:
                                fw.op(pool, lambda: nc.gpsimd.tensor_tensor(ubuf[:, 1, ch, 0:16], ubuf[:, fin, ch, 16:32], ppc(PP_IC + ch * 16, 16), ALU.mult),
                                      reads=[R_ub[fin], R_pp, R_ub[1]], writes=[R_ub[1]])
                                fw.op(pool, lambda: nc.gpsimd.tensor_tensor(dbf[:, ch, 0:16], ubuf[:, 1, ch, 0:16], ubuf[:, 0, ch, 16:32], ALU.subtract),
                                      reads=[R_ub[1], R_ub[0], R_dbf[ch]], writes=[R_dbf[ch]])

                    def attention_prompt(ti, steps):
                        t0, n, hs, is_s, nsub = tinfo(ti)
                        g = ti
                        iters = [(h, kp) for h in range(NH) for kp in range(g + 1)]

                        def emit_S(i):
                            h, kp = iters[i]
                            banks = ((0, 1, 6)[(2 * i) % 3], (0, 1, 6)[(2 * i + 1) % 3])
                            slot = i % 2
                            rds = [R_kT[2 * kp], R_kT[2 * kp + 1], R_qT[hs][0], R_qT[hs][1], R_c]
                            for c in range(2):
                                sbk = banks[c]
                                fw.deps(pe, rds, [R_pb[sbk]])
                                for e in range(2):
                                    kt = 2 * kp + e
                                    c0 = 128 if kt == 2 * g + 1 else 0
                                    diag = (kt >= 2 * g)
                                    oap = pb[sbk][:, e * 256 + c0:e * 256 + 256]
                                    ins = nc.tensor.matmul(oap, kT[c * 64:(c + 1) * 64, h, kt * 128:(kt + 1) * 128],
                                                           qT[c * 64:(c + 1) * 64, hs, h, c0:256], start=True, stop=not diag)
                                    if diag:
                                        ins = nc.tensor.matmul(oap, mrow[c * 64:c * 64 + 1, :], mcol[c * 64:c * 64 + 1, 0:256 - c0],
                                                               start=False, stop=True)
                                    if c0 > 0:
                                        ins = nc.tensor.matmul(pb[sbk][:, e * 256:e * 256 + c0], mrow[c * 64:c * 64 + 1, :],
                                                               mcol[c * 64:c * 64 + 1, 64:64 + c0], start=True, stop=True)
                                pe.count += 1
                                ins.then_inc(pe.sem, 1)
                                fw.commit((pe.sem, pe.count), rds, [R_pb[sbk]])
                            for c in range(2):
                                sbk = banks[c]
                                fw.op(act, lambda: nc.scalar.activation(PT[:, slot, :, c, :], pb[sbk][:].rearrange("p (e q) -> p e q", e=2),
                                                                        AF.Exp, scale=SCALE),
                                      reads=[R_pb[sbk]], writes=[R_PT[slot]])

                        def emit_PV(i):
                            h, kp = iters[i]
                            slot = i % 2
                            closed = []
                            for e in range(2):
                                kt = 2 * kp + e
                                for qs in range(2):
                                    if kt > 2 * g + qs:
                                        continue
                                    for c in range(2):
                                        ob = 2 + qs * 2 + c
                                        first = (kt == 0)
                                        last = (kt == 2 * g + qs)
                                        fw.deps(pe, [R_PT[slot], R_V[kt]], [R_pb[ob]] if first else [])
                                        ins = nc.tensor.matmul(pb[ob][:, 0:129], PT[:, slot, e, c, qs * 128:(qs + 1) * 128], Vx[:, kt, h, :],
                                                               start=first, stop=last)
                                        pe.count += 1
                                        ins.then_inc(pe.sem, 1)
                                        fw.commit((pe.sem, pe.count), [R_PT[slot], R_V[kt]], [R_pb[ob]])
                                    if kt == 2 * g + qs:
                                        closed.append(qs)
                            ks = []
                            for qs in closed:
                                k = ptc["n"] % 2
                                ptc["n"] += 1
                                ks.append(k)
                                for c in range(2):
                                    ob = 2 + qs * 2 + c
                                    fw.op(dve, lambda: nc.vector.tensor_copy(oev[:, k, c, :], pb[ob][:, 0:129]), reads=[R_pb[ob]], writes=[R_oev[k]])
                            for qs, k in zip(closed, ks):
                                epilogue(128, h, BV(oev[:, k, 0, :], R_oev[k]), BV(oev[:, k, 1, :], R_oev[k]), qs)

                        nsteps = 6
                        stride = max(1, len(iters) // (nsteps + 1))
                        misc["mode"] = "in"
                        emit_S(0)
                        for i in range(len(iters)):
                            if i + 1 < len(iters):
                                emit_S(i + 1)
                            emit_PV(i)
                            if steps is not None and i % stride == stride - 1:
                                next(steps, None)
                        if steps is not None:
                            for _ in steps:
                                pass
                        misc["mode"] = "out"
                        wb = mbank()
                        fw.deps(pe, [R_gu[0], R_c], [R_pb[wb]])
                        for _ in range(12):
                            ins = nc.tensor.matmul(pb[wb][:, :], onesb[:], wq[:, 0, :], start=True, stop=True)
                        pe.count += 1
                        ins.then_inc(pe.sem, 1)
                        fw.commit((pe.sem, pe.count), [R_gu[0], R_c], [R_pb[wb]])
                        flush_fin()
                        for qs in range(2):
                            transpose4(lambda h: yat[:, qs, h * 128:(h + 1) * 128], 128, mixT[:, 0:4, qs * 128:(qs + 1) * 128],
                                       R_yat[qs], R_mx[qs], mbank(), dve)
                        return [R_mx[0], R_mx[1]]

                    def attention_sample(ti):
                        t0, n, hs, is_s, nsub = tinfo(ti)
                        fw.op(pool, lambda: nc.gpsimd.memset(oacc, 0.0), writes=R_oacc)

                        def blk(b_):
                            cache = b_ < NCB
                            return cache, b_ % 2, (2 if cache else 1), (128 if cache else DEC)

                        def emit_T(b_):
                            cache, sl, nj, kp_ = blk(b_)
                            if not cache:
                                return
                            pbb = pb[5][:].bitcast(BF16)
                            fw.deps(pe, [R_kcs[sl], R_c], [R_pb[5]])
                            for j in range(2):
                                for h in range(NH):
                                    ins = nc.tensor.transpose(pbb[:, (h * 2 + j) * 128:(h * 2 + j + 1) * 128], kcs[:, sl, j, h * 128:(h + 1) * 128], idb[:])
                            pe.count += 1
                            ins.then_inc(pe.sem, 1)
                            fw.commit((pe.sem, pe.count), [R_kcs[sl], R_c], [R_pb[5]])
                            fw.op(dve, lambda: nc.vector.tensor_copy(kcT[:, sl], pbb[:].rearrange("p (h k) -> p h k", h=NH)),
                                  reads=[R_pb[5]], writes=[R_kcT[sl]])

                        def emit_S(b_):
                            cache, sl, nj, kp_ = blk(b_)
                            kres = [R_kcT[sl]] if cache else [R_kT[16]]
                            banks = (0, 1) if b_ % 2 == 0 else (6, 7)
                            for c in range(2):
                                sbk = banks[c]
                                Sv = pb[sbk][:, 0:256].rearrange("p (h j q) -> p h j q", h=4, j=2)
                                fw.deps(pe, kres + [R_qT[hs][0]], [R_pb[sbk]])
                                for h in range(NH):
                                    for j in range(nj):
                                        if cache:
                                            lk = kcT[c * 64:(c + 1) * 64, sl, h, j * 128:(j + 1) * 128]
                                        else:
                                            lk = kT[c * 64:(c + 1) * 64, h, SEQ:SEQ + DEC]
                                        ins = nc.tensor.matmul(Sv[0:kp_, h, j, :], lk, qT[c * 64:(c + 1) * 64, hs, h, 0:DEC],
                                                               start=True, stop=True)
                                pe.count += 1
                                ins.then_inc(pe.sem, 1)
                                fw.commit((pe.sem, pe.count), kres + [R_qT[hs][0]], [R_pb[sbk]])
                            for c in range(2):
                                sbk = banks[c]
                                Sv2 = pb[sbk][:, 0:256].rearrange("p (h j q) -> p h j q", h=4, j=2)
                                fw.op(act, lambda: nc.scalar.activation(PTs[0:kp_, sl, c, :, 0:nj, :], Sv2[0:kp_, :, 0:nj, :],
                                                                        AF.Exp, scale=SCALE),
                                      reads=[R_pb[sbk]], writes=[R_PTs[sl]])

                        def emit_PV(b_):
                            cache, sl, nj, kp_ = blk(b_)
                            vres = [R_vcs[sl]] if cache else [R_V[16]]
                            for bi, grp in enumerate([(0, 1, 2), (3, 4, 5), (6, 7)]):
                                ob = 2 + bi
                                rds = vres + [R_PTs[sl], R_c]
                                fw.deps(pe, rds, [R_pb[ob]])
                                for gi_, hc_ in enumerate(grp):
                                    h, c = hc_ // 2, hc_ % 2
                                    if cache:
                                        for j in range(nj):
                                            ins = nc.tensor.matmul(pb[ob][0:DEC, gi_ * 129:gi_ * 129 + 128], PTs[0:kp_, sl, c, h, j, :],
                                                                   vcs[0:kp_, sl, j, h * 128:(h + 1) * 128], start=(j == 0), stop=(j == nj - 1))
                                        for j in range(nj):
                                            ins = nc.tensor.matmul(pb[ob][0:DEC, gi_ * 129 + 128:gi_ * 129 + 129], PTs[0:kp_, sl, c, h, j, :],
                                                                   onesb[0:kp_, 0:1], start=(j == 0), stop=(j == nj - 1))
                                    else:
                                        ins = nc.tensor.matmul(pb[ob][0:DEC, gi_ * 129:(gi_ + 1) * 129], PTs[0:kp_, sl, c, h, 0, :],
                                                               Vx[0:kp_, 16, h, :], start=True, stop=True)
                                pe.count += 1
                                ins.then_inc(pe.sem, 1)
                                fw.commit((pe.sem, pe.count), rds, [R_pb[ob]])
                                ng = len(grp)
                                oav = oacc[:, grp[0]:grp[0] + ng, :].rearrange("p g e -> p (g e)")
                                fw.op(dve, lambda: nc.vector.tensor_tensor(oav, oav, pb[ob][0:DEC, 0:ng * 129], ALU.add),
                                      reads=[R_pb[ob]] + R_oacc, writes=R_oacc)

                        NB = NCB + 1
                        emit_T(0)
                        emit_S(0)
                        for b_ in range(NB):
                            if b_ + 1 < NB:
                                emit_T(b_ + 1)
                                if b_ + 2 < NCB:
                                    load_ck(b_ + 2)
                                emit_S(b_ + 1)
                            emit_PV(b_)
                            if b_ + 2 < NCB:
                                load_cv(b_ + 2)
                        for h in range(NH):
                            epilogue(DEC, h, BV(oacc[:, 2 * h, :], R_oacc), BV(oacc[:, 2 * h + 1, :], R_oacc), 0)
                        flush_fin()
                        transpose4(lambda h: yat[0:DEC, 0, h * 128:(h + 1) * 128], DEC, mixT[:, 0:4, 0:DEC],
                                   R_yat[0], R_mx[0], mbank(), dve)
                        return [R_mx[0]]

                    def tail(ti, attn_res):
                        t0, n, hs, is_s, nsub = tinfo(ti)
                        for ch in range(2):
                            b = mbank()
                            fproj(ti, wbc, ch, b)
                            fw.op(dve, lambda: nc.vector.tensor_tensor(mixT[:, 6 + ch, 0:n], pb[b][:, 0:n], cvt[:, ch, 0:n], ALU.mult),
                                  reads=[R_pb[b], R_cvt[ch]], writes=[R_mx[6 + ch]])
                        for ch in range(2):
                            b = mbank()
                            fw.op(pe, lambda: nc.tensor.matmul(pb[b][:, 0:n], pwb[:, l, ch, :], dbf[:, ch, 0:n], start=True, stop=True),
                                  reads=[R_pw, R_dbf[ch]], writes=[R_pb[b]])
                            fw.op(dve, lambda: nc.vector.tensor_scalar(mixT[:, 4 + ch, 0:n], pb[b][:, 0:n], ppc(PP_PS + l * 2 + ch), None, ALU.mult),
                                  reads=[R_pb[b], R_pp], writes=[R_mx[4 + ch]])
                        for dco in range(8):
                            b = mbank()
                            mm_group(pb[b][:, 0:n], [(wo[:, mc, dco * 128:(dco + 1) * 128], mixT[:, mc, 0:n]) for mc in range(8)],
                                     R_wo + attn_res + R_mx[4:8], [R_pb[b]])
                            xr = RX(t0, n, [dco])
                            fw.op(dve, lambda: nc.vector.tensor_tensor(xT[:, dco, t0:t0 + n], pb[b][:, 0:n], xT[:, dco, t0:t0 + n], ALU.add),
                                  reads=[R_pb[b]] + xr, writes=xr)

                    for _ in qchain(0):
                        pass
                    for ti in range(len(M2T)):
                        early(ti)
                        steps = qchain(ti + 1) if ti + 1 < len(M2T) else None
                        if ti < 8:
                            ares = attention_prompt(ti, steps)
                        else:
                            ares = attention_sample(ti)
                        tail(ti, ares)
                    fw.barrier()

        stage(0)
        for l in range(DEPTH):
            if stopped["v"]:
                break
            ffn(l, 0, l * 3 + 0, 2 if l == 0 else 1, prefetch=lambda: mixer_preload(l), xload=(l == 0))
            if stage(1 + 10 * l):
                break
            mixer(l)
            if stage(5 + 10 * l):
                break
            ffn(l, 1, l * 3 + 2, 0, prefetch=(lambda: ffn_load(l + 1, 0, 0)) if l + 1 < DEPTH else None)
            if stage(6 + 10 * l):
                break
        if stopped["v"]:
            fw.finish()
            return nc

        with ExitStack() as ph:
            nb = norm_bufs(ph)
            yf = sb(ph, "yf", (128, 2, 8, 512), F32)
            R_yf = [Res("yf0"), Res("yf1")]
            yst = sb(ph, "yst", (128, 2, D), F32)
            R_yst = [Res("yst0"), Res("yst1")]
            FTL = [(0, 512), (512, 512), (1024, 512), (1536, 512), (2048, 32)]

            def fin_out(ti, sub):
                t0, nt = FTL[ti]
                ks = ti % 2
                s = t0 // 128 + sub
                n = min(128, nt - sub * 128)
                k = s % 2
                for hb in range(2):
                    bank = (2 * s + hb) % 4
                    fw.deps(pe, [R_yf[ks], R_c], [R_pb[bank]])
                    for q in range(4):
                        dc = hb * 4 + q
                        ins = nc.tensor.transpose(pb[bank][0:n, q * 128:(q + 1) * 128], yf[:, ks, dc, sub * 128:sub * 128 + n], idf[:])
                    pe.count += 1
                    ins.then_inc(pe.sem, 1)
                    fw.commit((pe.sem, pe.count), [R_yf[ks], R_c], [R_pb[bank]])
                    if hb == 0:
                        fw.op(act, lambda: nc.scalar.copy(yst[0:n, k, 0:512], pb[bank][0:n, :]), reads=[R_pb[bank]], writes=[R_yst[k]])
                    else:
                        fw.op(dve, lambda: nc.vector.tensor_copy(yst[0:n, k, 512:1024], pb[bank][0:n, :]), reads=[R_pb[bank]], writes=[R_yst[k]])
                dst = yp_d[s * 128:(s + 1) * 128, :] if s < 16 else ys_d
                fw.dma(sp, dst, yst[0:n, k, :], R_yst[k], reads=[R_yst[k]])

            def fin_rms(ti):
                t0, nt = FTL[ti]
                ks = ti % 2
                rms_tile(nb, 6, t0, nt, lambda dc: yf[:, ks, dc, 0:nt], R_yf[ks], 6 + ti % 2, sq_eng="mix")

            fin_rms(0)
            for ti, (t0, nt) in enumerate(FTL):
                if ti + 1 < len(FTL):
                    fin_rms(ti + 1)
                for sub in range((nt + 127) // 128):
                    fin_out(ti, sub)
            fw.finish()
    return nc


def _param_pack(p):
    pp = np.zeros((128, NPP), np.float32)
    gains = [p["norm_ffn1"][0], p["norm_mix"][0], p["norm_ffn2"][0],
             p["norm_ffn1"][1], p["norm_mix"][1], p["norm_ffn2"][1], p["final_norm"]]
    for gi, g in enumerate(gains):
        pp[:, PP_G + gi * 8:PP_G + gi * 8 + 8] = np.asarray(g, np.float32).reshape(8, 128).T
    for l in range(DEPTH):
        pp[:, PP_PS + l * 2:PP_PS + l * 2 + 2] = np.asarray(p["pool_scale"][l], np.float32).reshape(2, 128).T
        cw = np.asarray(p["conv_w"][l], np.float32)
        for ch in range(2):
            pp[:, PP_CW + l * 6 + ch * 3:PP_CW + l * 6 + ch * 3 + 3] = cw[:, ch * 128:(ch + 1) * 128].T
        pp[:, PP_SUB + l * 128:PP_SUB + (l + 1) * 128] = np.asarray(p["subln"][l], np.float32)[None, :]
        for j, nm in enumerate(["lambda_q1", "lambda_k1", "lambda_q2", "lambda_k2"]):
            pp[:, PP_LAM + l * 256 + j * 64:PP_LAM + l * 256 + (j + 1) * 64] = np.asarray(p[nm][l], np.float32)[None, :]
    half = 32
    inv = np.power(np.float32(10000.0), -np.arange(half, dtype=np.float32) / np.float32(half)).astype(np.float32)
    pos = np.concatenate([np.arange(SEQ), PAST + np.arange(DEC)]).astype(np.float32)
    ang = (pos[:, None] * inv[None, :]).astype(np.float32)
    cos = np.cos(ang).astype(np.float32)
    sin = np.sin(ang).astype(np.float32)
    for s in range(NSUB):
        n = 128 if s < 16 else DEC
        pp[0:n, PP_COS + s * 32:PP_COS + (s + 1) * 32] = cos[s * 128:s * 128 + n]
        pp[0:n, PP_SIN + s * 32:PP_SIN + (s + 1) * 32] = sin[s * 128:s * 128 + n]
    wins = [2, 4, 8, 16]
    for ch in range(2):
        for half_i in range(2):
            w = wins[ch * 2 + half_i]
            for t in range(16):
                pp[half_i * 64:(half_i + 1) * 64, PP_IC + ch * 16 + t] = 1.0 / min(t + 1, w)
    return pp


_NC_CACHE = {}


def kernel(**inputs):
    p = {k: np.asarray(v) for k, v in inputs.items()}
    if "nc" not in _NC_CACHE:
        _NC_CACHE["nc"] = build_program()
    nc = _NC_CACHE["nc"]
    pp = _param_pack(p)
    pw = np.asarray(p["pool_w"], np.float32)
    pwbd = np.zeros((DEPTH, 2, 128, 128), np.float32)
    for l in range(DEPTH):
        for g in range(4):
            ch, o = g // 2, (g % 2) * 64
            pwbd[l, ch, o:o + 64, o:o + 64] = pw[l, g]
    shared = {
        "pp": pp, "pwbd": pwbd,
        "f1g": np.ascontiguousarray(p["ffn1_gate"], np.float32), "f1u": np.ascontiguousarray(p["ffn1_up"], np.float32),
        "f1d": np.ascontiguousarray(p["ffn1_down"], np.float32),
        "f2g": np.ascontiguousarray(p["ffn2_gate"], np.float32), "f2u": np.ascontiguousarray(p["ffn2_up"], np.float32),
        "f2d": np.ascontiguousarray(p["ffn2_down"], np.float32),
        "win": np.ascontiguousarray(p["w_in"], np.float32), "wout": np.ascontiguousarray(p["w_out"], np.float32),
    }
    in_maps = []
    for b in range(8):
        m = dict(shared)
        m["xp"] = np.ascontiguousarray(p["x_prompt"][b], np.float32)
        m["xs"] = np.ascontiguousarray(p["x_sample"][b], np.float32)
        m["ck"] = np.ascontiguousarray(p["cache_k"][:, b].reshape(DEPTH, PAST, 512), np.float32)
        m["cv"] = np.ascontiguousarray(p["cache_v"][:, b].reshape(DEPTH, PAST, 512), np.float32)
        m["stp"] = np.ascontiguousarray(p["state_pool"][:, b], np.float32)
        m["stc"] = np.ascontiguousarray(p["state_conv"][:, b], np.float32)
        in_maps.append(m)
    ncores = int(os.environ.get("KCORES", "8"))
    res = run_bass_kernel_spmd(nc, in_maps[:ncores], core_ids=list(range(ncores)))
    R = list(res.results)
    while len(R) < 8:
        R.append(R[0])

    def gather(name, shape_tail, axis_b):
        arrs = [np.asarray(R[b][name], np.float32) for b in range(8)]
        return np.stack(arrs, axis=axis_b)

    y_prompt = gather("yp", None, 0)
    y_sample = gather("ys", None, 0)
    k_prompt = gather("kp", None, 1).reshape(DEPTH, 8, SEQ, NH, 128)
    v_prompt = gather("vp", None, 1).reshape(DEPTH, 8, SEQ, NH, 128)
    pool_prompt = gather("plp", None, 1)
    conv_prompt = gather("cvp", None, 1)
    k_sample = gather("ks", None, 1).reshape(DEPTH, 8, DEC, NH, 128)
    v_sample = gather("vs", None, 1).reshape(DEPTH, 8, DEC, NH, 128)
    pool_sample = gather("pls", None, 1)
    conv_sample = gather("cvs", None, 1)
    return (y_prompt, y_sample, k_prompt, v_prompt, pool_prompt, conv_prompt,
            k_sample, v_sample, pool_sample, conv_sample)
```

```python
import math
import os
from contextlib import ExitStack

import numpy as np
import concourse.bass as bass
import concourse.mybir as mybir
from concourse.bass_utils import run_bass_kernel_spmd

F32 = mybir.dt.float32
BF16 = mybir.dt.bfloat16
ALU = mybir.AluOpType
AF = mybir.ActivationFunctionType
AX = mybir.AxisListType

D = 1024
SEQ = 2048
DEC = 32
NT = SEQ + DEC
PAST = 4096
DEPTH = 2
DFF = 2816
DIN = 2560
NH = 4
EPS = 1e-6
SCALE = 64 ** -0.5
NSUB = 17

PP_G = 0
PP_PS = PP_G + 56
PP_CW = PP_PS + 4
PP_SUB = PP_CW + 12
PP_LAM = PP_SUB + 256
PP_COS = PP_LAM + 512
PP_SIN = PP_COS + 544
PP_IC = PP_SIN + 544
NPP = PP_IC + 32


class Res:
    __slots__ = ("name", "writer", "readers", "dsem", "dcount", "nobar")

    def __init__(self, name, nobar=False):
        self.name = name
        self.writer = None
        self.readers = []
        self.dsem = None
        self.dcount = 0
        self.nobar = nobar


class Eng:
    def __init__(self, fw, raw, name, selfwait=True):
        self.fw = fw
        self.raw = raw
        self.name = name
        self.sem = fw.new_sem("e_" + name)
        self.count = 0
        self.seen = {}
        self.selfwait = selfwait

    def wait(self, tok):
        if tok is None:
            return
        sem, val = tok
        if sem is self.sem and not self.selfwait:
            return
        k = id(sem)
        if self.seen.get(k, 0) >= val:
            return
        self.raw.wait_ge(sem, val)
        self.seen[k] = val


class FW:
    def __init__(self, nc, es):
        self.nc = nc
        self.es = es
        self.pe = Eng(self, nc.tensor, "pe", selfwait=False)
        self.act = Eng(self, nc.scalar, "act")
        self.dve = Eng(self, nc.vector, "dve")
        self.pool = Eng(self, nc.gpsimd, "pool")
        self.sp = Eng(self, nc.sync, "sp", selfwait=False)
        self.engs = [self.pe, self.act, self.dve, self.pool, self.sp]
        self.dma_res = []

    def new_sem(self, name):
        self.nsem = getattr(self, "nsem", 0) + 1
        return self.es.enter_context(self.nc.semaphore(f"{name}_{self.nsem}"))

    def deps(self, eng, reads, writes):
        best = {}

        def add(tok):
            if tok is None:
                return
            k = id(tok[0])
            if k not in best or best[k][1] < tok[1]:
                best[k] = tok
        for r in reads:
            add(r.writer)
        for w in writes:
            add(w.writer)
            for t in w.readers:
                add(t)
        for tok in best.values():
            eng.wait(tok)

    def commit(self, tok, reads, writes):
        for r in reads:
            r.readers.append(tok)
            if len(r.readers) > 16:
                best = {}
                for s, v in r.readers:
                    if id(s) not in best or best[id(s)][1] < v:
                        best[id(s)] = (s, v)
                r.readers = list(best.values())
        for w in writes:
            w.writer = tok
            w.readers = []

    def op(self, eng, fn, reads=(), writes=(), signal=True):
        self.deps(eng, reads, writes)
        ins = fn()
        if signal:
            eng.count += 1
            ins.then_inc(eng.sem, 1)
            tok = (eng.sem, eng.count)
            self.commit(tok, reads, writes)
            return tok
        return None

    def dma(self, eng, out, in_, slot, reads=(), writes=(), nowait=False, **kw):
        if slot.dsem is None:
            slot.dsem = self.new_sem("d_" + slot.name)
            self.dma_res.append(slot)
        if not nowait:
            self.deps(eng, reads, writes)
        ins = eng.raw.dma_start(out=out, in_=in_, **kw)
        slot.dcount += 16
        ins.then_inc(slot.dsem, 16)
        tok = (slot.dsem, slot.dcount)
        self.commit(tok, reads, writes)
        return tok

    def barrier(self):
        toks = [(e.sem, e.count) for e in self.engs if e.count > 0]
        toks += [(r.dsem, r.dcount) for r in self.dma_res if r.dcount > 0 and not r.nobar]
        for e in self.engs:
            for t in toks:
                if t[0] is e.sem and not e.selfwait:
                    continue
                e.wait(t)

    def finish(self):
        toks = [(e.sem, e.count) for e in self.engs if e.count > 0]
        toks += [(r.dsem, r.dcount) for r in self.dma_res if r.dcount > 0]
        for t in toks:
            self.sp.wait(t)


class _Stop(Exception):
    pass


def build_program():
    nc = bass.Bass("TRN2", target_bir_lowering=False)
    STAGE = float(os.environ.get("KSTAGE", "99"))

    stopped = {"v": False}

    def stage(x):
        if STAGE <= x:
            stopped["v"] = True
        return stopped["v"]

    def din(name, shape):
        return nc.dram_tensor(name, list(shape), F32, kind="ExternalInput").ap()

    def dout(name, shape):
        return nc.dram_tensor(name, list(shape), F32, kind="ExternalOutput").ap()

    xp_d = din("xp", (SEQ, D))
    xs_d = din("xs", (DEC, D))
    ck_d = din("ck", (DEPTH, PAST, 512))
    cv_d = din("cv", (DEPTH, PAST, 512))
    stp_d = din("stp", (DEPTH, 15, 256))
    stc_d = din("stc", (DEPTH, 2, 256))
    pp_d = din("pp", (128, NPP))
    pw_d = din("pwbd", (DEPTH, 2, 128, 128))
    wg_d = [din("f1g", (DEPTH, D, DFF)), din("f2g", (DEPTH, D, DFF))]
    wu_d = [din("f1u", (DEPTH, D, DFF)), din("f2u", (DEPTH, D, DFF))]
    wd_d = [din("f1d", (DEPTH, DFF, D)), din("f2d", (DEPTH, DFF, D))]
    win_d = din("win", (DEPTH, D, DIN))
    wout_d = din("wout", (DEPTH, D, D))

    yp_d = dout("yp", (SEQ, D))
    ys_d = dout("ys", (DEC, D))
    kp_d = dout("kp", (DEPTH, SEQ, 512))
    vp_d = dout("vp", (DEPTH, SEQ, 512))
    plp_d = dout("plp", (DEPTH, 15, 256))
    cvp_d = dout("cvp", (DEPTH, 2, 256))
    ks_d = dout("ks", (DEPTH, DEC, 512))
    vs_d = dout("vs", (DEPTH, DEC, 512))
    pls_d = dout("pls", (DEPTH, 15, 256))
    cvs_d = dout("cvs", (DEPTH, 2, 256))

    with ExitStack() as es:
        fw = FW(nc, es)
        pe, act, dve, pool, sp = fw.pe, fw.act, fw.dve, fw.pool, fw.sp

        uid = {"n": 0}

        def sb(st, name, shape, dt):
            uid["n"] += 1
            return st.enter_context(nc.sbuf_tensor(f"s{uid['n']}_{name}", list(shape), dt))

        xT = sb(es, "xT", (128, 8, NT), F32)
        R_x = [[Res(f"x{dc}_{tl}") for tl in range(NSUB)] for dc in range(8)]

        def RX(t0, n, dcs=range(8)):
            s0, s1 = t0 // 128, (t0 + n - 1) // 128
            return [R_x[dc][s] for dc in dcs for s in range(s0, s1 + 1)]

        pp = sb(es, "pp", (128, NPP), F32)
        R_pp = Res("pp")
        pwb = sb(es, "pwb", (128, DEPTH, 2, 128), BF16)
        R_pw = Res("pwb")
        idf = sb(es, "idf", (128, 128), F32)
        idb = sb(es, "idb", (128, 128), BF16)
        onesb = sb(es, "onesb", (128, 128), BF16)
        R_c = Res("consts")
        lamt = sb(es, "lamt", (128, DEPTH, 4), F32)
        subg = sb(es, "subg", (128, DEPTH, 128), F32)
        msk = sb(es, "msk", (128, 2, 4), F32)
        invw = sb(es, "invw", (128, 2), F32)
        cneg = sb(es, "cneg", (128, 1), F32)
        mrow = sb(es, "mrow", (128, 128), BF16)
        mcol = sb(es, "mcol", (128, 256), BF16)
        gu = [sb(es, f"gu{i}", (128, 2, 8, 512), BF16) for i in range(2)]
        dn = sb(es, "dn", (128, 2, 4, 1024), BF16)
        R_gu = [Res(f"gu{i}", nobar=True) for i in range(2)]
        R_dn = [Res(f"dn{i}", nobar=True) for i in range(2)]
        pb = [es.enter_context(nc.psum_tensor(f"pb{i}", [128, 512], F32)) for i in range(8)]
        R_pb = [Res(f"pb{i}") for i in range(8)]

        def ppc(c0, n=1):
            return pp[:, c0:c0 + n]

        def mm_group(bank_ap, pairs, reads, writes):
            fw.deps(pe, reads, writes)
            n = len(pairs)
            for i, (l, r) in enumerate(pairs):
                ins = nc.tensor.matmul(bank_ap, l, r, start=(i == 0), stop=(i == n - 1))
            pe.count += 1
            ins.then_inc(pe.sem, 1)
            tok = (pe.sem, pe.count)
            fw.commit(tok, reads, writes)
            return tok

        rr = {"n": 0}

        fw.dma(sp, pp[:], pp_d, R_pp, writes=[R_pp])
        fw.dma(pool, pwb[:], pw_d.rearrange("l c p e -> p l c e"), R_pw, writes=[R_pw])
        fw.op(pool, lambda: nc.gpsimd.memset(idf[:], 1.0), writes=[R_c])
        fw.op(pool, lambda: nc.gpsimd.affine_select(idf[:], idf[:], pattern=[[-1, 128]], compare_op=ALU.is_equal,
                                                   fill=0.0, base=0, channel_multiplier=1), reads=[R_c], writes=[R_c])
        fw.op(dve, lambda: nc.vector.tensor_copy(idb[:], idf[:]), reads=[R_c], writes=[R_c])
        fw.op(dve, lambda: nc.vector.memset(onesb[:], 1.0), writes=[R_c])
        fw.op(dve, lambda: nc.vector.memset(msk[:], 0.0), writes=[R_c])
        fw.op(dve, lambda: nc.vector.memset(msk[:, :, 0:1], 1.0), reads=[R_c], writes=[R_c])
        fw.op(dve, lambda: nc.vector.memset(msk[64:128, 0, 1:2], 1.0), reads=[R_c], writes=[R_c])
        fw.op(dve, lambda: nc.vector.memset(msk[:, 1, 1:3], 1.0), reads=[R_c], writes=[R_c])
        fw.op(dve, lambda: nc.vector.memset(msk[64:128, 1, 3:4], 1.0), reads=[R_c], writes=[R_c])
        fw.op(dve, lambda: nc.vector.memset(cneg[:], -0.5), reads=[R_c], writes=[R_c])
        fw.op(dve, lambda: nc.vector.memset(mrow[:], 0.0), reads=[R_c], writes=[R_c])
        fw.op(dve, lambda: nc.vector.memset(mrow[:, 64:128], 1.0), reads=[R_c], writes=[R_c])
        fw.op(dve, lambda: nc.vector.memset(mcol[:], 0.0), reads=[R_c], writes=[R_c])
        fw.op(dve, lambda: nc.vector.memset(mcol[:, 0:64], -30000.0), reads=[R_c], writes=[R_c])
        fw.op(dve, lambda: nc.vector.memset(invw[0:64, 0:1], 0.5), reads=[R_c], writes=[R_c])
        fw.op(dve, lambda: nc.vector.memset(invw[64:128, 0:1], 0.25), reads=[R_c], writes=[R_c])
        fw.op(dve, lambda: nc.vector.memset(invw[0:64, 1:2], 0.125), reads=[R_c], writes=[R_c])
        fw.op(dve, lambda: nc.vector.memset(invw[64:128, 1:2], 0.0625), reads=[R_c], writes=[R_c])
        with ExitStack() as ph:
            ltmp = sb(ph, "ltmp", (128, 64), F32)
            R_lt = Res("ltmp")
            for l in range(DEPTH):
                lam_init = 0.8 - 0.6 * math.exp(-0.3 * l)
                for j in range(2):
                    c0 = PP_LAM + l * 256 + j * 128
                    fw.op(dve, lambda: nc.vector.tensor_tensor(ltmp[:], ppc(c0, 64), ppc(c0 + 64, 64), ALU.mult),
                          reads=[R_pp, R_lt], writes=[R_lt])
                    fw.op(dve, lambda: nc.vector.reduce_sum(lamt[:, l, 2 + j:3 + j], ltmp[:], axis=AX.X),
                          reads=[R_lt, R_c], writes=[R_c])
                fw.op(act, lambda: nc.scalar.activation(lamt[:, l, 2:4], lamt[:, l, 2:4], AF.Exp), reads=[R_c], writes=[R_c])
                fw.op(dve, lambda: nc.vector.tensor_tensor(lamt[:, l, 0:1], lamt[:, l, 2:3], lamt[:, l, 3:4], ALU.subtract),
                      reads=[R_c], writes=[R_c])
                fw.op(dve, lambda: nc.vector.tensor_scalar(lamt[:, l, 0:1], lamt[:, l, 0:1], lam_init, None, ALU.add),
                      reads=[R_c], writes=[R_c])
                fw.op(dve, lambda: nc.vector.tensor_scalar(lamt[:, l, 1:2], lamt[:, l, 0:1], -1.0, None, ALU.mult),
                      reads=[R_c], writes=[R_c])
                fw.op(dve, lambda: nc.vector.tensor_scalar(subg[:, l, :], ppc(PP_SUB + l * 128, 128), 1.0 - lam_init, None, ALU.mult),
                      reads=[R_pp, R_c], writes=[R_c])
            fw.barrier()

        def rms_tile(st, gi, t0, n, dst_fn, dst_res, bank, sq_eng="act"):
            sq, R_sq, rstd, R_rs = st["sq"], st["R_sq"], st["rstd"], st["R_rs"]
            xres = RX(t0, n)
            for dc in range(8):
                i = rr["n"] % 4
                rr["n"] += 1
                if sq_eng == "pool" or (sq_eng == "mix" and dc % 2 == 1):
                    fw.op(pool, lambda: nc.gpsimd.tensor_tensor(sq[:, i, 0:n], xT[:, dc, t0:t0 + n], xT[:, dc, t0:t0 + n], ALU.mult),
                          reads=RX(t0, n, [dc]), writes=[R_sq[i]])
                else:
                    fw.op(act, lambda: nc.scalar.activation(sq[:, i, 0:n], xT[:, dc, t0:t0 + n], AF.Square),
                          reads=RX(t0, n, [dc]), writes=[R_sq[i]])
                fw.op(pe, lambda: nc.tensor.matmul(pb[bank][:, 0:n], onesb[:], sq[:, i, 0:n], start=(dc == 0), stop=(dc == 7)),
                      reads=[R_sq[i], R_c], writes=[R_pb[bank]])
            j = st["rsi"] % st["nrs"]
            st["rsi"] += 1
            fw.op(act, lambda: nc.scalar.activation(rstd[:, j, 0:n], pb[bank][:, 0:n], AF.Ln, bias=st["epsb"][:, 0:1], scale=1.0 / D),
                  reads=[R_pb[bank]], writes=[R_rs[j]])
            fw.op(act, lambda: nc.scalar.activation(rstd[:, j, 0:n], rstd[:, j, 0:n], AF.Exp, scale=-0.5),
                  reads=[R_rs[j]], writes=[R_rs[j]])
            for dc in range(8):
                fw.op(dve, lambda: nc.vector.scalar_tensor_tensor(dst_fn(dc), xT[:, dc, t0:t0 + n], ppc(PP_G + gi * 8 + dc),
                                                                  rstd[:, j, 0:n], ALU.mult, ALU.mult),
                      reads=[R_rs[j], R_pp] + RX(t0, n, [dc]), writes=[dst_res])

        def norm_bufs(st_, nmax=512, nrs=2):
            d = {}
            d["sq"] = sb(st_, "sq", (128, 4, nmax), BF16)
            d["R_sq"] = [Res(f"sq{i}") for i in range(4)]
            d["rstd"] = sb(st_, "rstd", (128, nrs, nmax), F32)
            d["R_rs"] = [Res(f"rs{i}") for i in range(nrs)]
            d["nrs"] = nrs
            d["epsb"] = sb(st_, "epsb", (128, 1), F32)
            d["rsi"] = 0
            fw.op(dve, lambda: nc.vector.memset(d["epsb"][:], EPS), writes=[d["R_rs"][0]])
            return d

        TILES = [(0, 512), (512, 512), (1024, 512), (1536, 512), (2048, 32)]

        def load_cols(dst_ap, w_ap, l, c0, ncol, slot_res, nowait=False):
            src = w_ap[l, :, c0:c0 + ncol].rearrange("(dc p) c -> p dc c", p=128)
            fw.dma(pool, dst_ap, src, slot_res, writes=[slot_res], nowait=nowait)

        def load_rows(dst_ap, w_ap, l, r0, nr, slot_res):
            src = w_ap[l, r0:r0 + nr, :].rearrange("(fc p) n -> p fc n", p=128)
            fw.dma(pool, dst_ap, src, slot_res, writes=[slot_res])

        PARTS = [(0, 4), (512, 4), (1024, 4), (1536, 4), (2048, 4), (2560, 2)]

        def ffn_load(l, w, p):
            c0, nf = PARTS[p]
            s = p % 2
            load_cols(gu[s][:, 0, :, 0:nf * 128], wg_d[w], l, c0, nf * 128, R_gu[s])
            load_cols(gu[s][:, 1, :, 0:nf * 128], wu_d[w], l, c0, nf * 128, R_gu[s], nowait=True)
            load_rows(dn[:, s, 0:nf, :], wd_d[w], l, c0, nf * 128, R_dn[s])

        ffn_load(0, 0, 0)
        ffn_load(0, 0, 1)

        def xload_gen(st_):
            xst = sb(st_, "xst", (128, 2, D), F32)
            R_xst = [Res("xst0"), Res("xst1")]
            for s in range(NSUB):
                n = 128 if s < 16 else DEC
                src = xp_d[s * 128:(s + 1) * 128, :] if s < 16 else xs_d
                k = s % 2
                fw.dma(sp, xst[0:n, k, :], src, R_xst[k], writes=[R_xst[k]])
                for hb in range(2):
                    bank = (7, 5)[hb]
                    fw.deps(pe, [R_xst[k], R_c], [R_pb[bank]])
                    for q in range(4):
                        dc = hb * 4 + q
                        ins = nc.tensor.transpose(pb[bank][:, q * 128:q * 128 + n], xst[0:n, k, dc * 128:(dc + 1) * 128], idf[0:n, 0:n])
                    pe.count += 1
                    ins.then_inc(pe.sem, 1)
                    fw.commit((pe.sem, pe.count), [R_xst[k], R_c], [R_pb[bank]])
                    src_ap = pb[bank][:].rearrange("p (q t) -> p q t", q=4)[:, :, 0:n]
                    dst_ap = xT[:, hb * 4:hb * 4 + 4, s * 128:s * 128 + n]
                    if hb == 0:
                        fw.op(act, lambda: nc.scalar.copy(dst_ap, src_ap), reads=[R_pb[bank]],
                              writes=[R_x[dc][s] for dc in range(hb * 4, hb * 4 + 4)])
                    else:
                        fw.op(dve, lambda: nc.vector.tensor_copy(dst_ap, src_ap), reads=[R_pb[bank]],
                              writes=[R_x[dc][s] for dc in range(hb * 4, hb * 4 + 4)])
                yield s

        FT = [(i * 416, 416) for i in range(5)]

        def ffn(l, w, gi, preloaded, prefetch=None, xload=False):
            TILES = FT
            with ExitStack() as ph:
                hT = sb(ph, "hT", (128, 8, NT), BF16)
                R_h = [Res(f"h{t}") for t in range(5)]
                aT = sb(ph, "aT", (128, 4, NT), BF16)
                R_a = [[Res(f"a{fc}_{t}") for t in range(5)] for fc in range(4)]
                sg = sb(ph, "sg", (128, 2, 512), F32)
                R_sg = [Res("sg0"), Res("sg1")]
                nb = norm_bufs(ph)
                if preloaded < 1:
                    ffn_load(l, w, 0)
                if preloaded < 2:
                    ffn_load(l, w, 1)

                def ffn_rms(ti_):
                    t0_, n_ = TILES[ti_]
                    rms_tile(nb, gi, t0_, n_, lambda dc: hT[:, dc, t0_:t0_ + n_], R_h[ti_], 6, sq_eng="mix")

                xg = xload_gen(ph) if xload else None

                def need_tokens(upto):
                    if xg is not None:
                        while xstate["s"] < min(NSUB - 1, (upto - 1) // 128):
                            xstate["s"] = next(xg)

                xstate = {"s": -1}
                need_tokens(TILES[0][0] + TILES[0][1])
                ffn_rms(0)
                cnt = 0
                for p, (c0, nf) in enumerate(PARTS):
                    s = p % 2
                    for ti, (t0, n) in enumerate(TILES):
                        if p == 0 and ti + 1 < len(TILES):
                            need_tokens(TILES[ti + 1][0] + TILES[ti + 1][1])
                            ffn_rms(ti + 1)
                        for fc in range(nf):
                            gb = cnt % 2
                            ub = 2 + cnt % 2
                            cnt += 1
                            mm_group(pb[gb][:, 0:n], [(gu[s][:, 0, dc, fc * 128:(fc + 1) * 128], hT[:, dc, t0:t0 + n]) for dc in range(8)],
                                     [R_gu[s], R_h[ti]], [R_pb[gb]])
                            mm_group(pb[ub][:, 0:n], [(gu[s][:, 1, dc, fc * 128:(fc + 1) * 128], hT[:, dc, t0:t0 + n]) for dc in range(8)],
                                     [R_gu[s], R_h[ti]], [R_pb[ub]])
                            k = gb
                            fw.op(act, lambda: nc.scalar.activation(sg[:, k, 0:n], pb[gb][:, 0:n], AF.Silu),
                                  reads=[R_pb[gb]], writes=[R_sg[k]])
                            fw.op(dve, lambda: nc.vector.tensor_tensor(aT[:, fc, t0:t0 + n], sg[:, k, 0:n], pb[ub][:, 0:n], ALU.mult),
                                  reads=[R_sg[k], R_pb[ub]], writes=[R_a[fc][ti]])
                    dcnt = 0
                    for ti, (t0, n) in enumerate(TILES):
                        for dco in range(8):
                            db = 4 + dcnt % 2
                            dcnt += 1
                            mm_group(pb[db][:, 0:n], [(dn[:, s, fc, dco * 128:(dco + 1) * 128], aT[:, fc, t0:t0 + n]) for fc in range(nf)],
                                     [R_dn[s]] + [R_a[fc][ti] for fc in range(nf)], [R_pb[db]])
                            xr = RX(t0, n, [dco])
                            fw.op(dve, lambda: nc.vector.scalar_tensor_tensor(xT[:, dco, t0:t0 + n], pb[db][:, 0:n], 0.5,
                                                                              xT[:, dco, t0:t0 + n], ALU.mult, ALU.add),
                                  reads=[R_pb[db]] + xr, writes=xr)
                    if p + 2 < len(PARTS):
                        ffn_load(l, w, p + 2)
                    elif p == len(PARTS) - 2 and prefetch is not None:
                        prefetch()
                fw.barrier()

        def rope(src_bank, n, s, dst_ap, dst_res, tmp, R_tmp):
            sv = pb[src_bank][0:n, :].rearrange("p (g h d) -> p g h d", g=8, h=2)
            dv = dst_ap.rearrange("p (g h d) -> p g h d", g=8, h=2)
            cosb = pp[0:n, PP_COS + s * 32:PP_COS + s * 32 + 32].unsqueeze(1).to_broadcast([n, 8, 32])
            sinb = pp[0:n, PP_SIN + s * 32:PP_SIN + s * 32 + 32].unsqueeze(1).to_broadcast([n, 8, 32])
            tv = [tmp[0:n, i, 0:256].rearrange("p (g d) -> p g d", g=8) for i in range(4)]
            rd = [R_pb[src_bank], R_pp]
            fw.op(dve, lambda: nc.vector.tensor_tensor(tv[0], sv[:, :, 0, :], cosb, ALU.mult), reads=rd, writes=[R_tmp[0]])
            fw.op(dve, lambda: nc.vector.tensor_tensor(tv[1], sv[:, :, 1, :], sinb, ALU.mult), reads=rd, writes=[R_tmp[1]])
            fw.op(dve, lambda: nc.vector.tensor_tensor(tv[2], sv[:, :, 1, :], cosb, ALU.mult), reads=rd, writes=[R_tmp[2]])
            fw.op(dve, lambda: nc.vector.tensor_tensor(tv[3], sv[:, :, 0, :], sinb, ALU.mult), reads=rd, writes=[R_tmp[3]])
            fw.op(pool, lambda: nc.gpsimd.tensor_tensor(dv[:, :, 0, :], tv[0], tv[1], ALU.subtract),
                  reads=[R_tmp[0], R_tmp[1]], writes=[dst_res])
            fw.op(pool, lambda: nc.gpsimd.tensor_tensor(dv[:, :, 1, :], tv[2], tv[3], ALU.add),
                  reads=[R_tmp[2], R_tmp[3]], writes=[dst_res])

        def transpose4(src_fn, n, dst_ap, src_res, dst_res, bank, eng):
            pbb = pb[bank][:].bitcast(BF16)
            fw.deps(pe, [src_res, R_c], [R_pb[bank]])
            for h in range(4):
                ins = nc.tensor.transpose(pbb[:, h * 128:h * 128 + n], src_fn(h), idb[0:n, 0:n])
            pe.count += 1
            ins.then_inc(pe.sem, 1)
            fw.commit((pe.sem, pe.count), [src_res, R_c], [R_pb[bank]])
            src_ap = pbb[:, 0:512].rearrange("p (h t) -> p h t", h=4)[:, :, 0:n]
            if eng is act:
                fw.op(act, lambda: nc.scalar.copy(dst_ap, src_ap), reads=[R_pb[bank]], writes=[dst_res])
            else:
                fw.op(dve, lambda: nc.vector.tensor_copy(dst_ap, src_ap), reads=[R_pb[bank]], writes=[dst_res])

        def mixer_preload(l):
            load_cols(gu[0][:, 0], win_d, l, 512, 512, R_gu[0])
            load_cols(gu[0][:, 1], win_d, l, 1024, 512, R_gu[0], nowait=True)

        def mixer(l):
            gi = l * 3 + 1
            with ExitStack() as mph:
                kT = sb(mph, "kT", (128, NH, NT), BF16)
                Vx = sb(mph, "Vx", (128, NSUB, NH, 129), BF16)
                R_kT = [Res(f"kT{s}") for s in range(NSUB)]
                R_V = [Res(f"V{s}") for s in range(NSUB)]
                with ExitStack() as ph:
                    nb = norm_bufs(ph)
                    hTt = sb(ph, "hTt", (128, 2, 8, 512), BF16)
                    R_ht = [Res("ht0"), Res("ht1")]
                    kst = sb(ph, "kst", (128, 2, 512), F32)
                    R_kst = [Res("kst0"), Res("kst1")]
                    vst = sb(ph, "vst", (128, 2, 512), F32)
                    R_vst = [Res("vst0"), Res("vst1")]
                    kbb = sb(ph, "kbb", (128, 2, 512), BF16)
                    R_kb = [Res("kb0"), Res("kb1")]
                    rtmp = sb(ph, "rtmp", (128, 8, 256), F32)
                    R_rt = [Res(f"rt{i}") for i in range(8)]
                    wk = gu[0][:, 0]
                    wv = gu[0][:, 1]
                    load_cols(gu[1][:, 0], win_d, l, 2048, 512, R_gu[1])
                    load_rows(dn[:].rearrange("p a f n -> p (a f) n"), wout_d, l, 0, D, R_dn[0])
                    fw.op(pool, lambda: nc.gpsimd.memset(Vx[:, :, :, 128:129], 1.0), writes=R_V)
                    pend = []

                    def flush_one():
                        m_, c_, s_ = pend.pop(0)
                        transpose4(lambda h: kbb[0:m_, c_, h * 128:(h + 1) * 128], m_, kT[:, :, s_ * 128:s_ * 128 + m_],
                                   R_kb[c_], R_kT[s_], 4 + c_, act)

                    def pe_warm(nmm):
                        fw.deps(pe, [R_gu[0], R_c], [R_pb[7]])
                        for _ in range(nmm):
                            ins = nc.tensor.matmul(pb[7][:, :], onesb[:], wk[:, 0, :], start=True, stop=True)
                        pe.count += 1
                        ins.then_inc(pe.sem, 1)
                        fw.commit((pe.sem, pe.count), [R_gu[0], R_c], [R_pb[7]])

                    def m1_rms(ti_):
                        t0_, n_ = TILES[ti_]
                        hs_ = ti_ % 2
                        if n_ >= 512:
                            pe_warm(20)
                        rms_tile(nb, gi, t0_, n_, lambda dc: hTt[:, hs_, dc, 0:n_], R_ht[hs_], 6, sq_eng="mix")

                    m1_rms(0)
                    for ti, (t0, n) in enumerate(TILES):
                        hs = ti % 2
                        if ti + 1 < len(TILES):
                            m1_rms(ti + 1)
                        for sub in range((n + 127) // 128):
                            s = t0 // 128 + sub
                            m = min(128, n - sub * 128)
                            c = s % 2
                            kb_, vb_ = c, 2 + c
                            mm_group(pb[kb_][0:m, :], [(hTt[:, hs, dc, sub * 128:sub * 128 + m], wk[:, dc, :]) for dc in range(8)],
                                     [R_ht[hs], R_gu[0]], [R_pb[kb_]])
                            mm_group(pb[vb_][0:m, :], [(hTt[:, hs, dc, sub * 128:sub * 128 + m], wv[:, dc, :]) for dc in range(8)],
                                     [R_ht[hs], R_gu[0]], [R_pb[vb_]])
                            rope(kb_, m, s, kst[0:m, c, :], R_kst[c], rtmp[:, 4 * c:4 * c + 4], R_rt[4 * c:4 * c + 4])
                            kdst = kp_d[l, s * 128:s * 128 + m, :] if s < 16 else ks_d[l]
                            fw.dma(sp, kdst, kst[0:m, c, :], R_kst[c], reads=[R_kst[c]])
                            fw.op(act, lambda: nc.scalar.copy(kbb[0:m, c, :], kst[0:m, c, :]), reads=[R_kst[c]], writes=[R_kb[c]])
                            fw.op(act, lambda: nc.scalar.copy(vst[0:m, c, :], pb[vb_][0:m, :]), reads=[R_pb[vb_]], writes=[R_vst[c]])
                            vdst = vp_d[l, s * 128:s * 128 + m, :] if s < 16 else vs_d[l]
                            fw.dma(sp, vdst, vst[0:m, c, :], R_vst[c], reads=[R_vst[c]])
                            fw.op(act, lambda: nc.scalar.copy(Vx[0:m, s, :, 0:128], vst[0:m, c, :].rearrange("p (h e) -> p h e", h=4)),
                                  reads=[R_vst[c]], writes=[R_V[s]])
                            pend.append((m, c, s))
                            if len(pend) > 1:
                                flush_one()
                    while pend:
                        flush_one()
                    fw.barrier()
                if stage(2 + 10 * l):
                    return
                wq = gu[0][:, 0]
                wuh = gu[0][:, 1]
                wbc = gu[1][:, 0]
                wo = dn[:].rearrange("p a f n -> p (a f) n")
                load_cols(wq, win_d, l, 0, 512, R_gu[0])
                load_cols(wuh, win_d, l, 1536, 512, R_gu[0], nowait=True)
                R_wo = [R_dn[0]]
                M2T = [(i * 256, 256) for i in range(8)] + [(2048, 32)]
                with ExitStack() as ph:
                    nb = norm_bufs(ph, 256, 1)
                    hT2v = gu[1][:, 1]
                    R_ht = [Res("h20"), Res("h21")]
                    qbb = sb(ph, "qbb", (128, 2, 512), BF16)
                    R_qb = [Res("qb0"), Res("qb1")]
                    rtmp = sb(ph, "rtmp2", (128, 4, 258), F32)
                    R_rt = [Res(f"rt2{i}") for i in range(4)]
                    qT = sb(ph, "qT", (128, 2, NH, 256), BF16)
                    R_qT = [[Res(f"qT{a_}{b_}") for b_ in range(2)] for a_ in range(2)]
                    PT = sb(ph, "PT", (128, 2, 2, 2, 256), BF16)
                    R_PT = [Res(f"PT{i}") for i in range(2)]
                    mixT = sb(ph, "mixT", (128, 8, 256), BF16)
                    R_mx = [Res(f"mx{c}") for c in range(8)]
                    ubuf = sb(ph, "ubuf", (128, 3, 2, 272), F32)
                    R_ub = [Res(f"ub{i}") for i in range(3)]
                    cvb = sb(ph, "cvb", (128, 2, 258), F32)
                    R_cvb = Res("cvb")
                    cvt = sb(ph, "cvt", (128, 2, 256), F32)
                    R_cvt = [Res("cvt0"), Res("cvt1")]
                    dbf = sb(ph, "dbf", (128, 2, 256), BF16)
                    R_dbf = [Res("dbf0"), Res("dbf1")]
                    oev = sb(ph, "oev", (128, 2, 2, 129), F32)
                    R_oev = [Res("oev0"), Res("oev1")]
                    osb = sb(ph, "osb", (128, 2, 128), F32)
                    R_os = [Res("os0"), Res("os1")]
                    pend_fin = []
                    sml = sb(ph, "sml", (128, 2, 8), F32)
                    R_sml = [Res("sml0"), Res("sml1")]
                    yat = sb(ph, "yat", (128, 2, 512), BF16)
                    R_yat = [Res("yat0"), Res("yat1")]
                    kcs = sb(ph, "kcs", (128, 2, 2, 512), BF16)
                    R_kcs = [Res("kcs0"), Res("kcs1")]
                    kcT = sb(ph, "kcT", (128, 2, NH, 256), BF16)
                    R_kcT = [Res("kcT0"), Res("kcT1")]
                    vcs = sb(ph, "vcs", (128, 2, 2, 512), BF16)
                    R_vcs = [Res("vcs0"), Res("vcs1")]
                    PTs = PT[:].rearrange("p s e c q -> p s (e c q)")[:, :, 0:512].rearrange("p s (c h j q) -> p s c h j q", c=2, h=NH, j=2)
                    R_PTs = [R_PT[0], R_PT[1]]
                    oacc = rtmp[0:DEC].rearrange("p a b -> p (a b)").rearrange("p (g e) -> p g e", g=8)
                    R_oacc = R_rt

                    def hTt_ap(hs, dc, a_, b_):
                        return hT2v[:, dc, hs * 256 + a_:hs * 256 + b_]

                    fw.op(pool, lambda: nc.gpsimd.memset(ubuf[:], 0.0), writes=R_ub)
                    fw.op(pool, lambda: nc.gpsimd.memset(cvb[:], 0.0), writes=[R_cvb])
                    NCB = 16

                    def load_ck(cb):
                        sl = cb % 2
                        fw.dma(pool, kcs[:, sl], ck_d[l, cb * 256:(cb + 1) * 256, :].rearrange("(j p) f -> p j f", p=128),
                               R_kcs[sl], writes=[R_kcs[sl]])

                    def load_cv(cb):
                        sl = cb % 2
                        fw.dma(pool, vcs[:, sl], cv_d[l, cb * 256:(cb + 1) * 256, :].rearrange("(j p) f -> p j f", p=128),
                               R_vcs[sl], writes=[R_vcs[sl]])

                    if l == 0 and os.environ.get("KDEBUG"):
                        print("M2 sbuf bytes remaining:", nc.sbuf_bytes_remaining)
                    load_ck(0)
                    load_cv(0)
                    load_ck(1)
                    load_cv(1)
                    misc = {"n": 0}

                    def mbank():
                        if misc.get("mode") == "in":
                            return 7
                        b = (7, 0, 1, 6)[misc["n"] % 4]
                        misc["n"] += 1
                        return b

                    ptc = {"n": 0, "e": 0}

                    class BV:
                        def __init__(self, ap, res):
                            self.ap = ap
                            self.res = res if isinstance(res, list) else [res]

                        def __getitem__(self, k):
                            return self.ap[k]

                    def flush_fin():
                        while pend_fin:
                            pend_fin.pop(0)()

                    def epilogue(m, h, src1, src2, yslot):
                        k = ptc["e"] % 2
                        ptc["e"] += 1
                        r1, r2 = src1.res, src2.res
                        ob_ = osb[0:m, k, :]
                        fw.op(dve, lambda: nc.vector.reciprocal(sml[0:m, k, 0:1], src1[0:m, 128:129]), reads=r1, writes=[R_sml[k]])
                        fw.op(dve, lambda: nc.vector.reciprocal(sml[0:m, k, 1:2], src2[0:m, 128:129]), reads=r2 + [R_sml[k]], writes=[R_sml[k]])
                        fw.op(dve, lambda: nc.vector.tensor_tensor(sml[0:m, k, 1:2], sml[0:m, k, 1:2], lamt[0:m, l, 1:2], ALU.mult),
                              reads=[R_sml[k], R_c], writes=[R_sml[k]])
                        fw.op(dve, lambda: nc.vector.tensor_scalar(ob_, src1[0:m, 0:128], sml[0:m, k, 0:1], None, ALU.mult),
                              reads=r1 + [R_sml[k]], writes=[R_os[k]])
                        fw.op(dve, lambda: nc.vector.scalar_tensor_tensor(ob_, src2[0:m, 0:128], sml[0:m, k, 1:2], ob_,
                                                                          ALU.mult, ALU.add),
                              reads=r2 + [R_sml[k], R_os[k]], writes=[R_os[k]])
                        fw.op(dve, lambda: nc.vector.tensor_tensor(src1[0:m, 0:128], ob_, ob_, ALU.mult),
                              reads=[R_os[k]] + r1, writes=r1)
                        fw.op(dve, lambda: nc.vector.reduce_sum(sml[0:m, k, 2:3], src1[0:m, 0:128], axis=AX.X),
                              reads=r1 + [R_sml[k]], writes=[R_sml[k]])
                        flush_fin()
                        fw.op(dve, lambda: nc.vector.tensor_scalar(sml[0:m, k, 3:4], sml[0:m, k, 2:3], 1.0 / 128, EPS, ALU.mult, ALU.add),
                              reads=[R_sml[k]], writes=[R_sml[k]])
                        fw.op(pool, lambda: nc.gpsimd.tensor_tensor(sml[0:m, k, 3:4], sml[0:m, k, 3:4], cneg[0:m, 0:1], ALU.pow),
                              reads=[R_sml[k], R_c], writes=[R_sml[k]])

                        def fin():
                            fw.op(dve, lambda: nc.vector.scalar_tensor_tensor(yat[0:m, yslot, h * 128:(h + 1) * 128], ob_, sml[0:m, k, 3:4],
                                                                              subg[0:m, l, :], ALU.mult, ALU.mult),
                                  reads=[R_os[k], R_sml[k], R_c], writes=[R_yat[yslot]])
                        pend_fin.append(fin)

                    def tinfo(ti):
                        t0, n = M2T[ti]
                        return t0, n, ti % 2, (t0 >= SEQ), (n + 127) // 128

                    def qchain(ti):
                        t0, n, hs, is_s, nsub = tinfo(ti)
                        sq, R_sq, rstd, R_rs = nb["sq"], nb["R_sq"], nb["rstd"], nb["R_rs"]
                        bank = mbank()

                        def square(dc):
                            fw.op(act, lambda: nc.scalar.activation(sq[:, dc % 4, 0:n], xT[:, dc, t0:t0 + n], AF.Square),
                                  reads=RX(t0, n, [dc]), writes=[R_sq[dc % 4]])

                        def ssmm(dc):
                            fw.op(pe, lambda: nc.tensor.matmul(pb[bank][:, 0:n], onesb[:], sq[:, dc % 4, 0:n], start=(dc == 0), stop=(dc == 7)),
                                  reads=[R_sq[dc % 4], R_c], writes=[R_pb[bank]])
                        for dc in range(4):
                            square(dc)
                        yield
                        for dc in range(4):
                            ssmm(dc)
                            square(4 + dc)
                        for dc in range(4, 8):
                            ssmm(dc)
                        j = 0
                        fw.op(act, lambda: nc.scalar.activation(rstd[:, j, 0:n], pb[bank][:, 0:n], AF.Ln, bias=nb["epsb"][:, 0:1], scale=1.0 / D),
                              reads=[R_pb[bank]], writes=[R_rs[j]])
                        fw.op(act, lambda: nc.scalar.activation(rstd[:, j, 0:n], rstd[:, j, 0:n], AF.Exp, scale=-0.5),
                              reads=[R_rs[j]], writes=[R_rs[j]])
                        for dc in range(8):
                            fw.op(dve, lambda: nc.vector.scalar_tensor_tensor(hTt_ap(hs, dc, 0, n), xT[:, dc, t0:t0 + n], ppc(PP_G + gi * 8 + dc),
                                                                              rstd[:, j, 0:n], ALU.mult, ALU.mult),
                                  reads=[R_rs[j], R_pp] + RX(t0, n, [dc]), writes=[R_ht[hs]])
                        yield
                        for sub in range(nsub):
                            s_ = t0 // 128 + sub
                            m = min(128, n - sub * 128)
                            c = s_ % 2
                            qb_ = mbank()
                            mm_group(pb[qb_][0:m, :], [(hTt_ap(hs, dc, sub * 128, sub * 128 + m), wq[:, dc, :]) for dc in range(8)],
                                     [R_ht[hs], R_gu[0]], [R_pb[qb_]])
                            rope(qb_, m, s_, qbb[0:m, c, :], R_qb[c], rtmp, R_rt)
                            yield
                        for sub in range(nsub):
                            s_ = t0 // 128 + sub
                            m = min(128, n - sub * 128)
                            c = s_ % 2
                            transpose4(lambda h: qbb[0:m, c, h * 128:(h + 1) * 128], m, qT[:, hs, :, sub * 128:sub * 128 + m],
                                       R_qb[c], R_qT[hs][sub], mbank(), act)
                            yield

                    def fproj(ti, wt, col, bank):
                        t0, n, hs, is_s, nsub = tinfo(ti)
                        mm_group(pb[bank][:, 0:n], [(wt[:, dc, col * 128:(col + 1) * 128], hTt_ap(hs, dc, 0, n)) for dc in range(8)],
                                 [R_ht[hs], R_gu[0], R_gu[1]], [R_pb[bank]])

                    def early(ti):
                        t0, n, hs, is_s, nsub = tinfo(ti)
                        if ti == 0:
                            pass
                        elif is_s:
                            with nc.allow_non_contiguous_dma(reason="tiny state halo loads"):
                                for ch in range(2):
                                    fw.dma(sp, ubuf[:, 0, ch, 1:16], stp_d[l, :, ch * 128:(ch + 1) * 128].rearrange("t p -> p t"), R_ub[0],
                                           writes=[R_ub[0]], nowait=(ch > 0))
                                    fw.dma(sp, cvb[:, ch, 0:2], stc_d[l, :, ch * 128:(ch + 1) * 128].rearrange("t p -> p t"), R_cvb,
                                           writes=[R_cvb], nowait=(ch > 0))
                        else:
                            fw.op(pool, lambda: nc.gpsimd.tensor_copy(ubuf[:, 0, :, 0:16], ubuf[:, 0, :, 256:272]), reads=[R_ub[0]], writes=[R_ub[0]])
                            fw.op(pool, lambda: nc.gpsimd.tensor_copy(cvb[:, :, 0:2], cvb[:, :, 256:258]), reads=[R_cvb], writes=[R_cvb])
                        for ch in range(2):
                            b = mbank()
                            fproj(ti, wuh, ch, b)
                            fw.op(dve, lambda: nc.vector.tensor_copy(ubuf[:, 0, ch, 16:16 + n], pb[b][:, 0:n]), reads=[R_pb[b]], writes=[R_ub[0]])
                        for ch in range(2):
                            b = mbank()
                            fproj(ti, wuh, 2 + ch, b)
                            fw.op(dve, lambda: nc.vector.tensor_copy(cvt[:, ch, 0:n], pb[b][:, 0:n]), reads=[R_pb[b]], writes=[R_cvt[ch]])
                        for ch in range(2):
                            b = mbank()
                            fproj(ti, wbc, 2 + ch, b)
                            fw.op(dve, lambda: nc.vector.tensor_tensor(cvb[:, ch, 2:2 + n], pb[b][:, 0:n], cvt[:, ch, 0:n], ALU.mult),
                                  reads=[R_pb[b], R_cvt[ch]], writes=[R_cvb])
                        if t0 + n == SEQ or is_s:
                            pd = pls_d if is_s else plp_d
                            cd = cvs_d if is_s else cvp_d
                            with nc.allow_non_contiguous_dma(reason="tiny state outputs"):
                                for ch in range(2):
                                    fw.dma(sp, pd[l, :, ch * 128:(ch + 1) * 128].rearrange("t p -> p t"), ubuf[:, 0, ch, 16 + n - 15:16 + n],
                                           R_ub[0], reads=[R_ub[0]])
                                    fw.dma(sp, cd[l, :, ch * 128:(ch + 1) * 128].rearrange("t p -> p t"), cvb[:, ch, n:n + 2],
                                           R_cvb, reads=[R_cvb])
                        for ch in range(2):
                            cw = PP_CW + l * 6 + ch * 3
                            fw.op(dve, lambda: nc.vector.tensor_scalar(cvt[:, ch, 0:n], cvb[:, ch, 0:n], ppc(cw), None, ALU.mult),
                                  reads=[R_cvb, R_pp], writes=[R_cvt[ch]])
                            fw.op(dve, lambda: nc.vector.scalar_tensor_tensor(cvt[:, ch, 0:n], cvb[:, ch, 1:1 + n], ppc(cw + 1), cvt[:, ch, 0:n],
                                                                              ALU.mult, ALU.add),
                                  reads=[R_cvb, R_pp, R_cvt[ch]], writes=[R_cvt[ch]])
                            fw.op(dve, lambda: nc.vector.scalar_tensor_tensor(cvt[:, ch, 0:n], cvb[:, ch, 2:2 + n], ppc(cw + 2), cvt[:, ch, 0:n],
                                                                              ALU.mult, ALU.add),
                                  reads=[R_cvb, R_pp, R_cvt[ch]], writes=[R_cvt[ch]])
                        W = 16 + n
                        cur = 0
                        for step in range(4):
                            sh = 1 << step
                            nxt = 1 + (step % 2)
                            lo = 2 * sh
                            for ch in range(2):
                                if ch == 0 and step >= 2:
                                    continue
                                fw.op(dve, lambda: nc.vector.scalar_tensor_tensor(ubuf[:, nxt, ch, lo:W], ubuf[:, cur, ch, lo - sh:W - sh],
                                                                                  msk[:, ch, step:step + 1], ubuf[:, cur, ch, lo:W],
                                                                                  ALU.mult, ALU.add),
                                      reads=[R_ub[cur], R_c], writes=[R_ub[nxt]])
                            cur = nxt
                        for ch in range(2):
                            fin = 2
                            fw.op(dve, lambda: nc.vector.scalar_tensor_tensor(dbf[:, ch, 0:n], ubuf[:, fin, ch, 16:16 + n], invw[:, ch:ch + 1],
                                                                              ubuf[:, 0, ch, 16:16 + n], ALU.mult, ALU.subtract),
                                  reads=[R_ub[fin], R_ub[0], R_c], writes=[R_dbf[ch]])
                            if ti == 0:
                                fw.op(pool, lambda: nc.gpsimd.tensor_tensor(ubuf[:, 1, ch, 0:16], ubuf[:, fin, ch, 16:32], ppc(PP_IC + ch * 16, 16), ALU.mult),
                                      reads=[R_ub[fin], R_pp, R_ub[1]], writes=[R_ub[1]])
                                fw.op(pool, lambda: nc.gpsimd.tensor_tensor(dbf[:, ch, 0:16], ubuf[:, 1, ch, 0:16], ubuf[:, 0, ch, 16:32], ALU.subtract),
                                      reads=[R_ub[1], R_ub[0], R_dbf[ch]], writes=[R_dbf[ch]])

                    def attention_prompt(ti, steps):
                        t0, n, hs, is_s, nsub = tinfo(ti)
                        g = ti
                        iters = [(h, kp) for h in range(NH) for kp in range(g + 1)]

                        def emit_S(i):
                            h, kp = iters[i]
                            banks = ((0, 1, 6)[(2 * i) % 3], (0, 1, 6)[(2 * i + 1) % 3])
                            slot = i % 2
                            rds = [R_kT[2 * kp], R_kT[2 * kp + 1], R_qT[hs][0], R_qT[hs][1], R_c]
                            for c in range(2):
                                sbk = banks[c]
                                fw.deps(pe, rds, [R_pb[sbk]])
                                for e in range(2):
                                    kt = 2 * kp + e
                                    c0 = 128 if kt == 2 * g + 1 else 0
                                    diag = (kt >= 2 * g)
                                    oap = pb[sbk][:, e * 256 + c0:e * 256 + 256]
                                    ins = nc.tensor.matmul(oap, kT[c * 64:(c + 1) * 64, h, kt * 128:(kt + 1) * 128],
                                                           qT[c * 64:(c + 1) * 64, hs, h, c0:256], start=True, stop=not diag)
                                    if diag:
                                        ins = nc.tensor.matmul(oap, mrow[c * 64:c * 64 + 1, :], mcol[c * 64:c * 64 + 1, 0:256 - c0],
                                                               start=False, stop=True)
                                    if c0 > 0:
                                        ins = nc.tensor.matmul(pb[sbk][:, e * 256:e * 256 + c0], mrow[c * 64:c * 64 + 1, :],
                                                               mcol[c * 64:c * 64 + 1, 64:64 + c0], start=True, stop=True)
                                pe.count += 1
                                ins.then_inc(pe.sem, 1)
                                fw.commit((pe.sem, pe.count), rds, [R_pb[sbk]])
                            for c in range(2):
                                sbk = banks[c]
                                fw.op(act, lambda: nc.scalar.activation(PT[:, slot, :, c, :], pb[sbk][:].rearrange("p (e q) -> p e q", e=2),
                                                                        AF.Exp, scale=SCALE),
                                      reads=[R_pb[sbk]], writes=[R_PT[slot]])

                        def emit_PV(i):
                            h, kp = iters[i]
                            slot = i % 2
                            closed = []
                            for e in range(2):
                                kt = 2 * kp + e
                                for qs in range(2):
                                    if kt > 2 * g + qs:
                                        continue
                                    for c in range(2):
                                        ob = 2 + qs * 2 + c
                                        first = (kt == 0)
                                        last = (kt == 2 * g + qs)
                                        fw.deps(pe, [R_PT[slot], R_V[kt]], [R_pb[ob]] if first else [])
                                        ins = nc.tensor.matmul(pb[ob][:, 0:129], PT[:, slot, e, c, qs * 128:(qs + 1) * 128], Vx[:, kt, h, :],
                                                               start=first, stop=last)
                                        pe.count += 1
                                        ins.then_inc(pe.sem, 1)
                                        fw.commit((pe.sem, pe.count), [R_PT[slot], R_V[kt]], [R_pb[ob]])
                                    if kt == 2 * g + qs:
                                        k = ptc["n"] % 2
                                        ptc["n"] += 1
                                        closed.append((qs, k))
                                        for c in range(2):
                                            ob = 2 + qs * 2 + c
                                            fw.op(dve, lambda: nc.vector.tensor_copy(oev[:, k, c, :], pb[ob][:, 0:129]), reads=[R_pb[ob]], writes=[R_oev[k]])
                            for qs, k in closed:
                                epilogue(128, h, BV(oev[:, k, 0, :], R_oev[k]), BV(oev[:, k, 1, :], R_oev[k]), qs)

                        nsteps = 6
                        stride = max(1, len(iters) // (nsteps + 1))
                        misc["mode"] = "in"
                        emit_S(0)
                        for i in range(len(iters)):
                            if i + 1 < len(iters):
                                emit_S(i + 1)
                            emit_PV(i)
                            if steps is not None and i % stride == stride - 1:
                                next(steps, None)
                        if steps is not None:
                            for _ in steps:
                                pass
                        misc["mode"] = "out"
                        wb = mbank()
                        fw.deps(pe, [R_gu[0], R_c], [R_pb[wb]])
                        for _ in range(12):
                            ins = nc.tensor.matmul(pb[wb][:, :], onesb[:], wq[:, 0, :], start=True, stop=True)
                        pe.count += 1
                        ins.then_inc(pe.sem, 1)
                        fw.commit((pe.sem, pe.count), [R_gu[0], R_c], [R_pb[wb]])
                        flush_fin()
                        for qs in range(2):
                            transpose4(lambda h: yat[:, qs, h * 128:(h + 1) * 128], 128, mixT[:, 0:4, qs * 128:(qs + 1) * 128],
                                       R_yat[qs], R_mx[qs], mbank(), dve)
                        return [R_mx[0], R_mx[1]]

                    def attention_sample(ti):
                        t0, n, hs, is_s, nsub = tinfo(ti)
                        fw.op(pool, lambda: nc.gpsimd.memset(oacc, 0.0), writes=R_oacc)

                        def blk(b_):
                            cache = b_ < NCB
                            return cache, b_ % 2, (2 if cache else 1), (128 if cache else DEC)

                        def emit_T(b_):
                            cache, sl, nj, kp_ = blk(b_)
                            if not cache:
                                return
                            pbb = pb[5][:].bitcast(BF16)
                            fw.deps(pe, [R_kcs[sl], R_c], [R_pb[5]])
                            for j in range(2):
                                for h in range(NH):
                                    ins = nc.tensor.transpose(pbb[:, (h * 2 + j) * 128:(h * 2 + j + 1) * 128], kcs[:, sl, j, h * 128:(h + 1) * 128], idb[:])
                            pe.count += 1
                            ins.then_inc(pe.sem, 1)
                            fw.commit((pe.sem, pe.count), [R_kcs[sl], R_c], [R_pb[5]])
                            fw.op(dve, lambda: nc.vector.tensor_copy(kcT[:, sl], pbb[:].rearrange("p (h k) -> p h k", h=NH)),
                                  reads=[R_pb[5]], writes=[R_kcT[sl]])

                        def emit_S(b_):
                            cache, sl, nj, kp_ = blk(b_)
                            kres = [R_kcT[sl]] if cache else [R_kT[16]]
                            banks = (0, 1) if b_ % 2 == 0 else (6, 7)
                            for c in range(2):
                                sbk = banks[c]
                                Sv = pb[sbk][:, 0:256].rearrange("p (h j q) -> p h j q", h=4, j=2)
                                fw.deps(pe, kres + [R_qT[hs][0]], [R_pb[sbk]])
                                for h in range(NH):
                                    for j in range(nj):
                                        if cache:
                                            lk = kcT[c * 64:(c + 1) * 64, sl, h, j * 128:(j + 1) * 128]
                                        else:
                                            lk = kT[c * 64:(c + 1) * 64, h, SEQ:SEQ + DEC]
                                        ins = nc.tensor.matmul(Sv[0:kp_, h, j, :], lk, qT[c * 64:(c + 1) * 64, hs, h, 0:DEC],
                                                               start=True, stop=True)
                                pe.count += 1
                                ins.then_inc(pe.sem, 1)
                                fw.commit((pe.sem, pe.count), kres + [R_qT[hs][0]], [R_pb[sbk]])
                            for c in range(2):
                                sbk = banks[c]
                                Sv2 = pb[sbk][:, 0:256].rearrange("p (h j q) -> p h j q", h=4, j=2)
                                fw.op(act, lambda: nc.scalar.activation(PTs[0:kp_, sl, c, :, 0:nj, :], Sv2[0:kp_, :, 0:nj, :],
                                                                        AF.Exp, scale=SCALE),
                                      reads=[R_pb[sbk]], writes=[R_PTs[sl]])

                        def emit_PV(b_):
                            cache, sl, nj, kp_ = blk(b_)
                            vres = [R_vcs[sl]] if cache else [R_V[16]]
                            for bi, grp in enumerate([(0, 1, 2), (3, 4, 5), (6, 7)]):
                                ob = 2 + bi
                                rds = vres + [R_PTs[sl], R_c]
                                fw.deps(pe, rds, [R_pb[ob]])
                                for gi_, hc_ in enumerate(grp):
                                    h, c = hc_ // 2, hc_ % 2
                                    if cache:
                                        for j in range(nj):
                                            ins = nc.tensor.matmul(pb[ob][0:DEC, gi_ * 129:gi_ * 129 + 128], PTs[0:kp_, sl, c, h, j, :],
                                                                   vcs[0:kp_, sl, j, h * 128:(h + 1) * 128], start=(j == 0), stop=(j == nj - 1))
                                        for j in range(nj):
                                            ins = nc.tensor.matmul(pb[ob][0:DEC, gi_ * 129 + 128:gi_ * 129 + 129], PTs[0:kp_, sl, c, h, j, :],
                                                                   onesb[0:kp_, 0:1], start=(j == 0), stop=(j == nj - 1))
                                    else:
                                        ins = nc.tensor.matmul(pb[ob][0:DEC, gi_ * 129:(gi_ + 1) * 129], PTs[0:kp_, sl, c, h, 0, :],
                                                               Vx[0:kp_, 16, h, :], start=True, stop=True)
                                pe.count += 1
                                ins.then_inc(pe.sem, 1)
                                fw.commit((pe.sem, pe.count), rds, [R_pb[ob]])
                                ng = len(grp)
                                oav = oacc[:, grp[0]:grp[0] + ng, :].rearrange("p g e -> p (g e)")
                                fw.op(dve, lambda: nc.vector.tensor_tensor(oav, oav, pb[ob][0:DEC, 0:ng * 129], ALU.add),
                                      reads=[R_pb[ob]] + R_oacc, writes=R_oacc)

                        NB = NCB + 1
                        emit_T(0)
                        emit_S(0)
                        for b_ in range(NB):
                            if b_ + 1 < NB:
                                emit_T(b_ + 1)
                                if b_ + 2 < NCB:
                                    load_ck(b_ + 2)
                                emit_S(b_ + 1)
                            emit_PV(b_)
                            if b_ + 2 < NCB:
                                load_cv(b_ + 2)
                        for h in range(NH):
                            epilogue(DEC, h, BV(oacc[:, 2 * h, :], R_oacc), BV(oacc[:, 2 * h + 1, :], R_oacc), 0)
                        flush_fin()
                        transpose4(lambda h: yat[0:DEC, 0, h * 128:(h + 1) * 128], DEC, mixT[:, 0:4, 0:DEC],
                                   R_yat[0], R_mx[0], mbank(), dve)
                        return [R_mx[0]]

                    def tail(ti, attn_res):
                        t0, n, hs, is_s, nsub = tinfo(ti)
                        for ch in range(2):
                            b = mbank()
                            fproj(ti, wbc, ch, b)
                            fw.op(dve, lambda: nc.vector.tensor_tensor(mixT[:, 6 + ch, 0:n], pb[b][:, 0:n], cvt[:, ch, 0:n], ALU.mult),
                                  reads=[R_pb[b], R_cvt[ch]], writes=[R_mx[6 + ch]])
                        for ch in range(2):
                            b = mbank()
                            fw.op(pe, lambda: nc.tensor.matmul(pb[b][:, 0:n], pwb[:, l, ch, :], dbf[:, ch, 0:n], start=True, stop=True),
                                  reads=[R_pw, R_dbf[ch]], writes=[R_pb[b]])
                            fw.op(dve, lambda: nc.vector.tensor_scalar(mixT[:, 4 + ch, 0:n], pb[b][:, 0:n], ppc(PP_PS + l * 2 + ch), None, ALU.mult),
                                  reads=[R_pb[b], R_pp], writes=[R_mx[4 + ch]])
                        for dco in range(8):
                            b = mbank()
                            mm_group(pb[b][:, 0:n], [(wo[:, mc, dco * 128:(dco + 1) * 128], mixT[:, mc, 0:n]) for mc in range(8)],
                                     R_wo + attn_res + R_mx[4:8], [R_pb[b]])
                            xr = RX(t0, n, [dco])
                            fw.op(dve, lambda: nc.vector.tensor_tensor(xT[:, dco, t0:t0 + n], pb[b][:, 0:n], xT[:, dco, t0:t0 + n], ALU.add),
                                  reads=[R_pb[b]] + xr, writes=xr)

                    for _ in qchain(0):
                        pass
                    for ti in range(len(M2T)):
                        early(ti)
                        steps = qchain(ti + 1) if ti + 1 < len(M2T) else None
                        if ti < 8:
                            ares = attention_prompt(ti, steps)
                        else:
                            ares = attention_sample(ti)
                        tail(ti, ares)
                    fw.barrier()

        stage(0)
        for l in range(DEPTH):
            if stopped["v"]:
                break
            ffn(l, 0, l * 3 + 0, 2 if l == 0 else 1, prefetch=lambda: mixer_preload(l), xload=(l == 0))
            if stage(1 + 10 * l):
                break
            mixer(l)
            if stage(5 + 10 * l):
                break
            ffn(l, 1, l * 3 + 2, 0, prefetch=(lambda: ffn_load(l + 1, 0, 0)) if l + 1 < DEPTH else None)
            if stage(6 + 10 * l):
                break
        if stopped["v"]:
            fw.finish()
            return nc

        with ExitStack() as ph:
            nb = norm_bufs(ph)
            yf = sb(ph, "yf", (128, 2, 8, 512), F32)
            R_yf = [Res("yf0"), Res("yf1")]
            yst = sb(ph, "yst", (128, 2, D), F32)
            R_yst = [Res("yst0"), Res("yst1")]
            FTL = [(0, 512), (512, 512), (1024, 512), (1536, 512), (2048, 32)]

            def fin_out(ti, sub):
                t0, nt = FTL[ti]
                ks = ti % 2
                s = t0 // 128 + sub
                n = min(128, nt - sub * 128)
                k = s % 2
                for hb in range(2):
                    bank = (2 * s + hb) % 4
                    fw.deps(pe, [R_yf[ks], R_c], [R_pb[bank]])
                    for q in range(4):
                        dc = hb * 4 + q
                        ins = nc.tensor.transpose(pb[bank][0:n, q * 128:(q + 1) * 128], yf[:, ks, dc, sub * 128:sub * 128 + n], idf[:])
                    pe.count += 1
                    ins.then_inc(pe.sem, 1)
                    fw.commit((pe.sem, pe.count), [R_yf[ks], R_c], [R_pb[bank]])
                    if hb == 0:
                        fw.op(act, lambda: nc.scalar.copy(yst[0:n, k, 0:512], pb[bank][0:n, :]), reads=[R_pb[bank]], writes=[R_yst[k]])
                    else:
                        fw.op(dve, lambda: nc.vector.tensor_copy(yst[0:n, k, 512:1024], pb[bank][0:n, :]), reads=[R_pb[bank]], writes=[R_yst[k]])
                dst = yp_d[s * 128:(s + 1) * 128, :] if s < 16 else ys_d
                fw.dma(sp, dst, yst[0:n, k, :], R_yst[k], reads=[R_yst[k]])

            def fin_rms(ti):
                t0, nt = FTL[ti]
                ks = ti % 2
                rms_tile(nb, 6, t0, nt, lambda dc: yf[:, ks, dc, 0:nt], R_yf[ks], 6 + ti % 2, sq_eng="mix")

            fin_rms(0)
            for ti, (t0, nt) in enumerate(FTL):
                if ti + 1 < len(FTL):
                    fin_rms(ti + 1)
                for sub in range((nt + 127) // 128):
                    fin_out(ti, sub)
            fw.finish()
    return nc


def _param_pack(p):
    pp = np.zeros((128, NPP), np.float32)
    gains = [p["norm_ffn1"][0], p["norm_mix"][0], p["norm_ffn2"][0],
             p["norm_ffn1"][1], p["norm_mix"][1], p["norm_ffn2"][1], p["final_norm"]]
    for gi, g in enumerate(gains):
        pp[:, PP_G + gi * 8:PP_G + gi * 8 + 8] = np.asarray(g, np.float32).reshape(8, 128).T
    for l in range(DEPTH):
        pp[:, PP_PS + l * 2:PP_PS + l * 2 + 2] = np.asarray(p["pool_scale"][l], np.float32).reshape(2, 128).T
        cw = np.asarray(p["conv_w"][l], np.float32)
        for ch in range(2):
            pp[:, PP_CW + l * 6 + ch * 3:PP_CW + l * 6 + ch * 3 + 3] = cw[:, ch * 128:(ch + 1) * 128].T
        pp[:, PP_SUB + l * 128:PP_SUB + (l + 1) * 128] = np.asarray(p["subln"][l], np.float32)[None, :]
        for j, nm in enumerate(["lambda_q1", "lambda_k1", "lambda_q2", "lambda_k2"]):
            pp[:, PP_LAM + l * 256 + j * 64:PP_LAM + l * 256 + (j + 1) * 64] = np.asarray(p[nm][l], np.float32)[None, :]
    half = 32
    inv = np.power(np.float32(10000.0), -np.arange(half, dtype=np.float32) / np.float32(half)).astype(np.float32)
    pos = np.concatenate([np.arange(SEQ), PAST + np.arange(DEC)]).astype(np.float32)
    ang = (pos[:, None] * inv[None, :]).astype(np.float32)
    cos = np.cos(ang).astype(np.float32)
    sin = np.sin(ang).astype(np.float32)
    for s in range(NSUB):
        n = 128 if s < 16 else DEC
        pp[0:n, PP_COS + s * 32:PP_COS + (s + 1) * 32] = cos[s * 128:s * 128 + n]
        pp[0:n, PP_SIN + s * 32:PP_SIN + (s + 1) * 32] = sin[s * 128:s * 128 + n]
    wins = [2, 4, 8, 16]
    for ch in range(2):
        for half_i in range(2):
            w = wins[ch * 2 + half_i]
            for t in range(16):
                pp[half_i * 64:(half_i + 1) * 64, PP_IC + ch * 16 + t] = 1.0 / min(t + 1, w)
    return pp


_NC_CACHE = {}


def kernel(**inputs):
    p = {k: np.asarray(v) for k, v in inputs.items()}
    if "nc" not in _NC_CACHE:
        _NC_CACHE["nc"] = build_program()
    nc = _NC_CACHE["nc"]
    pp = _param_pack(p)
    pw = np.asarray(p["pool_w"], np.float32)
    pwbd = np.zeros((DEPTH, 2, 128, 128), np.float32)
    for l in range(DEPTH):
        for g in range(4):
            ch, o = g // 2, (g % 2) * 64
            pwbd[l, ch, o:o + 64, o:o + 64] = pw[l, g]
    shared = {
        "pp": pp, "pwbd": pwbd,
        "f1g": np.ascontiguousarray(p["ffn1_gate"], np.float32), "f1u": np.ascontiguousarray(p["ffn1_up"], np.float32),
        "f1d": np.ascontiguousarray(p["ffn1_down"], np.float32),
        "f2g": np.ascontiguousarray(p["ffn2_gate"], np.float32), "f2u": np.ascontiguousarray(p["ffn2_up"], np.float32),
        "f2d": np.ascontiguousarray(p["ffn2_down"], np.float32),
        "win": np.ascontiguousarray(p["w_in"], np.float32), "wout": np.ascontiguousarray(p["w_out"], np.float32),
    }
    in_maps = []
    for b in range(8):
        m = dict(shared)
        m["xp"] = np.ascontiguousarray(p["x_prompt"][b], np.float32)
        m["xs"] = np.ascontiguousarray(p["x_sample"][b], np.float32)
        m["ck"] = np.ascontiguousarray(p["cache_k"][:, b].reshape(DEPTH, PAST, 512), np.float32)
        m["cv"] = np.ascontiguousarray(p["cache_v"][:, b].reshape(DEPTH, PAST, 512), np.float32)
        m["stp"] = np.ascontiguousarray(p["state_pool"][:, b], np.float32)
        m["stc"] = np.ascontiguousarray(p["state_conv"][:, b], np.float32)
        in_maps.append(m)
    ncores = int(os.environ.get("KCORES", "8"))
    res = run_bass_kernel_spmd(nc, in_maps[:ncores], core_ids=list(range(ncores)))
    R = list(res.results)
    while len(R) < 8:
        R.append(R[0])

    def gather(name, shape_tail, axis_b):
        arrs = [np.asarray(R[b][name], np.float32) for b in range(8)]
        return np.stack(arrs, axis=axis_b)

    y_prompt = gather("yp", None, 0)
    y_sample = gather("ys", None, 0)
    k_prompt = gather("kp", None, 1).reshape(DEPTH, 8, SEQ, NH, 128)
    v_prompt = gather("vp", None, 1).reshape(DEPTH, 8, SEQ, NH, 128)
    pool_prompt = gather("plp", None, 1)
    conv_prompt = gather("cvp", None, 1)
    k_sample = gather("ks", None, 1).reshape(DEPTH, 8, DEC, NH, 128)
    v_sample = gather("vs", None, 1).reshape(DEPTH, 8, DEC, NH, 128)
    pool_sample = gather("pls", None, 1)
    conv_sample = gather("cvs", None, 1)
    return (y_prompt, y_sample, k_prompt, v_prompt, pool_prompt, conv_prompt,
            k_sample, v_sample, pool_sample, conv_sample)
```

```python
import math
import os
from contextlib import ExitStack

import numpy as np
import concourse.bass as bass
import concourse.mybir as mybir
from concourse.bass_utils import run_bass_kernel_spmd

F32 = mybir.dt.float32
BF16 = mybir.dt.bfloat16
ALU = mybir.AluOpType
AF = mybir.ActivationFunctionType
AX = mybir.AxisListType

D = 1024
SEQ = 2048
DEC = 32
NT = SEQ + DEC
PAST = 4096
DEPTH = 2
DFF = 2816
DIN = 2560
NH = 4
EPS = 1e-6
SCALE = 64 ** -0.5
NSUB = 17

PP_G = 0
PP_PS = PP_G + 56
PP_CW = PP_PS + 4
PP_SUB = PP_CW + 12
PP_LAM = PP_SUB + 256
PP_COS = PP_LAM + 512
PP_SIN = PP_COS + 544
PP_IC = PP_SIN + 544
NPP = PP_IC + 32


class Res:
    __slots__ = ("name", "writer", "readers", "dsem", "dcount", "nobar")

    def __init__(self, name, nobar=False):
        self.name = name
        self.writer = None
        self.readers = []
        self.dsem = None
        self.dcount = 0
        self.nobar = nobar


class Eng:
    def __init__(self, fw, raw, name, selfwait=True):
        self.fw = fw
        self.raw = raw
        self.name = name
        self.sem = fw.new_sem("e_" + name)
        self.count = 0
        self.seen = {}
        self.selfwait = selfwait

    def wait(self, tok):
        if tok is None:
            return
        sem, val = tok
        if sem is self.sem and not self.selfwait:
            return
        k = id(sem)
        if self.seen.get(k, 0) >= val:
            return
        self.raw.wait_ge(sem, val)
        self.seen[k] = val


class FW:
    def __init__(self, nc, es):
        self.nc = nc
        self.es = es
        self.pe = Eng(self, nc.tensor, "pe", selfwait=False)
        self.act = Eng(self, nc.scalar, "act")
        self.dve = Eng(self, nc.vector, "dve")
        self.pool = Eng(self, nc.gpsimd, "pool")
        self.sp = Eng(self, nc.sync, "sp", selfwait=False)
        self.engs = [self.pe, self.act, self.dve, self.pool, self.sp]
        self.dma_res = []

    def new_sem(self, name):
        self.nsem = getattr(self, "nsem", 0) + 1
        return self.es.enter_context(self.nc.semaphore(f"{name}_{self.nsem}"))

    def deps(self, eng, reads, writes):
        best = {}

        def add(tok):
            if tok is None:
                return
            k = id(tok[0])
            if k not in best or best[k][1] < tok[1]:
                best[k] = tok
        for r in reads:
            add(r.writer)
        for w in writes:
            add(w.writer)
            for t in w.readers:
                add(t)
        for tok in best.values():
            eng.wait(tok)

    def commit(self, tok, reads, writes):
        for r in reads:
            r.readers.append(tok)
            if len(r.readers) > 16:
                best = {}
                for s, v in r.readers:
                    if id(s) not in best or best[id(s)][1] < v:
                        best[id(s)] = (s, v)
                r.readers = list(best.values())
        for w in writes:
            w.writer = tok
            w.readers = []

    def op(self, eng, fn, reads=(), writes=(), signal=True):
        self.deps(eng, reads, writes)
        ins = fn()
        if signal:
            eng.count += 1
            ins.then_inc(eng.sem, 1)
            tok = (eng.sem, eng.count)
            self.commit(tok, reads, writes)
            return tok
        return None

    def dma(self, eng, out, in_, slot, reads=(), writes=(), nowait=False, **kw):
        if slot.dsem is None:
            slot.dsem = self.new_sem("d_" + slot.name)
            self.dma_res.append(slot)
        if not nowait:
            self.deps(eng, reads, writes)
        ins = eng.raw.dma_start(out=out, in_=in_, **kw)
        slot.dcount += 16
        ins.then_inc(slot.dsem, 16)
        tok = (slot.dsem, slot.dcount)
        self.commit(tok, reads, writes)
        return tok

    def barrier(self):
        toks = [(e.sem, e.count) for e in self.engs if e.count > 0]
        toks += [(r.dsem, r.dcount) for r in self.dma_res if r.dcount > 0 and not r.nobar]
        for e in self.engs:
            for t in toks:
                if t[0] is e.sem and not e.selfwait:
                    continue
                e.wait(t)

    def finish(self):
        toks = [(e.sem, e.count) for e in self.engs if e.count > 0]
        toks += [(r.dsem, r.dcount) for r in self.dma_res if r.dcount > 0]
        for t in toks:
            self.sp.wait(t)


class _Stop(Exception):
    pass


def build_program():
    nc = bass.Bass("TRN2", target_bir_lowering=False)
    STAGE = float(os.environ.get("KSTAGE", "99"))

    stopped = {"v": False}

    def stage(x):
        if STAGE <= x:
            stopped["v"] = True
        return stopped["v"]

    def din(name, shape):
        return nc.dram_tensor(name, list(shape), F32, kind="ExternalInput").ap()

    def dout(name, shape):
        return nc.dram_tensor(name, list(shape), F32, kind="ExternalOutput").ap()

    xp_d = din("xp", (SEQ, D))
    xs_d = din("xs", (DEC, D))
    ck_d = din("ck", (DEPTH, PAST, 512))
    cv_d = din("cv", (DEPTH, PAST, 512))
    stp_d = din("stp", (DEPTH, 15, 256))
    stc_d = din("stc", (DEPTH, 2, 256))
    pp_d = din("pp", (128, NPP))
    pw_d = din("pwbd", (DEPTH, 2, 128, 128))
    wg_d = [din("f1g", (DEPTH, D, DFF)), din("f2g", (DEPTH, D, DFF))]
    wu_d = [din("f1u", (DEPTH, D, DFF)), din("f2u", (DEPTH, D, DFF))]
    wd_d = [din("f1d", (DEPTH, DFF, D)), din("f2d", (DEPTH, DFF, D))]
    win_d = din("win", (DEPTH, D, DIN))
    wout_d = din("wout", (DEPTH, D, D))

    yp_d = dout("yp", (SEQ, D))
    ys_d = dout("ys", (DEC, D))
    kp_d = dout("kp", (DEPTH, SEQ, 512))
    vp_d = dout("vp", (DEPTH, SEQ, 512))
    plp_d = dout("plp", (DEPTH, 15, 256))
    cvp_d = dout("cvp", (DEPTH, 2, 256))
    ks_d = dout("ks", (DEPTH, DEC, 512))
    vs_d = dout("vs", (DEPTH, DEC, 512))
    pls_d = dout("pls", (DEPTH, 15, 256))
    cvs_d = dout("cvs", (DEPTH, 2, 256))

    with ExitStack() as es:
        fw = FW(nc, es)
        pe, act, dve, pool, sp = fw.pe, fw.act, fw.dve, fw.pool, fw.sp

        uid = {"n": 0}

        def sb(st, name, shape, dt):
            uid["n"] += 1
            return st.enter_context(nc.sbuf_tensor(f"s{uid['n']}_{name}", list(shape), dt))

        xT = sb(es, "xT", (128, 8, NT), F32)
        R_x = [[Res(f"x{dc}_{tl}") for tl in range(NSUB)] for dc in range(8)]

        def RX(t0, n, dcs=range(8)):
            s0, s1 = t0 // 128, (t0 + n - 1) // 128
            return [R_x[dc][s] for dc in dcs for s in range(s0, s1 + 1)]

        pp = sb(es, "pp", (128, NPP), F32)
        R_pp = Res("pp")
        pwb = sb(es, "pwb", (128, DEPTH, 2, 128), BF16)
        R_pw = Res("pwb")
        idf = sb(es, "idf", (128, 128), F32)
        idb = sb(es, "idb", (128, 128), BF16)
        onesb = sb(es, "onesb", (128, 128), BF16)
        R_c = Res("consts")
        lamt = sb(es, "lamt", (128, DEPTH, 4), F32)
        subg = sb(es, "subg", (128, DEPTH, 128), F32)
        msk = sb(es, "msk", (128, 2, 4), F32)
        invw = sb(es, "invw", (128, 2), F32)
        cneg = sb(es, "cneg", (128, 1), F32)
        mrow = sb(es, "mrow", (128, 128), BF16)
        mcol = sb(es, "mcol", (128, 256), BF16)
        gu = [sb(es, f"gu{i}", (128, 2, 8, 512), BF16) for i in range(2)]
        dn = sb(es, "dn", (128, 2, 4, 1024), BF16)
        R_gu = [Res(f"gu{i}", nobar=True) for i in range(2)]
        R_dn = [Res(f"dn{i}", nobar=True) for i in range(2)]
        pb = [es.enter_context(nc.psum_tensor(f"pb{i}", [128, 512], F32)) for i in range(8)]
        R_pb = [Res(f"pb{i}") for i in range(8)]

        def ppc(c0, n=1):
            return pp[:, c0:c0 + n]

        def mm_group(bank_ap, pairs, reads, writes):
            fw.deps(pe, reads, writes)
            n = len(pairs)
            for i, (l, r) in enumerate(pairs):
                ins = nc.tensor.matmul(bank_ap, l, r, start=(i == 0), stop=(i == n - 1))
            pe.count += 1
            ins.then_inc(pe.sem, 1)
            tok = (pe.sem, pe.count)
            fw.commit(tok, reads, writes)
            return tok

        rr = {"n": 0}

        fw.dma(sp, pp[:], pp_d, R_pp, writes=[R_pp])
        fw.dma(pool, pwb[:], pw_d.rearrange("l c p e -> p l c e"), R_pw, writes=[R_pw])
        fw.op(pool, lambda: nc.gpsimd.memset(idf[:], 1.0), writes=[R_c])
        fw.op(pool, lambda: nc.gpsimd.affine_select(idf[:], idf[:], pattern=[[-1, 128]], compare_op=ALU.is_equal,
                                                   fill=0.0, base=0, channel_multiplier=1), reads=[R_c], writes=[R_c])
        fw.op(dve, lambda: nc.vector.tensor_copy(idb[:], idf[:]), reads=[R_c], writes=[R_c])
        fw.op(dve, lambda: nc.vector.memset(onesb[:], 1.0), writes=[R_c])
        fw.op(dve, lambda: nc.vector.memset(msk[:], 0.0), writes=[R_c])
        fw.op(dve, lambda: nc.vector.memset(msk[:, :, 0:1], 1.0), reads=[R_c], writes=[R_c])
        fw.op(dve, lambda: nc.vector.memset(msk[64:128, 0, 1:2], 1.0), reads=[R_c], writes=[R_c])
        fw.op(dve, lambda: nc.vector.memset(msk[:, 1, 1:3], 1.0), reads=[R_c], writes=[R_c])
        fw.op(dve, lambda: nc.vector.memset(msk[64:128, 1, 3:4], 1.0), reads=[R_c], writes=[R_c])
        fw.op(dve, lambda: nc.vector.memset(cneg[:], -0.5), reads=[R_c], writes=[R_c])
        fw.op(dve, lambda: nc.vector.memset(mrow[:], 0.0), reads=[R_c], writes=[R_c])
        fw.op(dve, lambda: nc.vector.memset(mrow[:, 64:128], 1.0), reads=[R_c], writes=[R_c])
        fw.op(dve, lambda: nc.vector.memset(mcol[:], 0.0), reads=[R_c], writes=[R_c])
        fw.op(dve, lambda: nc.vector.memset(mcol[:, 0:64], -30000.0), reads=[R_c], writes=[R_c])
        fw.op(dve, lambda: nc.vector.memset(invw[0:64, 0:1], 0.5), reads=[R_c], writes=[R_c])
        fw.op(dve, lambda: nc.vector.memset(invw[64:128, 0:1], 0.25), reads=[R_c], writes=[R_c])
        fw.op(dve, lambda: nc.vector.memset(invw[0:64, 1:2], 0.125), reads=[R_c], writes=[R_c])
        fw.op(dve, lambda: nc.vector.memset(invw[64:128, 1:2], 0.0625), reads=[R_c], writes=[R_c])
        with ExitStack() as ph:
            ltmp = sb(ph, "ltmp", (128, 64), F32)
            R_lt = Res("ltmp")
            for l in range(DEPTH):
                lam_init = 0.8 - 0.6 * math.exp(-0.3 * l)
                for j in range(2):
                    c0 = PP_LAM + l * 256 + j * 128
                    fw.op(dve, lambda: nc.vector.tensor_tensor(ltmp[:], ppc(c0, 64), ppc(c0 + 64, 64), ALU.mult),
                          reads=[R_pp, R_lt], writes=[R_lt])
                    fw.op(dve, lambda: nc.vector.reduce_sum(lamt[:, l, 2 + j:3 + j], ltmp[:], axis=AX.X),
                          reads=[R_lt, R_c], writes=[R_c])
                fw.op(act, lambda: nc.scalar.activation(lamt[:, l, 2:4], lamt[:, l, 2:4], AF.Exp), reads=[R_c], writes=[R_c])
                fw.op(dve, lambda: nc.vector.tensor_tensor(lamt[:, l, 0:1], lamt[:, l, 2:3], lamt[:, l, 3:4], ALU.subtract),
                      reads=[R_c], writes=[R_c])
                fw.op(dve, lambda: nc.vector.tensor_scalar(lamt[:, l, 0:1], lamt[:, l, 0:1], lam_init, None, ALU.add),
                      reads=[R_c], writes=[R_c])
                fw.op(dve, lambda: nc.vector.tensor_scalar(lamt[:, l, 1:2], lamt[:, l, 0:1], -1.0, None, ALU.mult),
                      reads=[R_c], writes=[R_c])
                fw.op(dve, lambda: nc.vector.tensor_scalar(subg[:, l, :], ppc(PP_SUB + l * 128, 128), 1.0 - lam_init, None, ALU.mult),
                      reads=[R_pp, R_c], writes=[R_c])
            fw.barrier()

        def rms_tile(st, gi, t0, n, dst_fn, dst_res, bank, sq_eng="act"):
            sq, R_sq, rstd, R_rs = st["sq"], st["R_sq"], st["rstd"], st["R_rs"]
            xres = RX(t0, n)
            for dc in range(8):
                i = rr["n"] % 4
                rr["n"] += 1
                if sq_eng == "pool" or (sq_eng == "mix" and dc % 2 == 1):
                    fw.op(pool, lambda: nc.gpsimd.tensor_tensor(sq[:, i, 0:n], xT[:, dc, t0:t0 + n], xT[:, dc, t0:t0 + n], ALU.mult),
                          reads=RX(t0, n, [dc]), writes=[R_sq[i]])
                else:
                    fw.op(act, lambda: nc.scalar.activation(sq[:, i, 0:n], xT[:, dc, t0:t0 + n], AF.Square),
                          reads=RX(t0, n, [dc]), writes=[R_sq[i]])
                fw.op(pe, lambda: nc.tensor.matmul(pb[bank][:, 0:n], onesb[:], sq[:, i, 0:n], start=(dc == 0), stop=(dc == 7)),
                      reads=[R_sq[i], R_c], writes=[R_pb[bank]])
            j = st["rsi"] % st["nrs"]
            st["rsi"] += 1
            fw.op(act, lambda: nc.scalar.activation(rstd[:, j, 0:n], pb[bank][:, 0:n], AF.Ln, bias=st["epsb"][:, 0:1], scale=1.0 / D),
                  reads=[R_pb[bank]], writes=[R_rs[j]])
            fw.op(act, lambda: nc.scalar.activation(rstd[:, j, 0:n], rstd[:, j, 0:n], AF.Exp, scale=-0.5),
                  reads=[R_rs[j]], writes=[R_rs[j]])
            for dc in range(8):
                fw.op(dve, lambda: nc.vector.scalar_tensor_tensor(dst_fn(dc), xT[:, dc, t0:t0 + n], ppc(PP_G + gi * 8 + dc),
                                                                  rstd[:, j, 0:n], ALU.mult, ALU.mult),
                      reads=[R_rs[j], R_pp] + RX(t0, n, [dc]), writes=[dst_res])

        def norm_bufs(st_, nmax=512, nrs=2):
            d = {}
            d["sq"] = sb(st_, "sq", (128, 4, nmax), BF16)
            d["R_sq"] = [Res(f"sq{i}") for i in range(4)]
            d["rstd"] = sb(st_, "rstd", (128, nrs, nmax), F32)
            d["R_rs"] = [Res(f"rs{i}") for i in range(nrs)]
            d["nrs"] = nrs
            d["epsb"] = sb(st_, "epsb", (128, 1), F32)
            d["rsi"] = 0
            fw.op(dve, lambda: nc.vector.memset(d["epsb"][:], EPS), writes=[d["R_rs"][0]])
            return d

        TILES = [(0, 512), (512, 512), (1024, 512), (1536, 512), (2048, 32)]

        def load_cols(dst_ap, w_ap, l, c0, ncol, slot_res, nowait=False):
            src = w_ap[l, :, c0:c0 + ncol].rearrange("(dc p) c -> p dc c", p=128)
            fw.dma(pool, dst_ap, src, slot_res, writes=[slot_res], nowait=nowait)

        def load_rows(dst_ap, w_ap, l, r0, nr, slot_res):
            src = w_ap[l, r0:r0 + nr, :].rearrange("(fc p) n -> p fc n", p=128)
            fw.dma(pool, dst_ap, src, slot_res, writes=[slot_res])

        PARTS = [(0, 4), (512, 4), (1024, 4), (1536, 4), (2048, 4), (2560, 2)]

        def ffn_load(l, w, p):
            c0, nf = PARTS[p]
            s = p % 2
            load_cols(gu[s][:, 0, :, 0:nf * 128], wg_d[w], l, c0, nf * 128, R_gu[s])
            load_cols(gu[s][:, 1, :, 0:nf * 128], wu_d[w], l, c0, nf * 128, R_gu[s], nowait=True)
            load_rows(dn[:, s, 0:nf, :], wd_d[w], l, c0, nf * 128, R_dn[s])

        ffn_load(0, 0, 0)
        ffn_load(0, 0, 1)

        def xload_gen(st_):
            xst = sb(st_, "xst", (128, 2, D), F32)
            R_xst = [Res("xst0"), Res("xst1")]
            for s in range(NSUB):
                n = 128 if s < 16 else DEC
                src = xp_d[s * 128:(s + 1) * 128, :] if s < 16 else xs_d
                k = s % 2
                fw.dma(sp, xst[0:n, k, :], src, R_xst[k], writes=[R_xst[k]])
                for hb in range(2):
                    bank = (7, 5)[hb]
                    fw.deps(pe, [R_xst[k], R_c], [R_pb[bank]])
                    for q in range(4):
                        dc = hb * 4 + q
                        ins = nc.tensor.transpose(pb[bank][:, q * 128:q * 128 + n], xst[0:n, k, dc * 128:(dc + 1) * 128], idf[0:n, 0:n])
                    pe.count += 1
                    ins.then_inc(pe.sem, 1)
                    fw.commit((pe.sem, pe.count), [R_xst[k], R_c], [R_pb[bank]])
                    src_ap = pb[bank][:].rearrange("p (q t) -> p q t", q=4)[:, :, 0:n]
                    dst_ap = xT[:, hb * 4:hb * 4 + 4, s * 128:s * 128 + n]
                    if hb == 0:
                        fw.op(act, lambda: nc.scalar.copy(dst_ap, src_ap), reads=[R_pb[bank]],
                              writes=[R_x[dc][s] for dc in range(hb * 4, hb * 4 + 4)])
                    else:
                        fw.op(dve, lambda: nc.vector.tensor_copy(dst_ap, src_ap), reads=[R_pb[bank]],
                              writes=[R_x[dc][s] for dc in range(hb * 4, hb * 4 + 4)])
                yield s

        FT = [(i * 416, 416) for i in range(5)]

        def ffn(l, w, gi, preloaded, prefetch=None, xload=False):
            TILES = FT
            with ExitStack() as ph:
                hT = sb(ph, "hT", (128, 8, NT), BF16)
                R_h = [Res(f"h{t}") for t in range(5)]
                aT = sb(ph, "aT", (128, 4, NT), BF16)
                R_a = [[Res(f"a{fc}_{t}") for t in range(5)] for fc in range(4)]
                sg = sb(ph, "sg", (128, 2, 512), F32)
                R_sg = [Res("sg0"), Res("sg1")]
                nb = norm_bufs(ph)
                if preloaded < 1:
                    ffn_load(l, w, 0)
                if preloaded < 2:
                    ffn_load(l, w, 1)

                def ffn_rms(ti_):
                    t0_, n_ = TILES[ti_]
                    rms_tile(nb, gi, t0_, n_, lambda dc: hT[:, dc, t0_:t0_ + n_], R_h[ti_], 6, sq_eng="mix")

                xg = xload_gen(ph) if xload else None

                def need_tokens(upto):
                    if xg is not None:
                        while xstate["s"] < min(NSUB - 1, (upto - 1) // 128):
                            xstate["s"] = next(xg)

                xstate = {"s": -1}
                need_tokens(TILES[0][0] + TILES[0][1])
                ffn_rms(0)
                cnt = 0
                for p, (c0, nf) in enumerate(PARTS):
                    s = p % 2
                    for ti, (t0, n) in enumerate(TILES):
                        if p == 0 and ti + 1 < len(TILES):
                            need_tokens(TILES[ti + 1][0] + TILES[ti + 1][1])
                            ffn_rms(ti + 1)
                        for fc in range(nf):
                            gb = cnt % 2
                            ub = 2 + cnt % 2
                            cnt += 1
                            mm_group(pb[gb][:, 0:n], [(gu[s][:, 0, dc, fc * 128:(fc + 1) * 128], hT[:, dc, t0:t0 + n]) for dc in range(8)],
                                     [R_gu[s], R_h[ti]], [R_pb[gb]])
                            mm_group(pb[ub][:, 0:n], [(gu[s][:, 1, dc, fc * 128:(fc + 1) * 128], hT[:, dc, t0:t0 + n]) for dc in range(8)],
                                     [R_gu[s], R_h[ti]], [R_pb[ub]])
                            k = gb
                            fw.op(act, lambda: nc.scalar.activation(sg[:, k, 0:n], pb[gb][:, 0:n], AF.Silu),
                                  reads=[R_pb[gb]], writes=[R_sg[k]])
                            fw.op(dve, lambda: nc.vector.tensor_tensor(aT[:, fc, t0:t0 + n], sg[:, k, 0:n], pb[ub][:, 0:n], ALU.mult),
                                  reads=[R_sg[k], R_pb[ub]], writes=[R_a[fc][ti]])
                    dcnt = 0
                    for ti, (t0, n) in enumerate(TILES):
                        for dco in range(8):
                            db = 4 + dcnt % 2
                            dcnt += 1
                            mm_group(pb[db][:, 0:n], [(dn[:, s, fc, dco * 128:(dco + 1) * 128], aT[:, fc, t0:t0 + n]) for fc in range(nf)],
                                     [R_dn[s]] + [R_a[fc][ti] for fc in range(nf)], [R_pb[db]])
                            xr = RX(t0, n, [dco])
                            fw.op(dve, lambda: nc.vector.scalar_tensor_tensor(xT[:, dco, t0:t0 + n], pb[db][:, 0:n], 0.5,
                                                                              xT[:, dco, t0:t0 + n], ALU.mult, ALU.add),
                                  reads=[R_pb[db]] + xr, writes=xr)
                    if p + 2 < len(PARTS):
                        ffn_load(l, w, p + 2)
                    elif p == len(PARTS) - 2 and prefetch is not None:
                        prefetch()
                fw.barrier()

        def rope(src_bank, n, s, dst_ap, dst_res, tmp, R_tmp):
            sv = pb[src_bank][0:n, :].rearrange("p (g h d) -> p g h d", g=8, h=2)
            dv = dst_ap.rearrange("p (g h d) -> p g h d", g=8, h=2)
            cosb = pp[0:n, PP_COS + s * 32:PP_COS + s * 32 + 32].unsqueeze(1).to_broadcast([n, 8, 32])
            sinb = pp[0:n, PP_SIN + s * 32:PP_SIN + s * 32 + 32].unsqueeze(1).to_broadcast([n, 8, 32])
            tv = [tmp[0:n, i, 0:256].rearrange("p (g d) -> p g d", g=8) for i in range(4)]
            rd = [R_pb[src_bank], R_pp]
            fw.op(dve, lambda: nc.vector.tensor_tensor(tv[0], sv[:, :, 0, :], cosb, ALU.mult), reads=rd, writes=[R_tmp[0]])
            fw.op(dve, lambda: nc.vector.tensor_tensor(tv[1], sv[:, :, 1, :], sinb, ALU.mult), reads=rd, writes=[R_tmp[1]])
            fw.op(dve, lambda: nc.vector.tensor_tensor(tv[2], sv[:, :, 1, :], cosb, ALU.mult), reads=rd, writes=[R_tmp[2]])
            fw.op(dve, lambda: nc.vector.tensor_tensor(tv[3], sv[:, :, 0, :], sinb, ALU.mult), reads=rd, writes=[R_tmp[3]])
            fw.op(pool, lambda: nc.gpsimd.tensor_tensor(dv[:, :, 0, :], tv[0], tv[1], ALU.subtract),
                  reads=[R_tmp[0], R_tmp[1]], writes=[dst_res])
            fw.op(pool, lambda: nc.gpsimd.tensor_tensor(dv[:, :, 1, :], tv[2], tv[3], ALU.add),
                  reads=[R_tmp[2], R_tmp[3]], writes=[dst_res])

        def transpose4(src_fn, n, dst_ap, src_res, dst_res, bank, eng):
            pbb = pb[bank][:].bitcast(BF16)
            fw.deps(pe, [src_res, R_c], [R_pb[bank]])
            for h in range(4):
                ins = nc.tensor.transpose(pbb[:, h * 128:h * 128 + n], src_fn(h), idb[0:n, 0:n])
            pe.count += 1
            ins.then_inc(pe.sem, 1)
            fw.commit((pe.sem, pe.count), [src_res, R_c], [R_pb[bank]])
            src_ap = pbb[:, 0:512].rearrange("p (h t) -> p h t", h=4)[:, :, 0:n]
            if eng is act:
                fw.op(act, lambda: nc.scalar.copy(dst_ap, src_ap), reads=[R_pb[bank]], writes=[dst_res])
            else:
                fw.op(dve, lambda: nc.vector.tensor_copy(dst_ap, src_ap), reads=[R_pb[bank]], writes=[dst_res])

        def mixer_preload(l):
            load_cols(gu[0][:, 0], win_d, l, 512, 512, R_gu[0])
            load_cols(gu[0][:, 1], win_d, l, 1024, 512, R_gu[0], nowait=True)

        def mixer(l):
            gi = l * 3 + 1
            with ExitStack() as mph:
                kT = sb(mph, "kT", (128, NH, NT), BF16)
                Vx = sb(mph, "Vx", (128, NSUB, NH, 129), BF16)
                R_kT = [Res(f"kT{s}") for s in range(NSUB)]
                R_V = [Res(f"V{s}") for s in range(NSUB)]
                with ExitStack() as ph:
                    nb = norm_bufs(ph)
                    hTt = sb(ph, "hTt", (128, 2, 8, 512), BF16)
                    R_ht = [Res("ht0"), Res("ht1")]
                    kst = sb(ph, "kst", (128, 2, 512), F32)
                    R_kst = [Res("kst0"), Res("kst1")]
                    vst = sb(ph, "vst", (128, 2, 512), F32)
                    R_vst = [Res("vst0"), Res("vst1")]
                    kbb = sb(ph, "kbb", (128, 2, 512), BF16)
                    R_kb = [Res("kb0"), Res("kb1")]
                    rtmp = sb(ph, "rtmp", (128, 8, 256), F32)
                    R_rt = [Res(f"rt{i}") for i in range(8)]
                    wk = gu[0][:, 0]
                    wv = gu[0][:, 1]
                    load_cols(gu[1][:, 0], win_d, l, 2048, 512, R_gu[1])
                    load_rows(dn[:].rearrange("p a f n -> p (a f) n"), wout_d, l, 0, D, R_dn[0])
                    fw.op(pool, lambda: nc.gpsimd.memset(Vx[:, :, :, 128:129], 1.0), writes=R_V)
                    pend = []

                    def flush_one():
                        m_, c_, s_ = pend.pop(0)
                        transpose4(lambda h: kbb[0:m_, c_, h * 128:(h + 1) * 128], m_, kT[:, :, s_ * 128:s_ * 128 + m_],
                                   R_kb[c_], R_kT[s_], 4 + c_, act)

                    def pe_warm(nmm):
                        fw.deps(pe, [R_gu[0], R_c], [R_pb[7]])
                        for _ in range(nmm):
                            ins = nc.tensor.matmul(pb[7][:, :], onesb[:], wk[:, 0, :], start=True, stop=True)
                        pe.count += 1
                        ins.then_inc(pe.sem, 1)
                        fw.commit((pe.sem, pe.count), [R_gu[0], R_c], [R_pb[7]])

                    def m1_rms(ti_):
                        t0_, n_ = TILES[ti_]
                        hs_ = ti_ % 2
                        if n_ >= 512:
                            pe_warm(20)
                        rms_tile(nb, gi, t0_, n_, lambda dc: hTt[:, hs_, dc, 0:n_], R_ht[hs_], 6, sq_eng="mix")

                    m1_rms(0)
                    for ti, (t0, n) in enumerate(TILES):
                        hs = ti % 2
                        if ti + 1 < len(TILES):
                            m1_rms(ti + 1)
                        for sub in range((n + 127) // 128):
                            s = t0 // 128 + sub
                            m = min(128, n - sub * 128)
                            c = s % 2
                            kb_, vb_ = c, 2 + c
                            mm_group(pb[kb_][0:m, :], [(hTt[:, hs, dc, sub * 128:sub * 128 + m], wk[:, dc, :]) for dc in range(8)],
                                     [R_ht[hs], R_gu[0]], [R_pb[kb_]])
                            mm_group(pb[vb_][0:m, :], [(hTt[:, hs, dc, sub * 128:sub * 128 + m], wv[:, dc, :]) for dc in range(8)],
                                     [R_ht[hs], R_gu[0]], [R_pb[vb_]])
                            rope(kb_, m, s, kst[0:m, c, :], R_kst[c], rtmp[:, 4 * c:4 * c + 4], R_rt[4 * c:4 * c + 4])
                            kdst = kp_d[l, s * 128:s * 128 + m, :] if s < 16 else ks_d[l]
                            fw.dma(sp, kdst, kst[0:m, c, :], R_kst[c], reads=[R_kst[c]])
                            fw.op(act, lambda: nc.scalar.copy(kbb[0:m, c, :], kst[0:m, c, :]), reads=[R_kst[c]], writes=[R_kb[c]])
                            fw.op(act, lambda: nc.scalar.copy(vst[0:m, c, :], pb[vb_][0:m, :]), reads=[R_pb[vb_]], writes=[R_vst[c]])
                            vdst = vp_d[l, s * 128:s * 128 + m, :] if s < 16 else vs_d[l]
                            fw.dma(sp, vdst, vst[0:m, c, :], R_vst[c], reads=[R_vst[c]])
                            fw.op(act, lambda: nc.scalar.copy(Vx[0:m, s, :, 0:128], vst[0:m, c, :].rearrange("p (h e) -> p h e", h=4)),
                                  reads=[R_vst[c]], writes=[R_V[s]])
                            pend.append((m, c, s))
                            if len(pend) > 1:
                                flush_one()
                    while pend:
                        flush_one()
                    fw.barrier()
                if stage(2 + 10 * l):
                    return
                wq = gu[0][:, 0]
                wuh = gu[0][:, 1]
                wbc = gu[1][:, 0]
                wo = dn[:].rearrange("p a f n -> p (a f) n")
                load_cols(wq, win_d, l, 0, 512, R_gu[0])
                load_cols(wuh, win_d, l, 1536, 512, R_gu[0], nowait=True)
                R_wo = [R_dn[0]]
                M2T = [(i * 256, 256) for i in range(8)] + [(2048, 32)]
                with ExitStack() as ph:
                    nb = norm_bufs(ph, 256, 1)
                    hT2v = gu[1][:, 1]
                    R_ht = [Res("h20"), Res("h21")]
                    qbb = sb(ph, "qbb", (128, 2, 512), BF16)
                    R_qb = [Res("qb0"), Res("qb1")]
                    rtmp = sb(ph, "rtmp2", (128, 4, 258), F32)
                    R_rt = [Res(f"rt2{i}") for i in range(4)]
                    qT = sb(ph, "qT", (128, 2, NH, 256), BF16)
                    R_qT = [[Res(f"qT{a_}{b_}") for b_ in range(2)] for a_ in range(2)]
                    PT = sb(ph, "PT", (128, 2, 2, 2, 256), BF16)
                    R_PT = [Res(f"PT{i}") for i in range(2)]
                    mixT = sb(ph, "mixT", (128, 8, 256), BF16)
                    R_mx = [Res(f"mx{c}") for c in range(8)]
                    ubuf = sb(ph, "ubuf", (128, 3, 2, 272), F32)
                    R_ub = [Res(f"ub{i}") for i in range(3)]
                    cvb = sb(ph, "cvb", (128, 2, 258), F32)
                    R_cvb = Res("cvb")
                    cvt = sb(ph, "cvt", (128, 2, 256), F32)
                    R_cvt = [Res("cvt0"), Res("cvt1")]
                    dbf = sb(ph, "dbf", (128, 2, 256), BF16)
                    R_dbf = [Res("dbf0"), Res("dbf1")]
                    oev = sb(ph, "oev", (128, 2, 2, 129), F32)
                    R_oev = [Res("oev0"), Res("oev1")]
                    osb = sb(ph, "osb", (128, 2, 128), F32)
                    R_os = [Res("os0"), Res("os1")]
                    pend_fin = []
                    sml = sb(ph, "sml", (128, 2, 8), F32)
                    R_sml = [Res("sml0"), Res("sml1")]
                    yat = sb(ph, "yat", (128, 2, 512), BF16)
                    R_yat = [Res("yat0"), Res("yat1")]
                    kcs = sb(ph, "kcs", (128, 2, 2, 512), BF16)
                    R_kcs = [Res("kcs0"), Res("kcs1")]
                    kcT = sb(ph, "kcT", (128, 2, NH, 256), BF16)
                    R_kcT = [Res("kcT0"), Res("kcT1")]
                    vcs = sb(ph, "vcs", (128, 2, 2, 512), BF16)
                    R_vcs = [Res("vcs0"), Res("vcs1")]
                    PTs = PT[:].rearrange("p s e c q -> p s (e c q)")[:, :, 0:512].rearrange("p s (c h j q) -> p s c h j q", c=2, h=NH, j=2)
                    R_PTs = [R_PT[0], R_PT[1]]
                    oacc = rtmp[0:DEC].rearrange("p a b -> p (a b)").rearrange("p (g e) -> p g e", g=8)
                    R_oacc = R_rt

                    def hTt_ap(hs, dc, a_, b_):
                        return hT2v[:, dc, hs * 256 + a_:hs * 256 + b_]

                    fw.op(pool, lambda: nc.gpsimd.memset(ubuf[:], 0.0), writes=R_ub)
                    fw.op(pool, lambda: nc.gpsimd.memset(cvb[:], 0.0), writes=[R_cvb])
                    NCB = 16

                    def load_ck(cb):
                        sl = cb % 2
                        fw.dma(pool, kcs[:, sl], ck_d[l, cb * 256:(cb + 1) * 256, :].rearrange("(j p) f -> p j f", p=128),
                               R_kcs[sl], writes=[R_kcs[sl]])

                    def load_cv(cb):
                        sl = cb % 2
                        fw.dma(pool, vcs[:, sl], cv_d[l, cb * 256:(cb + 1) * 256, :].rearrange("(j p) f -> p j f", p=128),
                               R_vcs[sl], writes=[R_vcs[sl]])

                    if l == 0 and os.environ.get("KDEBUG"):
                        print("M2 sbuf bytes remaining:", nc.sbuf_bytes_remaining)
                    load_ck(0)
                    load_cv(0)
                    load_ck(1)
                    load_cv(1)
                    misc = {"n": 0}

                    def mbank():
                        if misc.get("mode") == "in":
                            return 7
                        b = (7, 0, 1, 6)[misc["n"] % 4]
                        misc["n"] += 1
                        return b

                    ptc = {"n": 0, "e": 0}

                    class BV:
                        def __init__(self, ap, res):
                            self.ap = ap
                            self.res = res if isinstance(res, list) else [res]

                        def __getitem__(self, k):
                            return self.ap[k]

                    def flush_fin():
                        while pend_fin:
                            pend_fin.pop(0)()

                    def epilogue(m, h, src1, src2, yslot):
                        k = ptc["e"] % 2
                        ptc["e"] += 1
                        r1, r2 = src1.res, src2.res
                        ob_ = osb[0:m, k, :]
                        fw.op(dve, lambda: nc.vector.reciprocal(sml[0:m, k, 0:1], src1[0:m, 128:129]), reads=r1, writes=[R_sml[k]])
                        fw.op(dve, lambda: nc.vector.reciprocal(sml[0:m, k, 1:2], src2[0:m, 128:129]), reads=r2 + [R_sml[k]], writes=[R_sml[k]])
                        fw.op(dve, lambda: nc.vector.tensor_tensor(sml[0:m, k, 1:2], sml[0:m, k, 1:2], lamt[0:m, l, 1:2], ALU.mult),
                              reads=[R_sml[k], R_c], writes=[R_sml[k]])
                        fw.op(dve, lambda: nc.vector.tensor_scalar(ob_, src1[0:m, 0:128], sml[0:m, k, 0:1], None, ALU.mult),
                              reads=r1 + [R_sml[k]], writes=[R_os[k]])
                        fw.op(dve, lambda: nc.vector.scalar_tensor_tensor(ob_, src2[0:m, 0:128], sml[0:m, k, 1:2], ob_,
                                                                          ALU.mult, ALU.add),
                              reads=r2 + [R_sml[k], R_os[k]], writes=[R_os[k]])
                        fw.op(dve, lambda: nc.vector.tensor_tensor(src1[0:m, 0:128], ob_, ob_, ALU.mult),
                              reads=[R_os[k]] + r1, writes=r1)
                        fw.op(dve, lambda: nc.vector.reduce_sum(sml[0:m, k, 2:3], src1[0:m, 0:128], axis=AX.X),
                              reads=r1 + [R_sml[k]], writes=[R_sml[k]])
                        flush_fin()
                        fw.op(dve, lambda: nc.vector.tensor_scalar(sml[0:m, k, 3:4], sml[0:m, k, 2:3], 1.0 / 128, EPS, ALU.mult, ALU.add),
                              reads=[R_sml[k]], writes=[R_sml[k]])
                        fw.op(pool, lambda: nc.gpsimd.tensor_tensor(sml[0:m, k, 3:4], sml[0:m, k, 3:4], cneg[0:m, 0:1], ALU.pow),
                              reads=[R_sml[k], R_c], writes=[R_sml[k]])

                        def fin():
                            fw.op(dve, lambda: nc.vector.scalar_tensor_tensor(yat[0:m, yslot, h * 128:(h + 1) * 128], ob_, sml[0:m, k, 3:4],
                                                                              subg[0:m, l, :], ALU.mult, ALU.mult),
                                  reads=[R_os[k], R_sml[k], R_c], writes=[R_yat[yslot]])
                        pend_fin.append(fin)

                    def tinfo(ti):
                        t0, n = M2T[ti]
                        return t0, n, ti % 2, (t0 >= SEQ), (n + 127) // 128

                    def qchain(ti):
                        t0, n, hs, is_s, nsub = tinfo(ti)
                        sq, R_sq, rstd, R_rs = nb["sq"], nb["R_sq"], nb["rstd"], nb["R_rs"]
                        bank = mbank()

                        def square(dc):
                            fw.op(act, lambda: nc.scalar.activation(sq[:, dc % 4, 0:n], xT[:, dc, t0:t0 + n], AF.Square),
                                  reads=RX(t0, n, [dc]), writes=[R_sq[dc % 4]])

                        def ssmm(dc):
                            fw.op(pe, lambda: nc.tensor.matmul(pb[bank][:, 0:n], onesb[:], sq[:, dc % 4, 0:n], start=(dc == 0), stop=(dc == 7)),
                                  reads=[R_sq[dc % 4], R_c], writes=[R_pb[bank]])
                        for dc in range(4):
                            square(dc)
                        yield
                        for dc in range(4):
                            ssmm(dc)
                            square(4 + dc)
                        for dc in range(4, 8):
                            ssmm(dc)
                        j = 0
                        fw.op(act, lambda: nc.scalar.activation(rstd[:, j, 0:n], pb[bank][:, 0:n], AF.Ln, bias=nb["epsb"][:, 0:1], scale=1.0 / D),
                              reads=[R_pb[bank]], writes=[R_rs[j]])
                        fw.op(act, lambda: nc.scalar.activation(rstd[:, j, 0:n], rstd[:, j, 0:n], AF.Exp, scale=-0.5),
                              reads=[R_rs[j]], writes=[R_rs[j]])
                        for dc in range(8):
                            fw.op(dve, lambda: nc.vector.scalar_tensor_tensor(hTt_ap(hs, dc, 0, n), xT[:, dc, t0:t0 + n], ppc(PP_G + gi * 8 + dc),
                                                                              rstd[:, j, 0:n], ALU.mult, ALU.mult),
                                  reads=[R_rs[j], R_pp] + RX(t0, n, [dc]), writes=[R_ht[hs]])
                        yield
                        for sub in range(nsub):
                            s_ = t0 // 128 + sub
                            m = min(128, n - sub * 128)
                            c = s_ % 2
                            qb_ = mbank()
                            mm_group(pb[qb_][0:m, :], [(hTt_ap(hs, dc, sub * 128, sub * 128 + m), wq[:, dc, :]) for dc in range(8)],
                                     [R_ht[hs], R_gu[0]], [R_pb[qb_]])
                            rope(qb_, m, s_, qbb[0:m, c, :], R_qb[c], rtmp, R_rt)
                            yield
                        for sub in range(nsub):
                            s_ = t0 // 128 + sub
                            m = min(128, n - sub * 128)
                            c = s_ % 2
                            transpose4(lambda h: qbb[0:m, c, h * 128:(h + 1) * 128], m, qT[:, hs, :, sub * 128:sub * 128 + m],
                                       R_qb[c], R_qT[hs][sub], mbank(), act)
                            yield

                    def fproj(ti, wt, col, bank):
                        t0, n, hs, is_s, nsub = tinfo(ti)
                        mm_group(pb[bank][:, 0:n], [(wt[:, dc, col * 128:(col + 1) * 128], hTt_ap(hs, dc, 0, n)) for dc in range(8)],
                                 [R_ht[hs], R_gu[0], R_gu[1]], [R_pb[bank]])

                    def early(ti):
                        t0, n, hs, is_s, nsub = tinfo(ti)
                        if ti == 0:
                            pass
                        elif is_s:
                            with nc.allow_non_contiguous_dma(reason="tiny state halo loads"):
                                for ch in range(2):
                                    fw.dma(sp, ubuf[:, 0, ch, 1:16], stp_d[l, :, ch * 128:(ch + 1) * 128].rearrange("t p -> p t"), R_ub[0],
                                           writes=[R_ub[0]], nowait=(ch > 0))
                                    fw.dma(sp, cvb[:, ch, 0:2], stc_d[l, :, ch * 128:(ch + 1) * 128].rearrange("t p -> p t"), R_cvb,
                                           writes=[R_cvb], nowait=(ch > 0))
                        else:
                            fw.op(pool, lambda: nc.gpsimd.tensor_copy(ubuf[:, 0, :, 0:16], ubuf[:, 0, :, 256:272]), reads=[R_ub[0]], writes=[R_ub[0]])
                            fw.op(pool, lambda: nc.gpsimd.tensor_copy(cvb[:, :, 0:2], cvb[:, :, 256:258]), reads=[R_cvb], writes=[R_cvb])
                        for ch in range(2):
                            b = mbank()
                            fproj(ti, wuh, ch, b)
                            fw.op(dve, lambda: nc.vector.tensor_copy(ubuf[:, 0, ch, 16:16 + n], pb[b][:, 0:n]), reads=[R_pb[b]], writes=[R_ub[0]])
                        for ch in range(2):
                            b = mbank()
                            fproj(ti, wuh, 2 + ch, b)
                            fw.op(dve, lambda: nc.vector.tensor_copy(cvt[:, ch, 0:n], pb[b][:, 0:n]), reads=[R_pb[b]], writes=[R_cvt[ch]])
                        for ch in range(2):
                            b = mbank()
                            fproj(ti, wbc, 2 + ch, b)
                            fw.op(dve, lambda: nc.vector.tensor_tensor(cvb[:, ch, 2:2 + n], pb[b][:, 0:n], cvt[:, ch, 0:n], ALU.mult),
                                  reads=[R_pb[b], R_cvt[ch]], writes=[R_cvb])
                        if t0 + n == SEQ or is_s:
                            pd = pls_d if is_s else plp_d
                            cd = cvs_d if is_s else cvp_d
                            with nc.allow_non_contiguous_dma(reason="tiny state outputs"):
                                for ch in range(2):
                                    fw.dma(sp, pd[l, :, ch * 128:(ch + 1) * 128].rearrange("t p -> p t"), ubuf[:, 0, ch, 16 + n - 15:16 + n],
                                           R_ub[0], reads=[R_ub[0]])
                                    fw.dma(sp, cd[l, :, ch * 128:(ch + 1) * 128].rearrange("t p -> p t"), cvb[:, ch, n:n + 2],
                                           R_cvb, reads=[R_cvb])
                        for ch in range(2):
                            cw = PP_CW + l * 6 + ch * 3
                            fw.op(dve, lambda: nc.vector.tensor_scalar(cvt[:, ch, 0:n], cvb[:, ch, 0:n], ppc(cw), None, ALU.mult),
                                  reads=[R_cvb, R_pp], writes=[R_cvt[ch]])
                            fw.op(dve, lambda: nc.vector.scalar_tensor_tensor(cvt[:, ch, 0:n], cvb[:, ch, 1:1 + n], ppc(cw + 1), cvt[:, ch, 0:n],
                                                                              ALU.mult, ALU.add),
                                  reads=[R_cvb, R_pp, R_cvt[ch]], writes=[R_cvt[ch]])
                            fw.op(dve, lambda: nc.vector.scalar_tensor_tensor(cvt[:, ch, 0:n], cvb[:, ch, 2:2 + n], ppc(cw + 2), cvt[:, ch, 0:n],
                                                                              ALU.mult, ALU.add),
                                  reads=[R_cvb, R_pp, R_cvt[ch]], writes=[R_cvt[ch]])
                        W = 16 + n
                        cur = 0
                        for step in range(4):
                            sh = 1 << step
                            nxt = 1 + (step % 2)
                            lo = 2 * sh
                            for ch in range(2):
                                if ch == 0 and step >= 2:
                                    continue
                                fw.op(dve, lambda: nc.vector.scalar_tensor_tensor(ubuf[:, nxt, ch, lo:W], ubuf[:, cur, ch, lo - sh:W - sh],
                                                                                  msk[:, ch, step:step + 1], ubuf[:, cur, ch, lo:W],
                                                                                  ALU.mult, ALU.add),
                                      reads=[R_ub[cur], R_c], writes=[R_ub[nxt]])
                            cur = nxt
                        for ch in range(2):
                            fin = 2
                            fw.op(dve, lambda: nc.vector.scalar_tensor_tensor(dbf[:, ch, 0:n], ubuf[:, fin, ch, 16:16 + n], invw[:, ch:ch + 1],
                                                                              ubuf[:, 0, ch, 16:16 + n], ALU.mult, ALU.subtract),
                                  reads=[R_ub[fin], R_ub[0], R_c], writes=[R_dbf[ch]])
                            if ti == 0:
                                fw.op(pool, lambda: nc.gpsimd.tensor_tensor(ubuf[:, 1, ch, 0:16], ubuf[:, fin, ch, 16:32], ppc(PP_IC + ch * 16, 16), ALU.mult),
                                      reads=[R_ub[fin], R_pp, R_ub[1]], writes=[R_ub[1]])
                                fw.op(pool, lambda: nc.gpsimd.tensor_tensor(dbf[:, ch, 0:16], ubuf[:, 1, ch, 0:16], ubuf[:, 0, ch, 16:32], ALU.subtract),
                                      reads=[R_ub[1], R_ub[0], R_dbf[ch]], writes=[R_dbf[ch]])

                    def attention_prompt(ti, steps):
                        t0, n, hs, is_s, nsub = tinfo(ti)
                        g = ti
                        iters = [(h, kp) for h in range(NH) for kp in range(g + 1)]

                        def emit_S(i):
                            h, kp = iters[i]
                            banks = ((0, 1, 6)[(2 * i) % 3], (0, 1, 6)[(2 * i + 1) % 3])
                            slot = i % 2
                            rds = [R_kT[2 * kp], R_kT[2 * kp + 1], R_qT[hs][0], R_qT[hs][1], R_c]
                            for c in range(2):
                                sbk = banks[c]
                                fw.deps(pe, rds, [R_pb[sbk]])
                                for e in range(2):
                                    kt = 2 * kp + e
                                    c0 = 128 if kt == 2 * g + 1 else 0
                                    diag = (kt >= 2 * g)
                                    oap = pb[sbk][:, e * 256 + c0:e * 256 + 256]
                                    ins = nc.tensor.matmul(oap, kT[c * 64:(c + 1) * 64, h, kt * 128:(kt + 1) * 128],
                                                           qT[c * 64:(c + 1) * 64, hs, h, c0:256], start=True, stop=not diag)
                                    if diag:
                                        ins = nc.tensor.matmul(oap, mrow[c * 64:c * 64 + 1, :], mcol[c * 64:c * 64 + 1, 0:256 - c0],
                                                               start=False, stop=True)
                                    if c0 > 0:
                                        ins = nc.tensor.matmul(pb[sbk][:, e * 256:e * 256 + c0], mrow[c * 64:c * 64 + 1, :],
                                                               mcol[c * 64:c * 64 + 1, 64:64 + c0], start=True, stop=True)
                                pe.count += 1
                                ins.then_inc(pe.sem, 1)
                                fw.commit((pe.sem, pe.count), rds, [R_pb[sbk]])
                            for c in range(2):
                                sbk = banks[c]
                                fw.op(act, lambda: nc.scalar.activation(PT[:, slot, :, c, :], pb[sbk][:].rearrange("p (e q) -> p e q", e=2),
                                                                        AF.Exp, scale=SCALE),
                                      reads=[R_pb[sbk]], writes=[R_PT[slot]])

                        def emit_PV(i):
                            h, kp = iters[i]
                            slot = i % 2
                            closed = []
                            st_ = {"ins": None, "banks": [], "kts": []}

                            def signal():
                                if st_["ins"] is None:
                                    return
                                pe.count += 1
                                st_["ins"].then_inc(pe.sem, 1)
                                fw.commit((pe.sem, pe.count), [R_PT[slot]] + [R_V[k_] for k_ in st_["kts"]],
                                          [R_pb[b_] for b_ in st_["banks"]])
                                st_["ins"], st_["banks"], st_["kts"] = None, [], []

                            for e in range(2):
                                kt = 2 * kp + e
                                for qs in range(2):
                                    if kt > 2 * g + qs:
                                        continue
                                    for c in range(2):
                                        ob = 2 + qs * 2 + c
                                        first = (kt == 0)
                                        last = (kt == 2 * g + qs)
                                        fw.deps(pe, [R_PT[slot], R_V[kt]], [R_pb[ob]] if first else [])
                                        st_["ins"] = nc.tensor.matmul(pb[ob][:, 0:129], PT[:, slot, e, c, qs * 128:(qs + 1) * 128], Vx[:, kt, h, :],
                                                                      start=first, stop=last)
                                        if ob not in st_["banks"]:
                                            st_["banks"].append(ob)
                                        if kt not in st_["kts"]:
                                            st_["kts"].append(kt)
                                    if kt == 2 * g + qs:
                                        signal()
                                        k = ptc["n"] % 2
                                        ptc["n"] += 1
                                        closed.append((qs, k))
                                        for c in range(2):
                                            ob = 2 + qs * 2 + c
                                            fw.op(dve, lambda: nc.vector.tensor_copy(oev[:, k, c, :], pb[ob][:, 0:129]), reads=[R_pb[ob]], writes=[R_oev[k]])
                            signal()
                            for qs, k in closed:
                                epilogue(128, h, BV(oev[:, k, 0, :], R_oev[k]), BV(oev[:, k, 1, :], R_oev[k]), qs)

                        nsteps = 6
                        stride = max(1, len(iters) // (nsteps + 1))
                        misc["mode"] = "in"
                        emit_S(0)
                        for i in range(len(iters)):
                            if i + 1 < len(iters):
                                emit_S(i + 1)
                            emit_PV(i)
                            if steps is not None and i % stride == stride - 1:
                                next(steps, None)
                        if steps is not None:
                            for _ in steps:
                                pass
                        misc["mode"] = "out"
                        wb = mbank()
                        fw.deps(pe, [R_gu[0], R_c], [R_pb[wb]])
                        for _ in range(12):
                            ins = nc.tensor.matmul(pb[wb][:, :], onesb[:], wq[:, 0, :], start=True, stop=True)
                        pe.count += 1
                        ins.then_inc(pe.sem, 1)
                        fw.commit((pe.sem, pe.count), [R_gu[0], R_c], [R_pb[wb]])
                        flush_fin()
                        for qs in range(2):
                            transpose4(lambda h: yat[:, qs, h * 128:(h + 1) * 128], 128, mixT[:, 0:4, qs * 128:(qs + 1) * 128],
                                       R_yat[qs], R_mx[qs], mbank(), dve)
                        return [R_mx[0], R_mx[1]]

                    def attention_sample(ti):
                        t0, n, hs, is_s, nsub = tinfo(ti)
                        fw.op(pool, lambda: nc.gpsimd.memset(oacc, 0.0), writes=R_oacc)

                        def blk(b_):
                            cache = b_ < NCB
                            return cache, b_ % 2, (2 if cache else 1), (128 if cache else DEC)

                        def emit_T(b_):
                            cache, sl, nj, kp_ = blk(b_)
                            if not cache:
                                return
                            pbb = pb[5][:].bitcast(BF16)
                            fw.deps(pe, [R_kcs[sl], R_c], [R_pb[5]])
                            for j in range(2):
                                for h in range(NH):
                                    ins = nc.tensor.transpose(pbb[:, (h * 2 + j) * 128:(h * 2 + j + 1) * 128], kcs[:, sl, j, h * 128:(h + 1) * 128], idb[:])
                            pe.count += 1
                            ins.then_inc(pe.sem, 1)
                            fw.commit((pe.sem, pe.count), [R_kcs[sl], R_c], [R_pb[5]])
                            fw.op(dve, lambda: nc.vector.tensor_copy(kcT[:, sl], pbb[:].rearrange("p (h k) -> p h k", h=NH)),
                                  reads=[R_pb[5]], writes=[R_kcT[sl]])

                        def emit_S(b_):
                            cache, sl, nj, kp_ = blk(b_)
                            kres = [R_kcT[sl]] if cache else [R_kT[16]]
                            banks = (0, 1) if b_ % 2 == 0 else (6, 7)
                            for c in range(2):
                                sbk = banks[c]
                                Sv = pb[sbk][:, 0:256].rearrange("p (h j q) -> p h j q", h=4, j=2)
                                fw.deps(pe, kres + [R_qT[hs][0]], [R_pb[sbk]])
                                for h in range(NH):
                                    for j in range(nj):
                                        if cache:
                                            lk = kcT[c * 64:(c + 1) * 64, sl, h, j * 128:(j + 1) * 128]
                                        else:
                                            lk = kT[c * 64:(c + 1) * 64, h, SEQ:SEQ + DEC]
                                        ins = nc.tensor.matmul(Sv[0:kp_, h, j, :], lk, qT[c * 64:(c + 1) * 64, hs, h, 0:DEC],
                                                               start=True, stop=True)
                                pe.count += 1
                                ins.then_inc(pe.sem, 1)
                                fw.commit((pe.sem, pe.count), kres + [R_qT[hs][0]], [R_pb[sbk]])
                            for c in range(2):
                                sbk = banks[c]
                                Sv2 = pb[sbk][:, 0:256].rearrange("p (h j q) -> p h j q", h=4, j=2)
                                fw.op(act, lambda: nc.scalar.activation(PTs[0:kp_, sl, c, :, 0:nj, :], Sv2[0:kp_, :, 0:nj, :],
                                                                        AF.Exp, scale=SCALE),
                                      reads=[R_pb[sbk]], writes=[R_PTs[sl]])

                        def emit_PV(b_):
                            cache, sl, nj, kp_ = blk(b_)
                            vres = [R_vcs[sl]] if cache else [R_V[16]]
                            for bi, grp in enumerate([(0, 1, 2), (3, 4, 5), (6, 7)]):
                                ob = 2 + bi
                                rds = vres + [R_PTs[sl], R_c]
                                fw.deps(pe, rds, [R_pb[ob]])
                                for gi_, hc_ in enumerate(grp):
                                    h, c = hc_ // 2, hc_ % 2
                                    if cache:
                                        for j in range(nj):
                                            ins = nc.tensor.matmul(pb[ob][0:DEC, gi_ * 129:gi_ * 129 + 128], PTs[0:kp_, sl, c, h, j, :],
                                                                   vcs[0:kp_, sl, j, h * 128:(h + 1) * 128], start=(j == 0), stop=(j == nj - 1))
                                        for j in range(nj):
                                            ins = nc.tensor.matmul(pb[ob][0:DEC, gi_ * 129 + 128:gi_ * 129 + 129], PTs[0:kp_, sl, c, h, j, :],
                                                                   onesb[0:kp_, 0:1], start=(j == 0), stop=(j == nj - 1))
                                    else:
                                        ins = nc.tensor.matmul(pb[ob][0:DEC, gi_ * 129:(gi_ + 1) * 129], PTs[0:kp_, sl, c, h, 0, :],
                                                               Vx[0:kp_, 16, h, :], start=True, stop=True)
                                pe.count += 1
                                ins.then_inc(pe.sem, 1)
                                fw.commit((pe.sem, pe.count), rds, [R_pb[ob]])
                                ng = len(grp)
                                oav = oacc[:, grp[0]:grp[0] + ng, :].rearrange("p g e -> p (g e)")
                                fw.op(dve, lambda: nc.vector.tensor_tensor(oav, oav, pb[ob][0:DEC, 0:ng * 129], ALU.add),
                                      reads=[R_pb[ob]] + R_oacc, writes=R_oacc)

                        NB = NCB + 1
                        emit_T(0)
                        emit_S(0)
                        for b_ in range(NB):
                            if b_ + 1 < NB:
                                emit_T(b_ + 1)
                                if b_ + 2 < NCB:
                                    load_ck(b_ + 2)
                                emit_S(b_ + 1)
                            emit_PV(b_)
                            if b_ + 2 < NCB:
                                load_cv(b_ + 2)
                        for h in range(NH):
                            epilogue(DEC, h, BV(oacc[:, 2 * h, :], R_oacc), BV(oacc[:, 2 * h + 1, :], R_oacc), 0)
                        flush_fin()
                        transpose4(lambda h: yat[0:DEC, 0, h * 128:(h + 1) * 128], DEC, mixT[:, 0:4, 0:DEC],
                                   R_yat[0], R_mx[0], mbank(), dve)
                        return [R_mx[0]]

                    def tail(ti, attn_res):
                        t0, n, hs, is_s, nsub = tinfo(ti)
                        for ch in range(2):
                            b = mbank()
                            fproj(ti, wbc, ch, b)
                            fw.op(dve, lambda: nc.vector.tensor_tensor(mixT[:, 6 + ch, 0:n], pb[b][:, 0:n], cvt[:, ch, 0:n], ALU.mult),
                                  reads=[R_pb[b], R_cvt[ch]], writes=[R_mx[6 + ch]])
                        for ch in range(2):
                            b = mbank()
                            fw.op(pe, lambda: nc.tensor.matmul(pb[b][:, 0:n], pwb[:, l, ch, :], dbf[:, ch, 0:n], start=True, stop=True),
                                  reads=[R_pw, R_dbf[ch]], writes=[R_pb[b]])
                            fw.op(dve, lambda: nc.vector.tensor_scalar(mixT[:, 4 + ch, 0:n], pb[b][:, 0:n], ppc(PP_PS + l * 2 + ch), None, ALU.mult),
                                  reads=[R_pb[b], R_pp], writes=[R_mx[4 + ch]])
                        for dco in range(8):
                            b = mbank()
                            mm_group(pb[b][:, 0:n], [(wo[:, mc, dco * 128:(dco + 1) * 128], mixT[:, mc, 0:n]) for mc in range(8)],
                                     R_wo + attn_res + R_mx[4:8], [R_pb[b]])
                            xr = RX(t0, n, [dco])
                            fw.op(dve, lambda: nc.vector.tensor_tensor(xT[:, dco, t0:t0 + n], pb[b][:, 0:n], xT[:, dco, t0:t0 + n], ALU.add),
                                  reads=[R_pb[b]] + xr, writes=xr)

                    for _ in qchain(0):
                        pass
                    for ti in range(len(M2T)):
                        early(ti)
                        steps = qchain(ti + 1) if ti + 1 < len(M2T) else None
                        if ti < 8:
                            ares = attention_prompt(ti, steps)
                        else:
                            ares = attention_sample(ti)
                        tail(ti, ares)
                    fw.barrier()

        stage(0)
        for l in range(DEPTH):
            if stopped["v"]:
                break
            ffn(l, 0, l * 3 + 0, 2 if l == 0 else 1, prefetch=lambda: mixer_preload(l), xload=(l == 0))
            if stage(1 + 10 * l):
                break
            mixer(l)
            if stage(5 + 10 * l):
                break
            ffn(l, 1, l * 3 + 2, 0, prefetch=(lambda: ffn_load(l + 1, 0, 0)) if l + 1 < DEPTH else None)
            if stage(6 + 10 * l):
                break
        if stopped["v"]:
            fw.finish()
            return nc

        with ExitStack() as ph:
            nb = norm_bufs(ph)
            yf = sb(ph, "yf", (128, 2, 8, 512), F32)
            R_yf = [Res("yf0"), Res("yf1")]
            yst = sb(ph, "yst", (128, 2, D), F32)
            R_yst = [Res("yst0"), Res("yst1")]
            FTL = [(0, 512), (512, 512), (1024, 512), (1536, 512), (2048, 32)]

            def fin_out(ti, sub):
                t0, nt = FTL[ti]
                ks = ti % 2
                s = t0 // 128 + sub
                n = min(128, nt - sub * 128)
                k = s % 2
                for hb in range(2):
                    bank = (2 * s + hb) % 4
                    fw.deps(pe, [R_yf[ks], R_c], [R_pb[bank]])
                    for q in range(4):
                        dc = hb * 4 + q
                        ins = nc.tensor.transpose(pb[bank][0:n, q * 128:(q + 1) * 128], yf[:, ks, dc, sub * 128:sub * 128 + n], idf[:])
                    pe.count += 1
                    ins.then_inc(pe.sem, 1)
                    fw.commit((pe.sem, pe.count), [R_yf[ks], R_c], [R_pb[bank]])
                    if hb == 0:
                        fw.op(act, lambda: nc.scalar.copy(yst[0:n, k, 0:512], pb[bank][0:n, :]), reads=[R_pb[bank]], writes=[R_yst[k]])
                    else:
                        fw.op(dve, lambda: nc.vector.tensor_copy(yst[0:n, k, 512:1024], pb[bank][0:n, :]), reads=[R_pb[bank]], writes=[R_yst[k]])
                dst = yp_d[s * 128:(s + 1) * 128, :] if s < 16 else ys_d
                fw.dma(sp, dst, yst[0:n, k, :], R_yst[k], reads=[R_yst[k]])

            def fin_rms(ti):
                t0, nt = FTL[ti]
                ks = ti % 2
                rms_tile(nb, 6, t0, nt, lambda dc: yf[:, ks, dc, 0:nt], R_yf[ks], 6 + ti % 2, sq_eng="mix")

            fin_rms(0)
            for ti, (t0, nt) in enumerate(FTL):
                if ti + 1 < len(FTL):
                    fin_rms(ti + 1)
                for sub in range((nt + 127) // 128):
                    fin_out(ti, sub)
            fw.finish()
    return nc


def _param_pack(p):
    pp = np.zeros((128, NPP), np.float32)
    gains = [p["norm_ffn1"][0], p["norm_mix"][0], p["norm_ffn2"][0],
             p["norm_ffn1"][1], p["norm_mix"][1], p["norm_ffn2"][1], p["final_norm"]]
    for gi, g in enumerate(gains):
        pp[:, PP_G + gi * 8:PP_G + gi * 8 + 8] = np.asarray(g, np.float32).reshape(8, 128).T
    for l in range(DEPTH):
        pp[:, PP_PS + l * 2:PP_PS + l * 2 + 2] = np.asarray(p["pool_scale"][l], np.float32).reshape(2, 128).T
        cw = np.asarray(p["conv_w"][l], np.float32)
        for ch in range(2):
            pp[:, PP_CW + l * 6 + ch * 3:PP_CW + l * 6 + ch * 3 + 3] = cw[:, ch * 128:(ch + 1) * 128].T
        pp[:, PP_SUB + l * 128:PP_SUB + (l + 1) * 128] = np.asarray(p["subln"][l], np.float32)[None, :]
        for j, nm in enumerate(["lambda_q1", "lambda_k1", "lambda_q2", "lambda_k2"]):
            pp[:, PP_LAM + l * 256 + j * 64:PP_LAM + l * 256 + (j + 1) * 64] = np.asarray(p[nm][l], np.float32)[None, :]
    half = 32
    inv = np.power(np.float32(10000.0), -np.arange(half, dtype=np.float32) / np.float32(half)).astype(np.float32)
    pos = np.concatenate([np.arange(SEQ), PAST + np.arange(DEC)]).astype(np.float32)
    ang = (pos[:, None] * inv[None, :]).astype(np.float32)
    cos = np.cos(ang).astype(np.float32)
    sin = np.sin(ang).astype(np.float32)
    for s in range(NSUB):
        n = 128 if s < 16 else DEC
        pp[0:n, PP_COS + s * 32:PP_COS + (s + 1) * 32] = cos[s * 128:s * 128 + n]
        pp[0:n, PP_SIN + s * 32:PP_SIN + (s + 1) * 32] = sin[s * 128:s * 128 + n]
    wins = [2, 4, 8, 16]
    for ch in range(2):
        for half_i in range(2):
            w = wins[ch * 2 + half_i]
            for t in range(16):
                pp[half_i * 64:(half_i + 1) * 64, PP_IC + ch * 16 + t] = 1.0 / min(t + 1, w)
    return pp


_NC_CACHE = {}


def kernel(**inputs):
    p = {k: np.asarray(v) for k, v in inputs.items()}
    if "nc" not in _NC_CACHE:
        _NC_CACHE["nc"] = build_program()
    nc = _NC_CACHE["nc"]
    pp = _param_pack(p)
    pw = np.asarray(p["pool_w"], np.float32)
    pwbd = np.zeros((DEPTH, 2, 128, 128), np.float32)
    for l in range(DEPTH):
        for g in range(4):
            ch, o = g // 2, (g % 2) * 64
            pwbd[l, ch, o:o + 64, o:o + 64] = pw[l, g]
    shared = {
        "pp": pp, "pwbd": pwbd,
        "f1g": np.ascontiguousarray(p["ffn1_gate"], np.float32), "f1u": np.ascontiguousarray(p["ffn1_up"], np.float32),
        "f1d": np.ascontiguousarray(p["ffn1_down"], np.float32),
        "f2g": np.ascontiguousarray(p["ffn2_gate"], np.float32), "f2u": np.ascontiguousarray(p["ffn2_up"], np.float32),
        "f2d": np.ascontiguousarray(p["ffn2_down"], np.float32),
        "win": np.ascontiguousarray(p["w_in"], np.float32), "wout": np.ascontiguousarray(p["w_out"], np.float32),
    }
    in_maps = []
    for b in range(8):
        m = dict(shared)
        m["xp"] = np.ascontiguousarray(p["x_prompt"][b], np.float32)
        m["xs"] = np.ascontiguousarray(p["x_sample"][b], np.float32)
        m["ck"] = np.ascontiguousarray(p["cache_k"][:, b].reshape(DEPTH, PAST, 512), np.float32)
        m["cv"] = np.ascontiguousarray(p["cache_v"][:, b].reshape(DEPTH, PAST, 512), np.float32)
        m["stp"] = np.ascontiguousarray(p["state_pool"][:, b], np.float32)
        m["stc"] = np.ascontiguousarray(p["state_conv"][:, b], np.float32)
        in_maps.append(m)
    ncores = int(os.environ.get("KCORES", "8"))
    res = run_bass_kernel_spmd(nc, in_maps[:ncores], core_ids=list(range(ncores)))
    R = list(res.results)
    while len(R) < 8:
        R.append(R[0])

    def gather(name, shape_tail, axis_b):
        arrs = [np.asarray(R[b][name], np.float32) for b in range(8)]
        return np.stack(arrs, axis=axis_b)

    y_prompt = gather("yp", None, 0)
    y_sample = gather("ys", None, 0)
    k_prompt = gather("kp", None, 1).reshape(DEPTH, 8, SEQ, NH, 128)
    v_prompt = gather("vp", None, 1).reshape(DEPTH, 8, SEQ, NH, 128)
    pool_prompt = gather("plp", None, 1)
    conv_prompt = gather("cvp", None, 1)
    k_sample = gather("ks", None, 1).reshape(DEPTH, 8, DEC, NH, 128)
    v_sample = gather("vs", None, 1).reshape(DEPTH, 8, DEC, NH, 128)
    pool_sample = gather("pls", None, 1)
    conv_sample = gather("cvs", None, 1)
    return (y_prompt, y_sample, k_prompt, v_prompt, pool_prompt, conv_prompt,
            k_sample, v_sample, pool_sample, conv_sample)
```

```python
import math
import os
from contextlib import ExitStack

import numpy as np
import concourse.bass as bass
import concourse.mybir as mybir
from concourse.bass_utils import run_bass_kernel_spmd

F32 = mybir.dt.float32
BF16 = mybir.dt.bfloat16
ALU = mybir.AluOpType
AF = mybir.ActivationFunctionType
AX = mybir.AxisListType

D = 1024
SEQ = 2048
DEC = 32
NT = SEQ + DEC
PAST = 4096
DEPTH = 2
DFF = 2816
DIN = 2560
NH = 4
EPS = 1e-6
SCALE = 64 ** -0.5
NSUB = 17

PP_G = 0
PP_PS = PP_G + 56
PP_CW = PP_PS + 4
PP_SUB = PP_CW + 12
PP_LAM = PP_SUB + 256
PP_COS = PP_LAM + 512
PP_SIN = PP_COS + 544
PP_IC = PP_SIN + 544
NPP = PP_IC + 32


class Res:
    __slots__ = ("name", "writer", "readers", "dsem", "dcount", "nobar")

    def __init__(self, name, nobar=False):
        self.name = name
        self.writer = None
        self.readers = []
        self.dsem = None
        self.dcount = 0
        self.nobar = nobar


class Eng:
    def __init__(self, fw, raw, name, selfwait=True):
        self.fw = fw
        self.raw = raw
        self.name = name
        self.sem = fw.new_sem("e_" + name)
        self.count = 0
        self.seen = {}
        self.selfwait = selfwait

    def wait(self, tok):
        if tok is None:
            return
        sem, val = tok
        if sem is self.sem and not self.selfwait:
            return
        k = id(sem)
        if self.seen.get(k, 0) >= val:
            return
        self.raw.wait_ge(sem, val)
        self.seen[k] = val


class FW:
    def __init__(self, nc, es):
        self.nc = nc
        self.es = es
        self.pe = Eng(self, nc.tensor, "pe", selfwait=False)
        self.act = Eng(self, nc.scalar, "act")
        self.dve = Eng(self, nc.vector, "dve")
        self.pool = Eng(self, nc.gpsimd, "pool")
        self.sp = Eng(self, nc.sync, "sp", selfwait=False)
        self.engs = [self.pe, self.act, self.dve, self.pool, self.sp]
        self.dma_res = []

    def new_sem(self, name):
        self.nsem = getattr(self, "nsem", 0) + 1
        return self.es.enter_context(self.nc.semaphore(f"{name}_{self.nsem}"))

    def deps(self, eng, reads, writes):
        best = {}

        def add(tok):
            if tok is None:
                return
            k = id(tok[0])
            if k not in best or best[k][1] < tok[1]:
                best[k] = tok
        for r in reads:
            add(r.writer)
        for w in writes:
            add(w.writer)
            for t in w.readers:
                add(t)
        for tok in best.values():
            eng.wait(tok)

    def commit(self, tok, reads, writes):
        for r in reads:
            r.readers.append(tok)
            if len(r.readers) > 16:
                best = {}
                for s, v in r.readers:
                    if id(s) not in best or best[id(s)][1] < v:
                        best[id(s)] = (s, v)
                r.readers = list(best.values())
        for w in writes:
            w.writer = tok
            w.readers = []

    def op(self, eng, fn, reads=(), writes=(), signal=True):
        self.deps(eng, reads, writes)
        ins = fn()
        if signal:
            eng.count += 1
            ins.then_inc(eng.sem, 1)
            tok = (eng.sem, eng.count)
            self.commit(tok, reads, writes)
            return tok
        return None

    def dma(self, eng, out, in_, slot, reads=(), writes=(), nowait=False, **kw):
        if slot.dsem is None:
            slot.dsem = self.new_sem("d_" + slot.name)
            self.dma_res.append(slot)
        if not nowait:
            self.deps(eng, reads, writes)
        ins = eng.raw.dma_start(out=out, in_=in_, **kw)
        slot.dcount += 16
        ins.then_inc(slot.dsem, 16)
        tok = (slot.dsem, slot.dcount)
        self.commit(tok, reads, writes)
        return tok

    def barrier(self):
        toks = [(e.sem, e.count) for e in self.engs if e.count > 0]
        toks += [(r.dsem, r.dcount) for r in self.dma_res if r.dcount > 0 and not r.nobar]
        for e in self.engs:
            for t in toks:
                if t[0] is e.sem and not e.selfwait:
                    continue
                e.wait(t)

    def finish(self):
        toks = [(e.sem, e.count) for e in self.engs if e.count > 0]
        toks += [(r.dsem, r.dcount) for r in self.dma_res if r.dcount > 0]
        for t in toks:
            self.sp.wait(t)


class _Stop(Exception):
    pass


def build_program():
    nc = bass.Bass("TRN2", target_bir_lowering=False)
    STAGE = float(os.environ.get("KSTAGE", "99"))

    stopped = {"v": False}

    def stage(x):
        if STAGE <= x:
            stopped["v"] = True
        return stopped["v"]

    def din(name, shape):
        return nc.dram_tensor(name, list(shape), F32, kind="ExternalInput").ap()

    def dout(name, shape):
        return nc.dram_tensor(name, list(shape), F32, kind="ExternalOutput").ap()

    xp_d = din("xp", (SEQ, D))
    xs_d = din("xs", (DEC, D))
    ck_d = din("ck", (DEPTH, PAST, 512))
    cv_d = din("cv", (DEPTH, PAST, 512))
    stp_d = din("stp", (DEPTH, 15, 256))
    stc_d = din("stc", (DEPTH, 2, 256))
    pp_d = din("pp", (128, NPP))
    pw_d = din("pwbd", (DEPTH, 2, 128, 128))
    wg_d = [din("f1g", (DEPTH, D, DFF)), din("f2g", (DEPTH, D, DFF))]
    wu_d = [din("f1u", (DEPTH, D, DFF)), din("f2u", (DEPTH, D, DFF))]
    wd_d = [din("f1d", (DEPTH, DFF, D)), din("f2d", (DEPTH, DFF, D))]
    win_d = din("win", (DEPTH, D, DIN))
    wout_d = din("wout", (DEPTH, D, D))

    yp_d = dout("yp", (SEQ, D))
    ys_d = dout("ys", (DEC, D))
    kp_d = dout("kp", (DEPTH, SEQ, 512))
    vp_d = dout("vp", (DEPTH, SEQ, 512))
    plp_d = dout("plp", (DEPTH, 15, 256))
    cvp_d = dout("cvp", (DEPTH, 2, 256))
    ks_d = dout("ks", (DEPTH, DEC, 512))
    vs_d = dout("vs", (DEPTH, DEC, 512))
    pls_d = dout("pls", (DEPTH, 15, 256))
    cvs_d = dout("cvs", (DEPTH, 2, 256))

    with ExitStack() as es:
        fw = FW(nc, es)
        pe, act, dve, pool, sp = fw.pe, fw.act, fw.dve, fw.pool, fw.sp

        uid = {"n": 0}

        def sb(st, name, shape, dt):
            uid["n"] += 1
            return st.enter_context(nc.sbuf_tensor(f"s{uid['n']}_{name}", list(shape), dt))

        xT = sb(es, "xT", (128, 8, NT), F32)
        R_x = [[Res(f"x{dc}_{tl}") for tl in range(NSUB)] for dc in range(8)]

        def RX(t0, n, dcs=range(8)):
            s0, s1 = t0 // 128, (t0 + n - 1) // 128
            return [R_x[dc][s] for dc in dcs for s in range(s0, s1 + 1)]

        pp = sb(es, "pp", (128, NPP), F32)
        R_pp = Res("pp")
        pwb = sb(es, "pwb", (128, DEPTH, 2, 128), BF16)
        R_pw = Res("pwb")
        idf = sb(es, "idf", (128, 128), F32)
        idb = sb(es, "idb", (128, 128), BF16)
        onesb = sb(es, "onesb", (128, 128), BF16)
        R_c = Res("consts")
        lamt = sb(es, "lamt", (128, DEPTH, 4), F32)
        subg = sb(es, "subg", (128, DEPTH, 128), F32)
        msk = sb(es, "msk", (128, 2, 4), F32)
        invw = sb(es, "invw", (128, 2), F32)
        cneg = sb(es, "cneg", (128, 1), F32)
        mrow = sb(es, "mrow", (128, 128), BF16)
        mcol = sb(es, "mcol", (128, 256), BF16)
        gu = [sb(es, f"gu{i}", (128, 2, 8, 512), BF16) for i in range(2)]
        dn = sb(es, "dn", (128, 2, 4, 1024), BF16)
        R_gu = [Res(f"gu{i}", nobar=True) for i in range(2)]
        R_dn = [Res(f"dn{i}", nobar=True) for i in range(2)]
        pb = [es.enter_context(nc.psum_tensor(f"pb{i}", [128, 512], F32)) for i in range(8)]
        R_pb = [Res(f"pb{i}") for i in range(8)]

        def ppc(c0, n=1):
            return pp[:, c0:c0 + n]

        def mm_group(bank_ap, pairs, reads, writes):
            fw.deps(pe, reads, writes)
            n = len(pairs)
            for i, (l, r) in enumerate(pairs):
                ins = nc.tensor.matmul(bank_ap, l, r, start=(i == 0), stop=(i == n - 1))
            pe.count += 1
            ins.then_inc(pe.sem, 1)
            tok = (pe.sem, pe.count)
            fw.commit(tok, reads, writes)
            return tok

        rr = {"n": 0}

        fw.dma(sp, pp[:], pp_d, R_pp, writes=[R_pp])
        fw.dma(pool, pwb[:], pw_d.rearrange("l c p e -> p l c e"), R_pw, writes=[R_pw])
        fw.op(pool, lambda: nc.gpsimd.memset(idf[:], 1.0), writes=[R_c])
        fw.op(pool, lambda: nc.gpsimd.affine_select(idf[:], idf[:], pattern=[[-1, 128]], compare_op=ALU.is_equal,
                                                   fill=0.0, base=0, channel_multiplier=1), reads=[R_c], writes=[R_c])
        fw.op(dve, lambda: nc.vector.tensor_copy(idb[:], idf[:]), reads=[R_c], writes=[R_c])
        fw.op(dve, lambda: nc.vector.memset(onesb[:], 1.0), writes=[R_c])
        fw.op(dve, lambda: nc.vector.memset(msk[:], 0.0), writes=[R_c])
        fw.op(dve, lambda: nc.vector.memset(msk[:, :, 0:1], 1.0), reads=[R_c], writes=[R_c])
        fw.op(dve, lambda: nc.vector.memset(msk[64:128, 0, 1:2], 1.0), reads=[R_c], writes=[R_c])
        fw.op(dve, lambda: nc.vector.memset(msk[:, 1, 1:3], 1.0), reads=[R_c], writes=[R_c])
        fw.op(dve, lambda: nc.vector.memset(msk[64:128, 1, 3:4], 1.0), reads=[R_c], writes=[R_c])
        fw.op(dve, lambda: nc.vector.memset(cneg[:], -0.5), reads=[R_c], writes=[R_c])
        fw.op(dve, lambda: nc.vector.memset(mrow[:], 0.0), reads=[R_c], writes=[R_c])
        fw.op(dve, lambda: nc.vector.memset(mrow[:, 64:128], 1.0), reads=[R_c], writes=[R_c])
        fw.op(dve, lambda: nc.vector.memset(mcol[:], 0.0), reads=[R_c], writes=[R_c])
        fw.op(dve, lambda: nc.vector.memset(mcol[:, 0:64], -30000.0), reads=[R_c], writes=[R_c])
        fw.op(dve, lambda: nc.vector.memset(invw[0:64, 0:1], 0.5), reads=[R_c], writes=[R_c])
        fw.op(dve, lambda: nc.vector.memset(invw[64:128, 0:1], 0.25), reads=[R_c], writes=[R_c])
        fw.op(dve, lambda: nc.vector.memset(invw[0:64, 1:2], 0.125), reads=[R_c], writes=[R_c])
        fw.op(dve, lambda: nc.vector.memset(invw[64:128, 1:2], 0.0625), reads=[R_c], writes=[R_c])
        with ExitStack() as ph:
            ltmp = sb(ph, "ltmp", (128, 64), F32)
            R_lt = Res("ltmp")
            for l in range(DEPTH):
                lam_init = 0.8 - 0.6 * math.exp(-0.3 * l)
                for j in range(2):
                    c0 = PP_LAM + l * 256 + j * 128
                    fw.op(dve, lambda: nc.vector.tensor_tensor(ltmp[:], ppc(c0, 64), ppc(c0 + 64, 64), ALU.mult),
                          reads=[R_pp, R_lt], writes=[R_lt])
                    fw.op(dve, lambda: nc.vector.reduce_sum(lamt[:, l, 2 + j:3 + j], ltmp[:], axis=AX.X),
                          reads=[R_lt, R_c], writes=[R_c])
                fw.op(act, lambda: nc.scalar.activation(lamt[:, l, 2:4], lamt[:, l, 2:4], AF.Exp), reads=[R_c], writes=[R_c])
                fw.op(dve, lambda: nc.vector.tensor_tensor(lamt[:, l, 0:1], lamt[:, l, 2:3], lamt[:, l, 3:4], ALU.subtract),
                      reads=[R_c], writes=[R_c])
                fw.op(dve, lambda: nc.vector.tensor_scalar(lamt[:, l, 0:1], lamt[:, l, 0:1], lam_init, None, ALU.add),
                      reads=[R_c], writes=[R_c])
                fw.op(dve, lambda: nc.vector.tensor_scalar(lamt[:, l, 1:2], lamt[:, l, 0:1], -1.0, None, ALU.mult),
                      reads=[R_c], writes=[R_c])
                fw.op(dve, lambda: nc.vector.tensor_scalar(subg[:, l, :], ppc(PP_SUB + l * 128, 128), 1.0 - lam_init, None, ALU.mult),
                      reads=[R_pp, R_c], writes=[R_c])
            fw.barrier()

        def rms_tile(st, gi, t0, n, dst_fn, dst_res, bank, sq_eng="act"):
            sq, R_sq, rstd, R_rs = st["sq"], st["R_sq"], st["rstd"], st["R_rs"]
            xres = RX(t0, n)
            for dc in range(8):
                i = rr["n"] % 4
                rr["n"] += 1
                if sq_eng == "pool" or (sq_eng == "mix" and dc % 2 == 1):
                    fw.op(pool, lambda: nc.gpsimd.tensor_tensor(sq[:, i, 0:n], xT[:, dc, t0:t0 + n], xT[:, dc, t0:t0 + n], ALU.mult),
                          reads=RX(t0, n, [dc]), writes=[R_sq[i]])
                else:
                    fw.op(act, lambda: nc.scalar.activation(sq[:, i, 0:n], xT[:, dc, t0:t0 + n], AF.Square),
                          reads=RX(t0, n, [dc]), writes=[R_sq[i]])
                fw.op(pe, lambda: nc.tensor.matmul(pb[bank][:, 0:n], onesb[:], sq[:, i, 0:n], start=(dc == 0), stop=(dc == 7)),
                      reads=[R_sq[i], R_c], writes=[R_pb[bank]])
            j = st["rsi"] % st["nrs"]
            st["rsi"] += 1
            fw.op(act, lambda: nc.scalar.activation(rstd[:, j, 0:n], pb[bank][:, 0:n], AF.Ln, bias=st["epsb"][:, 0:1], scale=1.0 / D),
                  reads=[R_pb[bank]], writes=[R_rs[j]])
            fw.op(act, lambda: nc.scalar.activation(rstd[:, j, 0:n], rstd[:, j, 0:n], AF.Exp, scale=-0.5),
                  reads=[R_rs[j]], writes=[R_rs[j]])
            for dc in range(8):
                fw.op(dve, lambda: nc.vector.scalar_tensor_tensor(dst_fn(dc), xT[:, dc, t0:t0 + n], ppc(PP_G + gi * 8 + dc),
                                                                  rstd[:, j, 0:n], ALU.mult, ALU.mult),
                      reads=[R_rs[j], R_pp] + RX(t0, n, [dc]), writes=[dst_res])

        def norm_bufs(st_, nmax=512, nrs=2):
            d = {}
            d["sq"] = sb(st_, "sq", (128, 4, nmax), BF16)
            d["R_sq"] = [Res(f"sq{i}") for i in range(4)]
            d["rstd"] = sb(st_, "rstd", (128, nrs, nmax), F32)
            d["R_rs"] = [Res(f"rs{i}") for i in range(nrs)]
            d["nrs"] = nrs
            d["epsb"] = sb(st_, "epsb", (128, 1), F32)
            d["rsi"] = 0
            fw.op(dve, lambda: nc.vector.memset(d["epsb"][:], EPS), writes=[d["R_rs"][0]])
            return d

        TILES = [(0, 512), (512, 512), (1024, 512), (1536, 512), (2048, 32)]

        def load_cols(dst_ap, w_ap, l, c0, ncol, slot_res, nowait=False):
            src = w_ap[l, :, c0:c0 + ncol].rearrange("(dc p) c -> p dc c", p=128)
            fw.dma(pool, dst_ap, src, slot_res, writes=[slot_res], nowait=nowait)

        def load_rows(dst_ap, w_ap, l, r0, nr, slot_res):
            src = w_ap[l, r0:r0 + nr, :].rearrange("(fc p) n -> p fc n", p=128)
            fw.dma(pool, dst_ap, src, slot_res, writes=[slot_res])

        PARTS = [(0, 4), (512, 4), (1024, 4), (1536, 4), (2048, 4), (2560, 2)]

        def ffn_load(l, w, p):
            c0, nf = PARTS[p]
            s = p % 2
            load_cols(gu[s][:, 0, :, 0:nf * 128], wg_d[w], l, c0, nf * 128, R_gu[s])
            load_cols(gu[s][:, 1, :, 0:nf * 128], wu_d[w], l, c0, nf * 128, R_gu[s], nowait=True)
            load_rows(dn[:, s, 0:nf, :], wd_d[w], l, c0, nf * 128, R_dn[s])

        ffn_load(0, 0, 0)
        ffn_load(0, 0, 1)

        def xload_gen(st_):
            xst = sb(st_, "xst", (128, 2, D), F32)
            R_xst = [Res("xst0"), Res("xst1")]
            for s in range(NSUB):
                n = 128 if s < 16 else DEC
                src = xp_d[s * 128:(s + 1) * 128, :] if s < 16 else xs_d
                k = s % 2
                fw.dma(sp, xst[0:n, k, :], src, R_xst[k], writes=[R_xst[k]])
                for hb in range(2):
                    bank = (7, 5)[hb]
                    fw.deps(pe, [R_xst[k], R_c], [R_pb[bank]])
                    for q in range(4):
                        dc = hb * 4 + q
                        ins = nc.tensor.transpose(pb[bank][:, q * 128:q * 128 + n], xst[0:n, k, dc * 128:(dc + 1) * 128], idf[0:n, 0:n])
                    pe.count += 1
                    ins.then_inc(pe.sem, 1)
                    fw.commit((pe.sem, pe.count), [R_xst[k], R_c], [R_pb[bank]])
                    src_ap = pb[bank][:].rearrange("p (q t) -> p q t", q=4)[:, :, 0:n]
                    dst_ap = xT[:, hb * 4:hb * 4 + 4, s * 128:s * 128 + n]
                    if hb == 0:
                        fw.op(act, lambda: nc.scalar.copy(dst_ap, src_ap), reads=[R_pb[bank]],
                              writes=[R_x[dc][s] for dc in range(hb * 4, hb * 4 + 4)])
                    else:
                        fw.op(dve, lambda: nc.vector.tensor_copy(dst_ap, src_ap), reads=[R_pb[bank]],
                              writes=[R_x[dc][s] for dc in range(hb * 4, hb * 4 + 4)])
                yield s

        FT = [(i * 416, 416) for i in range(5)]

        def ffn(l, w, gi, preloaded, prefetch=None, xload=False):
            TILES = FT
            with ExitStack() as ph:
                hT = sb(ph, "hT", (128, 8, NT), BF16)
                R_h = [Res(f"h{t}") for t in range(5)]
                aT = sb(ph, "aT", (128, 4, NT), BF16)
                R_a = [[Res(f"a{fc}_{t}") for t in range(5)] for fc in range(4)]
                sg = sb(ph, "sg", (128, 2, 512), F32)
                R_sg = [Res("sg0"), Res("sg1")]
                nb = norm_bufs(ph)
                if preloaded < 1:
                    ffn_load(l, w, 0)
                if preloaded < 2:
                    ffn_load(l, w, 1)

                def ffn_rms(ti_):
                    t0_, n_ = TILES[ti_]
                    rms_tile(nb, gi, t0_, n_, lambda dc: hT[:, dc, t0_:t0_ + n_], R_h[ti_], 6, sq_eng="mix")

                xg = xload_gen(ph) if xload else None

                def need_tokens(upto):
                    if xg is not None:
                        while xstate["s"] < min(NSUB - 1, (upto - 1) // 128):
                            xstate["s"] = next(xg)

                xstate = {"s": -1}
                need_tokens(TILES[0][0] + TILES[0][1])
                ffn_rms(0)
                cnt = 0
                for p, (c0, nf) in enumerate(PARTS):
                    s = p % 2
                    for ti, (t0, n) in enumerate(TILES):
                        if p == 0 and ti + 1 < len(TILES):
                            need_tokens(TILES[ti + 1][0] + TILES[ti + 1][1])
                            ffn_rms(ti + 1)
                        for fc in range(nf):
                            gb = cnt % 2
                            ub = 2 + cnt % 2
                            cnt += 1
                            mm_group(pb[gb][:, 0:n], [(gu[s][:, 0, dc, fc * 128:(fc + 1) * 128], hT[:, dc, t0:t0 + n]) for dc in range(8)],
                                     [R_gu[s], R_h[ti]], [R_pb[gb]])
                            mm_group(pb[ub][:, 0:n], [(gu[s][:, 1, dc, fc * 128:(fc + 1) * 128], hT[:, dc, t0:t0 + n]) for dc in range(8)],
                                     [R_gu[s], R_h[ti]], [R_pb[ub]])
                            k = gb
                            fw.op(act, lambda: nc.scalar.activation(sg[:, k, 0:n], pb[gb][:, 0:n], AF.Silu),
                                  reads=[R_pb[gb]], writes=[R_sg[k]])
                            fw.op(dve, lambda: nc.vector.tensor_tensor(aT[:, fc, t0:t0 + n], sg[:, k, 0:n], pb[ub][:, 0:n], ALU.mult),
                                  reads=[R_sg[k], R_pb[ub]], writes=[R_a[fc][ti]])
                    dcnt = 0
                    for ti, (t0, n) in enumerate(TILES):
                        for dco in range(8):
                            db = 4 + dcnt % 2
                            dcnt += 1
                            mm_group(pb[db][:, 0:n], [(dn[:, s, fc, dco * 128:(dco + 1) * 128], aT[:, fc, t0:t0 + n]) for fc in range(nf)],
                                     [R_dn[s]] + [R_a[fc][ti] for fc in range(nf)], [R_pb[db]])
                            xr = RX(t0, n, [dco])
                            fw.op(dve, lambda: nc.vector.scalar_tensor_tensor(xT[:, dco, t0:t0 + n], pb[db][:, 0:n], 0.5,
                                                                              xT[:, dco, t0:t0 + n], ALU.mult, ALU.add),
                                  reads=[R_pb[db]] + xr, writes=xr)
                    if p + 2 < len(PARTS):
                        ffn_load(l, w, p + 2)
                    elif p == len(PARTS) - 2 and prefetch is not None:
                        prefetch()
                fw.barrier()

        def rope(src_bank, n, s, dst_ap, dst_res, tmp, R_tmp):
            sv = pb[src_bank][0:n, :].rearrange("p (g h d) -> p g h d", g=8, h=2)
            dv = dst_ap.rearrange("p (g h d) -> p g h d", g=8, h=2)
            cosb = pp[0:n, PP_COS + s * 32:PP_COS + s * 32 + 32].unsqueeze(1).to_broadcast([n, 8, 32])
            sinb = pp[0:n, PP_SIN + s * 32:PP_SIN + s * 32 + 32].unsqueeze(1).to_broadcast([n, 8, 32])
            tv = [tmp[0:n, i, 0:256].rearrange("p (g d) -> p g d", g=8) for i in range(4)]
            rd = [R_pb[src_bank], R_pp]
            fw.op(dve, lambda: nc.vector.tensor_tensor(tv[0], sv[:, :, 0, :], cosb, ALU.mult), reads=rd, writes=[R_tmp[0]])
            fw.op(dve, lambda: nc.vector.tensor_tensor(tv[1], sv[:, :, 1, :], sinb, ALU.mult), reads=rd, writes=[R_tmp[1]])
            fw.op(dve, lambda: nc.vector.tensor_tensor(tv[2], sv[:, :, 1, :], cosb, ALU.mult), reads=rd, writes=[R_tmp[2]])
            fw.op(dve, lambda: nc.vector.tensor_tensor(tv[3], sv[:, :, 0, :], sinb, ALU.mult), reads=rd, writes=[R_tmp[3]])
            fw.op(pool, lambda: nc.gpsimd.tensor_tensor(dv[:, :, 0, :], tv[0], tv[1], ALU.subtract),
                  reads=[R_tmp[0], R_tmp[1]], writes=[dst_res])
            fw.op(pool, lambda: nc.gpsimd.tensor_tensor(dv[:, :, 1, :], tv[2], tv[3], ALU.add),
                  reads=[R_tmp[2], R_tmp[3]], writes=[dst_res])

        def transpose4(src_fn, n, dst_ap, src_res, dst_res, bank, eng):
            pbb = pb[bank][:].bitcast(BF16)
            fw.deps(pe, [src_res, R_c], [R_pb[bank]])
            for h in range(4):
                ins = nc.tensor.transpose(pbb[:, h * 128:h * 128 + n], src_fn(h), idb[0:n, 0:n])
            pe.count += 1
            ins.then_inc(pe.sem, 1)
            fw.commit((pe.sem, pe.count), [src_res, R_c], [R_pb[bank]])
            src_ap = pbb[:, 0:512].rearrange("p (h t) -> p h t", h=4)[:, :, 0:n]
            if eng is act:
                fw.op(act, lambda: nc.scalar.copy(dst_ap, src_ap), reads=[R_pb[bank]], writes=[dst_res])
            else:
                fw.op(dve, lambda: nc.vector.tensor_copy(dst_ap, src_ap), reads=[R_pb[bank]], writes=[dst_res])

        def mixer_preload(l):
            load_cols(gu[0][:, 0], win_d, l, 512, 512, R_gu[0])
            load_cols(gu[0][:, 1], win_d, l, 1024, 512, R_gu[0], nowait=True)

        def mixer(l):
            gi = l * 3 + 1
            with ExitStack() as mph:
                kT = sb(mph, "kT", (128, NH, NT), BF16)
                Vx = sb(mph, "Vx", (128, NSUB, NH, 129), BF16)
                R_kT = [Res(f"kT{s}") for s in range(NSUB)]
                R_V = [Res(f"V{s}") for s in range(NSUB)]
                with ExitStack() as ph:
                    nb = norm_bufs(ph)
                    hTt = sb(ph, "hTt", (128, 2, 8, 512), BF16)
                    R_ht = [Res("ht0"), Res("ht1")]
                    kst = sb(ph, "kst", (128, 2, 512), F32)
                    R_kst = [Res("kst0"), Res("kst1")]
                    vst = sb(ph, "vst", (128, 2, 512), F32)
                    R_vst = [Res("vst0"), Res("vst1")]
                    kbb = sb(ph, "kbb", (128, 2, 512), BF16)
                    R_kb = [Res("kb0"), Res("kb1")]
                    rtmp = sb(ph, "rtmp", (128, 8, 256), F32)
                    R_rt = [Res(f"rt{i}") for i in range(8)]
                    wk = gu[0][:, 0]
                    wv = gu[0][:, 1]
                    load_cols(gu[1][:, 0], win_d, l, 2048, 512, R_gu[1])
                    load_rows(dn[:].rearrange("p a f n -> p (a f) n"), wout_d, l, 0, D, R_dn[0])
                    fw.op(pool, lambda: nc.gpsimd.memset(Vx[:, :, :, 128:129], 1.0), writes=R_V)
                    pend = []

                    def flush_one():
                        m_, c_, s_ = pend.pop(0)
                        transpose4(lambda h: kbb[0:m_, c_, h * 128:(h + 1) * 128], m_, kT[:, :, s_ * 128:s_ * 128 + m_],
                                   R_kb[c_], R_kT[s_], 4 + c_, act)

                    def pe_warm(nmm):
                        fw.deps(pe, [R_gu[0], R_c], [R_pb[7]])
                        for _ in range(nmm):
                            ins = nc.tensor.matmul(pb[7][:, :], onesb[:], wk[:, 0, :], start=True, stop=True)
                        pe.count += 1
                        ins.then_inc(pe.sem, 1)
                        fw.commit((pe.sem, pe.count), [R_gu[0], R_c], [R_pb[7]])

                    def m1_rms(ti_):
                        t0_, n_ = TILES[ti_]
                        hs_ = ti_ % 2
                        if n_ >= 512:
                            pe_warm(20)
                        rms_tile(nb, gi, t0_, n_, lambda dc: hTt[:, hs_, dc, 0:n_], R_ht[hs_], 6, sq_eng="mix")

                    m1_rms(0)
                    for ti, (t0, n) in enumerate(TILES):
                        hs = ti % 2
                        if ti + 1 < len(TILES):
                            m1_rms(ti + 1)
                        for sub in range((n + 127) // 128):
                            s = t0 // 128 + sub
                            m = min(128, n - sub * 128)
                            c = s % 2
                            kb_, vb_ = c, 2 + c
                            mm_group(pb[kb_][0:m, :], [(hTt[:, hs, dc, sub * 128:sub * 128 + m], wk[:, dc, :]) for dc in range(8)],
                                     [R_ht[hs], R_gu[0]], [R_pb[kb_]])
                            mm_group(pb[vb_][0:m, :], [(hTt[:, hs, dc, sub * 128:sub * 128 + m], wv[:, dc, :]) for dc in range(8)],
                                     [R_ht[hs], R_gu[0]], [R_pb[vb_]])
                            rope(kb_, m, s, kst[0:m, c, :], R_kst[c], rtmp[:, 4 * c:4 * c + 4], R_rt[4 * c:4 * c + 4])
                            kdst = kp_d[l, s * 128:s * 128 + m, :] if s < 16 else ks_d[l]
                            fw.dma(sp, kdst, kst[0:m, c, :], R_kst[c], reads=[R_kst[c]])
                            fw.op(act, lambda: nc.scalar.copy(kbb[0:m, c, :], kst[0:m, c, :]), reads=[R_kst[c]], writes=[R_kb[c]])
                            fw.op(act, lambda: nc.scalar.copy(vst[0:m, c, :], pb[vb_][0:m, :]), reads=[R_pb[vb_]], writes=[R_vst[c]])
                            vdst = vp_d[l, s * 128:s * 128 + m, :] if s < 16 else vs_d[l]
                            fw.dma(sp, vdst, vst[0:m, c, :], R_vst[c], reads=[R_vst[c]])
                            fw.op(act, lambda: nc.scalar.copy(Vx[0:m, s, :, 0:128], vst[0:m, c, :].rearrange("p (h e) -> p h e", h=4)),
                                  reads=[R_vst[c]], writes=[R_V[s]])
                            pend.append((m, c, s))
                            if len(pend) > 1:
                                flush_one()
                    while pend:
                        flush_one()
                    fw.barrier()
                if stage(2 + 10 * l):
                    return
                wq = gu[0][:, 0]
                wuh = gu[0][:, 1]
                wbc = gu[1][:, 0]
                wo = dn[:].rearrange("p a f n -> p (a f) n")
                load_cols(wq, win_d, l, 0, 512, R_gu[0])
                load_cols(wuh, win_d, l, 1536, 512, R_gu[0], nowait=True)
                R_wo = [R_dn[0]]
                M2T = [(i * 256, 256) for i in range(8)] + [(2048, 32)]
                with ExitStack() as ph:
                    nb = norm_bufs(ph, 256, 1)
                    hT2v = gu[1][:, 1]
                    R_ht = [Res("h20"), Res("h21")]
                    qbb = sb(ph, "qbb", (128, 2, 512), BF16)
                    R_qb = [Res("qb0"), Res("qb1")]
                    rtmp = sb(ph, "rtmp2", (128, 4, 258), F32)
                    R_rt = [Res(f"rt2{i}") for i in range(4)]
                    qT = sb(ph, "qT", (128, 2, NH, 256), BF16)
                    R_qT = [[Res(f"qT{a_}{b_}") for b_ in range(2)] for a_ in range(2)]
                    PT = sb(ph, "PT", (128, 2, 2, 2, 256), BF16)
                    R_PT = [Res(f"PT{i}") for i in range(2)]
                    mixT = sb(ph, "mixT", (128, 8, 256), BF16)
                    R_mx = [Res(f"mx{c}") for c in range(8)]
                    ubuf = sb(ph, "ubuf", (128, 3, 2, 272), F32)
                    R_ub = [Res(f"ub{i}") for i in range(3)]
                    cvb = sb(ph, "cvb", (128, 2, 258), F32)
                    R_cvb = Res("cvb")
                    cvt = sb(ph, "cvt", (128, 2, 256), F32)
                    R_cvt = [Res("cvt0"), Res("cvt1")]
                    dbf = sb(ph, "dbf", (128, 2, 256), BF16)
                    R_dbf = [Res("dbf0"), Res("dbf1")]
                    oev = sb(ph, "oev", (128, 2, 2, 129), F32)
                    R_oev = [Res("oev0"), Res("oev1")]
                    osb = sb(ph, "osb", (128, 2, 128), F32)
                    R_os = [Res("os0"), Res("os1")]
                    pend_fin = []
                    sml = sb(ph, "sml", (128, 2, 8), F32)
                    R_sml = [Res("sml0"), Res("sml1")]
                    yat = sb(ph, "yat", (128, 2, 512), BF16)
                    R_yat = [Res("yat0"), Res("yat1")]
                    kcs = sb(ph, "kcs", (128, 2, 2, 512), BF16)
                    R_kcs = [Res("kcs0"), Res("kcs1")]
                    kcT = sb(ph, "kcT", (128, 2, NH, 256), BF16)
                    R_kcT = [Res("kcT0"), Res("kcT1")]
                    vcs = sb(ph, "vcs", (128, 2, 2, 512), BF16)
                    R_vcs = [Res("vcs0"), Res("vcs1")]
                    PTs = PT[:].rearrange("p s e c q -> p s (e c q)")[:, :, 0:512].rearrange("p s (c h j q) -> p s c h j q", c=2, h=NH, j=2)
                    R_PTs = [R_PT[0], R_PT[1]]
                    oacc = rtmp[0:DEC].rearrange("p a b -> p (a b)").rearrange("p (g e) -> p g e", g=8)
                    R_oacc = R_rt

                    def hTt_ap(hs, dc, a_, b_):
                        return hT2v[:, dc, hs * 256 + a_:hs * 256 + b_]

                    fw.op(pool, lambda: nc.gpsimd.memset(ubuf[:], 0.0), writes=R_ub)
                    fw.op(pool, lambda: nc.gpsimd.memset(cvb[:], 0.0), writes=[R_cvb])
                    NCB = 16

                    def load_ck(cb):
                        sl = cb % 2
                        fw.dma(pool, kcs[:, sl], ck_d[l, cb * 256:(cb + 1) * 256, :].rearrange("(j p) f -> p j f", p=128),
                               R_kcs[sl], writes=[R_kcs[sl]])

                    def load_cv(cb):
                        sl = cb % 2
                        fw.dma(pool, vcs[:, sl], cv_d[l, cb * 256:(cb + 1) * 256, :].rearrange("(j p) f -> p j f", p=128),
                               R_vcs[sl], writes=[R_vcs[sl]])

                    if l == 0 and os.environ.get("KDEBUG"):
                        print("M2 sbuf bytes remaining:", nc.sbuf_bytes_remaining)
                    load_ck(0)
                    load_cv(0)
                    load_ck(1)
                    load_cv(1)
                    misc = {"n": 0}

                    def mbank():
                        if misc.get("mode") == "in":
                            return 7
                        b = (7, 0, 1, 6)[misc["n"] % 4]
                        misc["n"] += 1
                        return b

                    ptc = {"n": 0, "e": 0}

                    class BV:
                        def __init__(self, ap, res):
                            self.ap = ap
                            self.res = res if isinstance(res, list) else [res]

                        def __getitem__(self, k):
                            return self.ap[k]

                    def flush_fin():
                        while pend_fin:
                            pend_fin.pop(0)()

                    def epilogue(m, h, src1, src2, yslot):
                        k = ptc["e"] % 2
                        ptc["e"] += 1
                        r1, r2 = src1.res, src2.res
                        ob_ = osb[0:m, k, :]
                        fw.op(dve, lambda: nc.vector.reciprocal(sml[0:m, k, 0:1], src1[0:m, 128:129]), reads=r1, writes=[R_sml[k]])
                        fw.op(dve, lambda: nc.vector.reciprocal(sml[0:m, k, 1:2], src2[0:m, 128:129]), reads=r2 + [R_sml[k]], writes=[R_sml[k]])
                        fw.op(dve, lambda: nc.vector.tensor_tensor(sml[0:m, k, 1:2], sml[0:m, k, 1:2], lamt[0:m, l, 1:2], ALU.mult),
                              reads=[R_sml[k], R_c], writes=[R_sml[k]])
                        fw.op(dve, lambda: nc.vector.tensor_scalar(ob_, src1[0:m, 0:128], sml[0:m, k, 0:1], None, ALU.mult),
                              reads=r1 + [R_sml[k]], writes=[R_os[k]])
                        fw.op(dve, lambda: nc.vector.scalar_tensor_tensor(ob_, src2[0:m, 0:128], sml[0:m, k, 1:2], ob_,
                                                                          ALU.mult, ALU.add),
                              reads=r2 + [R_sml[k], R_os[k]], writes=[R_os[k]])
                        fw.op(dve, lambda: nc.vector.tensor_tensor(src1[0:m, 0:128], ob_, ob_, ALU.mult),
                              reads=[R_os[k]] + r1, writes=r1)
                        fw.op(dve, lambda: nc.vector.reduce_sum(sml[0:m, k, 2:3], src1[0:m, 0:128], axis=AX.X),
                              reads=r1 + [R_sml[k]], writes=[R_sml[k]])
                        flush_fin()
                        fw.op(dve, lambda: nc.vector.tensor_scalar(sml[0:m, k, 3:4], sml[0:m, k, 2:3], 1.0 / 128, EPS, ALU.mult, ALU.add),
                              reads=[R_sml[k]], writes=[R_sml[k]])
                        fw.op(pool, lambda: nc.gpsimd.tensor_tensor(sml[0:m, k, 3:4], sml[0:m, k, 3:4], cneg[0:m, 0:1], ALU.pow),
                              reads=[R_sml[k], R_c], writes=[R_sml[k]])

                        def fin():
                            fw.op(dve, lambda: nc.vector.scalar_tensor_tensor(yat[0:m, yslot, h * 128:(h + 1) * 128], ob_, sml[0:m, k, 3:4],
                                                                              subg[0:m, l, :], ALU.mult, ALU.mult),
                                  reads=[R_os[k], R_sml[k], R_c], writes=[R_yat[yslot]])
                        pend_fin.append(fin)

                    def tinfo(ti):
                        t0, n = M2T[ti]
                        return t0, n, ti % 2, (t0 >= SEQ), (n + 127) // 128

                    def qchain(ti):
                        t0, n, hs, is_s, nsub = tinfo(ti)
                        sq, R_sq, rstd, R_rs = nb["sq"], nb["R_sq"], nb["rstd"], nb["R_rs"]
                        bank = mbank()

                        def square(dc):
                            fw.op(act, lambda: nc.scalar.activation(sq[:, dc % 4, 0:n], xT[:, dc, t0:t0 + n], AF.Square),
                                  reads=RX(t0, n, [dc]), writes=[R_sq[dc % 4]])

                        def ssmm(dc):
                            fw.op(pe, lambda: nc.tensor.matmul(pb[bank][:, 0:n], onesb[:], sq[:, dc % 4, 0:n], start=(dc == 0), stop=(dc == 7)),
                                  reads=[R_sq[dc % 4], R_c], writes=[R_pb[bank]])
                        for dc in range(4):
                            square(dc)
                        yield
                        for dc in range(4):
                            ssmm(dc)
                            square(4 + dc)
                        for dc in range(4, 8):
                            ssmm(dc)
                        j = 0
                        fw.op(act, lambda: nc.scalar.activation(rstd[:, j, 0:n], pb[bank][:, 0:n], AF.Ln, bias=nb["epsb"][:, 0:1], scale=1.0 / D),
                              reads=[R_pb[bank]], writes=[R_rs[j]])
                        fw.op(act, lambda: nc.scalar.activation(rstd[:, j, 0:n], rstd[:, j, 0:n], AF.Exp, scale=-0.5),
                              reads=[R_rs[j]], writes=[R_rs[j]])
                        for dc in range(8):
                            fw.op(dve, lambda: nc.vector.scalar_tensor_tensor(hTt_ap(hs, dc, 0, n), xT[:, dc, t0:t0 + n], ppc(PP_G + gi * 8 + dc),
                                                                              rstd[:, j, 0:n], ALU.mult, ALU.mult),
                                  reads=[R_rs[j], R_pp] + RX(t0, n, [dc]), writes=[R_ht[hs]])
                        yield
                        for sub in range(nsub):
                            s_ = t0 // 128 + sub
                            m = min(128, n - sub * 128)
                            c = s_ % 2
                            qb_ = mbank()
                            mm_group(pb[qb_][0:m, :], [(hTt_ap(hs, dc, sub * 128, sub * 128 + m), wq[:, dc, :]) for dc in range(8)],
                                     [R_ht[hs], R_gu[0]], [R_pb[qb_]])
                            rope(qb_, m, s_, qbb[0:m, c, :], R_qb[c], rtmp, R_rt)
                            yield
                        for sub in range(nsub):
                            s_ = t0 // 128 + sub
                            m = min(128, n - sub * 128)
                            c = s_ % 2
                            transpose4(lambda h: qbb[0:m, c, h * 128:(h + 1) * 128], m, qT[:, hs, :, sub * 128:sub * 128 + m],
                                       R_qb[c], R_qT[hs][sub], mbank(), act)
                            yield

                    def fproj(ti, wt, col, bank):
                        t0, n, hs, is_s, nsub = tinfo(ti)
                        mm_group(pb[bank][:, 0:n], [(wt[:, dc, col * 128:(col + 1) * 128], hTt_ap(hs, dc, 0, n)) for dc in range(8)],
                                 [R_ht[hs], R_gu[0], R_gu[1]], [R_pb[bank]])

                    def early(ti):
                        t0, n, hs, is_s, nsub = tinfo(ti)
                        if ti == 0:
                            pass
                        elif is_s:
                            with nc.allow_non_contiguous_dma(reason="tiny state halo loads"):
                                for ch in range(2):
                                    fw.dma(sp, ubuf[:, 0, ch, 1:16], stp_d[l, :, ch * 128:(ch + 1) * 128].rearrange("t p -> p t"), R_ub[0],
                                           writes=[R_ub[0]], nowait=(ch > 0))
                                    fw.dma(sp, cvb[:, ch, 0:2], stc_d[l, :, ch * 128:(ch + 1) * 128].rearrange("t p -> p t"), R_cvb,
                                           writes=[R_cvb], nowait=(ch > 0))
                        else:
                            fw.op(pool, lambda: nc.gpsimd.tensor_copy(ubuf[:, 0, :, 0:16], ubuf[:, 0, :, 256:272]), reads=[R_ub[0]], writes=[R_ub[0]])
                            fw.op(pool, lambda: nc.gpsimd.tensor_copy(cvb[:, :, 0:2], cvb[:, :, 256:258]), reads=[R_cvb], writes=[R_cvb])
                        for ch in range(2):
                            b = mbank()
                            fproj(ti, wuh, ch, b)
                            fw.op(dve, lambda: nc.vector.tensor_copy(ubuf[:, 0, ch, 16:16 + n], pb[b][:, 0:n]), reads=[R_pb[b]], writes=[R_ub[0]])
                        for ch in range(2):
                            b = mbank()
                            fproj(ti, wuh, 2 + ch, b)
                            fw.op(dve, lambda: nc.vector.tensor_copy(cvt[:, ch, 0:n], pb[b][:, 0:n]), reads=[R_pb[b]], writes=[R_cvt[ch]])
                        for ch in range(2):
                            b = mbank()
                            fproj(ti, wbc, 2 + ch, b)
                            fw.op(dve, lambda: nc.vector.tensor_tensor(cvb[:, ch, 2:2 + n], pb[b][:, 0:n], cvt[:, ch, 0:n], ALU.mult),
                                  reads=[R_pb[b], R_cvt[ch]], writes=[R_cvb])
                        if t0 + n == SEQ or is_s:
                            pd = pls_d if is_s else plp_d
                            cd = cvs_d if is_s else cvp_d
                            with nc.allow_non_contiguous_dma(reason="tiny state outputs"):
                                for ch in range(2):
                                    fw.dma(sp, pd[l, :, ch * 128:(ch + 1) * 128].rearrange("t p -> p t"), ubuf[:, 0, ch, 16 + n - 15:16 + n],
                                           R_ub[0], reads=[R_ub[0]])
                                    fw.dma(sp, cd[l, :, ch * 128:(ch + 1) * 128].rearrange("t p -> p t"), cvb[:, ch, n:n + 2],
                                           R_cvb, reads=[R_cvb])
                        for ch in range(2):
                            cw = PP_CW + l * 6 + ch * 3
                            fw.op(dve, lambda: nc.vector.tensor_scalar(cvt[:, ch, 0:n], cvb[:, ch, 0:n], ppc(cw), None, ALU.mult),
                                  reads=[R_cvb, R_pp], writes=[R_cvt[ch]])
                            fw.op(dve, lambda: nc.vector.scalar_tensor_tensor(cvt[:, ch, 0:n], cvb[:, ch, 1:1 + n], ppc(cw + 1), cvt[:, ch, 0:n],
                                                                              ALU.mult, ALU.add),
                                  reads=[R_cvb, R_pp, R_cvt[ch]], writes=[R_cvt[ch]])
                            fw.op(dve, lambda: nc.vector.scalar_tensor_tensor(cvt[:, ch, 0:n], cvb[:, ch, 2:2 + n], ppc(cw + 2), cvt[:, ch, 0:n],
                                                                              ALU.mult, ALU.add),
                                  reads=[R_cvb, R_pp, R_cvt[ch]], writes=[R_cvt[ch]])
                        W = 16 + n
                        cur = 0
                        for step in range(4):
                            sh = 1 << step
                            nxt = 1 + (step % 2)
                            lo = 2 * sh
                            for ch in range(2):
                                if ch == 0 and step >= 2:
                                    continue
                                fw.op(dve, lambda: nc.vector.scalar_tensor_tensor(ubuf[:, nxt, ch, lo:W], ubuf[:, cur, ch, lo - sh:W - sh],
                                                                                  msk[:, ch, step:step + 1], ubuf[:, cur, ch, lo:W],
                                                                                  ALU.mult, ALU.add),
                                      reads=[R_ub[cur], R_c], writes=[R_ub[nxt]])
                            cur = nxt
                        for ch in range(2):
                            fin = 2
                            fw.op(dve, lambda: nc.vector.scalar_tensor_tensor(dbf[:, ch, 0:n], ubuf[:, fin, ch, 16:16 + n], invw[:, ch:ch + 1],
                                                                              ubuf[:, 0, ch, 16:16 + n], ALU.mult, ALU.subtract),
                                  reads=[R_ub[fin], R_ub[0], R_c], writes=[R_dbf[ch]])
                            if ti == 0:
                                fw.op(pool, lambda: nc.gpsimd.tensor_tensor(ubuf[:, 1, ch, 0:16], ubuf[:, fin, ch, 16:32], ppc(PP_IC + ch * 16, 16), ALU.mult),
                                      reads=[R_ub[fin], R_pp, R_ub[1]], writes=[R_ub[1]])
                                fw.op(pool, lambda: nc.gpsimd.tensor_tensor(dbf[:, ch, 0:16], ubuf[:, 1, ch, 0:16], ubuf[:, 0, ch, 16:32], ALU.subtract),
                                      reads=[R_ub[1], R_ub[0], R_dbf[ch]], writes=[R_dbf[ch]])

                    def attention_prompt(ti, steps):
                        t0, n, hs, is_s, nsub = tinfo(ti)
                        g = ti
                        iters = [(h, kp) for h in range(NH) for kp in range(g + 1)]

                        def emit_S(i):
                            h, kp = iters[i]
                            banks = ((0, 1, 6)[(2 * i) % 3], (0, 1, 6)[(2 * i + 1) % 3])
                            slot = i % 2
                            rds = [R_kT[2 * kp], R_kT[2 * kp + 1], R_qT[hs][0], R_qT[hs][1], R_c]
                            for c in range(2):
                                sbk = banks[c]
                                fw.deps(pe, rds, [R_pb[sbk]])
                                for e in range(2):
                                    kt = 2 * kp + e
                                    c0 = 128 if kt == 2 * g + 1 else 0
                                    diag = (kt >= 2 * g)
                                    oap = pb[sbk][:, e * 256 + c0:e * 256 + 256]
                                    ins = nc.tensor.matmul(oap, kT[c * 64:(c + 1) * 64, h, kt * 128:(kt + 1) * 128],
                                                           qT[c * 64:(c + 1) * 64, hs, h, c0:256], start=True, stop=not diag)
                                    if diag:
                                        ins = nc.tensor.matmul(oap, mrow[c * 64:c * 64 + 1, :], mcol[c * 64:c * 64 + 1, 0:256 - c0],
                                                               start=False, stop=True)
                                    if c0 > 0:
                                        ins = nc.tensor.matmul(pb[sbk][:, e * 256:e * 256 + c0], mrow[c * 64:c * 64 + 1, :],
                                                               mcol[c * 64:c * 64 + 1, 64:64 + c0], start=True, stop=True)
                                pe.count += 1
                                ins.then_inc(pe.sem, 1)
                                fw.commit((pe.sem, pe.count), rds, [R_pb[sbk]])
                            for c in range(2):
                                sbk = banks[c]
                                fw.op(act, lambda: nc.scalar.activation(PT[:, slot, :, c, :], pb[sbk][:].rearrange("p (e q) -> p e q", e=2),
                                                                        AF.Exp, scale=SCALE),
                                      reads=[R_pb[sbk]], writes=[R_PT[slot]])

                        def emit_PV(i):
                            h, kp = iters[i]
                            slot = i % 2
                            closed = []
                            for e in range(2):
                                kt = 2 * kp + e
                                for qs in range(2):
                                    if kt > 2 * g + qs:
                                        continue
                                    for c in range(2):
                                        ob = 2 + qs * 2 + c
                                        first = (kt == 0)
                                        last = (kt == 2 * g + qs)
                                        fw.deps(pe, [R_PT[slot], R_V[kt]], [R_pb[ob]] if first else [])
                                        ins = nc.tensor.matmul(pb[ob][:, 0:129], PT[:, slot, e, c, qs * 128:(qs + 1) * 128], Vx[:, kt, h, :],
                                                               start=first, stop=last)
                                        pe.count += 1
                                        ins.then_inc(pe.sem, 1)
                                        fw.commit((pe.sem, pe.count), [R_PT[slot], R_V[kt]], [R_pb[ob]])
                                    if kt == 2 * g + qs:
                                        k = ptc["n"] % 2
                                        ptc["n"] += 1
                                        closed.append((qs, k))
                                        for c in range(2):
                                            ob = 2 + qs * 2 + c
                                            if c == 0:
                                                fw.op(dve, lambda: nc.vector.tensor_copy(oev[:, k, c, :], pb[ob][:, 0:129]), reads=[R_pb[ob]], writes=[R_oev[k]])
                                            else:
                                                fw.op(act, lambda: nc.scalar.copy(oev[:, k, c, :], pb[ob][:, 0:129]), reads=[R_pb[ob]], writes=[R_oev[k]])
                            for qs, k in closed:
                                epilogue(128, h, BV(oev[:, k, 0, :], R_oev[k]), BV(oev[:, k, 1, :], R_oev[k]), qs)

                        nsteps = 6
                        stride = max(1, len(iters) // (nsteps + 1))
                        misc["mode"] = "in"
                        emit_S(0)
                        for i in range(len(iters)):
                            if i + 1 < len(iters):
                                emit_S(i + 1)
                            emit_PV(i)
                            if steps is not None and i % stride == stride - 1:
                                next(steps, None)
                        if steps is not None:
                            for _ in steps:
                                pass
                        misc["mode"] = "out"
                        wb = mbank()
                        fw.deps(pe, [R_gu[0], R_c], [R_pb[wb]])
                        for _ in range(12):
                            ins = nc.tensor.matmul(pb[wb][:, :], onesb[:], wq[:, 0, :], start=True, stop=True)
                        pe.count += 1
                        ins.then_inc(pe.sem, 1)
                        fw.commit((pe.sem, pe.count), [R_gu[0], R_c], [R_pb[wb]])
                        flush_fin()
                        for qs in range(2):
                            transpose4(lambda h: yat[:, qs, h * 128:(h + 1) * 128], 128, mixT[:, 0:4, qs * 128:(qs + 1) * 128],
                                       R_yat[qs], R_mx[qs], mbank(), dve)
                        return [R_mx[0], R_mx[1]]

                    def attention_sample(ti):
                        t0, n, hs, is_s, nsub = tinfo(ti)
                        fw.op(pool, lambda: nc.gpsimd.memset(oacc, 0.0), writes=R_oacc)

                        def blk(b_):
                            cache = b_ < NCB
                            return cache, b_ % 2, (2 if cache else 1), (128 if cache else DEC)

                        def emit_T(b_):
                            cache, sl, nj, kp_ = blk(b_)
                            if not cache:
                                return
                            pbb = pb[5][:].bitcast(BF16)
                            fw.deps(pe, [R_kcs[sl], R_c], [R_pb[5]])
                            for j in range(2):
                                for h in range(NH):
                                    ins = nc.tensor.transpose(pbb[:, (h * 2 + j) * 128:(h * 2 + j + 1) * 128], kcs[:, sl, j, h * 128:(h + 1) * 128], idb[:])
                            pe.count += 1
                            ins.then_inc(pe.sem, 1)
                            fw.commit((pe.sem, pe.count), [R_kcs[sl], R_c], [R_pb[5]])
                            fw.op(dve, lambda: nc.vector.tensor_copy(kcT[:, sl], pbb[:].rearrange("p (h k) -> p h k", h=NH)),
                                  reads=[R_pb[5]], writes=[R_kcT[sl]])

                        def emit_S(b_):
                            cache, sl, nj, kp_ = blk(b_)
                            kres = [R_kcT[sl]] if cache else [R_kT[16]]
                            banks = (0, 1) if b_ % 2 == 0 else (6, 7)
                            for c in range(2):
                                sbk = banks[c]
                                Sv = pb[sbk][:, 0:256].rearrange("p (h j q) -> p h j q", h=4, j=2)
                                fw.deps(pe, kres + [R_qT[hs][0]], [R_pb[sbk]])
                                for h in range(NH):
                                    for j in range(nj):
                                        if cache:
                                            lk = kcT[c * 64:(c + 1) * 64, sl, h, j * 128:(j + 1) * 128]
                                        else:
                                            lk = kT[c * 64:(c + 1) * 64, h, SEQ:SEQ + DEC]
                                        ins = nc.tensor.matmul(Sv[0:kp_, h, j, :], lk, qT[c * 64:(c + 1) * 64, hs, h, 0:DEC],
                                                               start=True, stop=True)
                                pe.count += 1
                                ins.then_inc(pe.sem, 1)
                                fw.commit((pe.sem, pe.count), kres + [R_qT[hs][0]], [R_pb[sbk]])
                            for c in range(2):
                                sbk = banks[c]
                                Sv2 = pb[sbk][:, 0:256].rearrange("p (h j q) -> p h j q", h=4, j=2)
                                fw.op(act, lambda: nc.scalar.activation(PTs[0:kp_, sl, c, :, 0:nj, :], Sv2[0:kp_, :, 0:nj, :],
                                                                        AF.Exp, scale=SCALE),
                                      reads=[R_pb[sbk]], writes=[R_PTs[sl]])

                        def emit_PV(b_):
                            cache, sl, nj, kp_ = blk(b_)
                            vres = [R_vcs[sl]] if cache else [R_V[16]]
                            for bi, grp in enumerate([(0, 1, 2), (3, 4, 5), (6, 7)]):
                                ob = 2 + bi
                                rds = vres + [R_PTs[sl], R_c]
                                fw.deps(pe, rds, [R_pb[ob]])
                                for gi_, hc_ in enumerate(grp):
                                    h, c = hc_ // 2, hc_ % 2
                                    if cache:
                                        for j in range(nj):
                                            ins = nc.tensor.matmul(pb[ob][0:DEC, gi_ * 129:gi_ * 129 + 128], PTs[0:kp_, sl, c, h, j, :],
                                                                   vcs[0:kp_, sl, j, h * 128:(h + 1) * 128], start=(j == 0), stop=(j == nj - 1))
                                        for j in range(nj):
                                            ins = nc.tensor.matmul(pb[ob][0:DEC, gi_ * 129 + 128:gi_ * 129 + 129], PTs[0:kp_, sl, c, h, j, :],
                                                                   onesb[0:kp_, 0:1], start=(j == 0), stop=(j == nj - 1))
                                    else:
                                        ins = nc.tensor.matmul(pb[ob][0:DEC, gi_ * 129:(gi_ + 1) * 129], PTs[0:kp_, sl, c, h, 0, :],
                                                               Vx[0:kp_, 16, h, :], start=True, stop=True)
                                pe.count += 1
                                ins.then_inc(pe.sem, 1)
                                fw.commit((pe.sem, pe.count), rds, [R_pb[ob]])
                                ng = len(grp)
                                oav = oacc[:, grp[0]:grp[0] + ng, :].rearrange("p g e -> p (g e)")
                                fw.op(dve, lambda: nc.vector.tensor_tensor(oav, oav, pb[ob][0:DEC, 0:ng * 129], ALU.add),
                                      reads=[R_pb[ob]] + R_oacc, writes=R_oacc)

                        NB = NCB + 1
                        emit_T(0)
                        emit_S(0)
                        for b_ in range(NB):
                            if b_ + 1 < NB:
                                emit_T(b_ + 1)
                                if b_ + 2 < NCB:
                                    load_ck(b_ + 2)
                                emit_S(b_ + 1)
                            emit_PV(b_)
                            if b_ + 2 < NCB:
                                load_cv(b_ + 2)
                        for h in range(NH):
                            epilogue(DEC, h, BV(oacc[:, 2 * h, :], R_oacc), BV(oacc[:, 2 * h + 1, :], R_oacc), 0)
                        flush_fin()
                        transpose4(lambda h: yat[0:DEC, 0, h * 128:(h + 1) * 128], DEC, mixT[:, 0:4, 0:DEC],
                                   R_yat[0], R_mx[0], mbank(), dve)
                        return [R_mx[0]]

                    def tail(ti, attn_res):
                        t0, n, hs, is_s, nsub = tinfo(ti)
                        for ch in range(2):
                            b = mbank()
                            fproj(ti, wbc, ch, b)
                            fw.op(dve, lambda: nc.vector.tensor_tensor(mixT[:, 6 + ch, 0:n], pb[b][:, 0:n], cvt[:, ch, 0:n], ALU.mult),
                                  reads=[R_pb[b], R_cvt[ch]], writes=[R_mx[6 + ch]])
                        for ch in range(2):
                            b = mbank()
                            fw.op(pe, lambda: nc.tensor.matmul(pb[b][:, 0:n], pwb[:, l, ch, :], dbf[:, ch, 0:n], start=True, stop=True),
                                  reads=[R_pw, R_dbf[ch]], writes=[R_pb[b]])
                            fw.op(dve, lambda: nc.vector.tensor_scalar(mixT[:, 4 + ch, 0:n], pb[b][:, 0:n], ppc(PP_PS + l * 2 + ch), None, ALU.mult),
                                  reads=[R_pb[b], R_pp], writes=[R_mx[4 + ch]])
                        for dco in range(8):
                            b = mbank()
                            mm_group(pb[b][:, 0:n], [(wo[:, mc, dco * 128:(dco + 1) * 128], mixT[:, mc, 0:n]) for mc in range(8)],
                                     R_wo + attn_res + R_mx[4:8], [R_pb[b]])
                            xr = RX(t0, n, [dco])
                            fw.op(dve, lambda: nc.vector.tensor_tensor(xT[:, dco, t0:t0 + n], pb[b][:, 0:n], xT[:, dco, t0:t0 + n], ALU.add),
                                  reads=[R_pb[b]] + xr, writes=xr)

                    for _ in qchain(0):
                        pass
                    for ti in range(len(M2T)):
                        early(ti)
                        steps = qchain(ti + 1) if ti + 1 < len(M2T) else None
                        if ti < 8:
                            ares = attention_prompt(ti, steps)
                        else:
                            ares = attention_sample(ti)
                        tail(ti, ares)
                    fw.barrier()

        stage(0)
        for l in range(DEPTH):
            if stopped["v"]:
                break
            ffn(l, 0, l * 3 + 0, 2 if l == 0 else 1, prefetch=lambda: mixer_preload(l), xload=(l == 0))
            if stage(1 + 10 * l):
                break
            mixer(l)
            if stage(5 + 10 * l):
                break
            ffn(l, 1, l * 3 + 2, 0, prefetch=(lambda: ffn_load(l + 1, 0, 0)) if l + 1 < DEPTH else None)
            if stage(6 + 10 * l):
                break
        if stopped["v"]:
            fw.finish()
            return nc

        with ExitStack() as ph:
            nb = norm_bufs(ph)
            yf = sb(ph, "yf", (128, 2, 8, 512), F32)
            R_yf = [Res("yf0"), Res("yf1")]
            yst = sb(ph, "yst", (128, 2, D), F32)
            R_yst = [Res("yst0"), Res("yst1")]
            FTL = [(0, 512), (512, 512), (1024, 512), (1536, 512), (2048, 32)]

            def fin_out(ti, sub):
                t0, nt = FTL[ti]
                ks = ti % 2
                s = t0 // 128 + sub
                n = min(128, nt - sub * 128)
                k = s % 2
                for hb in range(2):
                    bank = (2 * s + hb) % 4
                    fw.deps(pe, [R_yf[ks], R_c], [R_pb[bank]])
                    for q in range(4):
                        dc = hb * 4 + q
                        ins = nc.tensor.transpose(pb[bank][0:n, q * 128:(q + 1) * 128], yf[:, ks, dc, sub * 128:sub * 128 + n], idf[:])
                    pe.count += 1
                    ins.then_inc(pe.sem, 1)
                    fw.commit((pe.sem, pe.count), [R_yf[ks], R_c], [R_pb[bank]])
                    if hb == 0:
                        fw.op(act, lambda: nc.scalar.copy(yst[0:n, k, 0:512], pb[bank][0:n, :]), reads=[R_pb[bank]], writes=[R_yst[k]])
                    else:
                        fw.op(dve, lambda: nc.vector.tensor_copy(yst[0:n, k, 512:1024], pb[bank][0:n, :]), reads=[R_pb[bank]], writes=[R_yst[k]])
                dst = yp_d[s * 128:(s + 1) * 128, :] if s < 16 else ys_d
                fw.dma(sp, dst, yst[0:n, k, :], R_yst[k], reads=[R_yst[k]])

            def fin_rms(ti):
                t0, nt = FTL[ti]
                ks = ti % 2
                rms_tile(nb, 6, t0, nt, lambda dc: yf[:, ks, dc, 0:nt], R_yf[ks], 6 + ti % 2, sq_eng="mix")

            fin_rms(0)
            for ti, (t0, nt) in enumerate(FTL):
                if ti + 1 < len(FTL):
                    fin_rms(ti + 1)
                for sub in range((nt + 127) // 128):
                    fin_out(ti, sub)
            fw.finish()
    return nc


def _param_pack(p):
    pp = np.zeros((128, NPP), np.float32)
    gains = [p["norm_ffn1"][0], p["norm_mix"][0], p["norm_ffn2"][0],
             p["norm_ffn1"][1], p["norm_mix"][1], p["norm_ffn2"][1], p["final_norm"]]
    for gi, g in enumerate(gains):
        pp[:, PP_G + gi * 8:PP_G + gi * 8 + 8] = np.asarray(g, np.float32).reshape(8, 128).T
    for l in range(DEPTH):
        pp[:, PP_PS + l * 2:PP_PS + l * 2 + 2] = np.asarray(p["pool_scale"][l], np.float32).reshape(2, 128).T
        cw = np.asarray(p["conv_w"][l], np.float32)
        for ch in range(2):
            pp[:, PP_CW + l * 6 + ch * 3:PP_CW + l * 6 + ch * 3 + 3] = cw[:, ch * 128:(ch + 1) * 128].T
        pp[:, PP_SUB + l * 128:PP_SUB + (l + 1) * 128] = np.asarray(p["subln"][l], np.float32)[None, :]
        for j, nm in enumerate(["lambda_q1", "lambda_k1", "lambda_q2", "lambda_k2"]):
            pp[:, PP_LAM + l * 256 + j * 64:PP_LAM + l * 256 + (j + 1) * 64] = np.asarray(p[nm][l], np.float32)[None, :]
    half = 32
    inv = np.power(np.float32(10000.0), -np.arange(half, dtype=np.float32) / np.float32(half)).astype(np.float32)
    pos = np.concatenate([np.arange(SEQ), PAST + np.arange(DEC)]).astype(np.float32)
    ang = (pos[:, None] * inv[None, :]).astype(np.float32)
    cos = np.cos(ang).astype(np.float32)
    sin = np.sin(ang).astype(np.float32)
    for s in range(NSUB):
        n = 128 if s < 16 else DEC
        pp[0:n, PP_COS + s * 32:PP_COS + (s + 1) * 32] = cos[s * 128:s * 128 + n]
        pp[0:n, PP_SIN + s * 32:PP_SIN + (s + 1) * 32] = sin[s * 128:s * 128 + n]
    wins = [2, 4, 8, 16]
    for ch in range(2):
        for half_i in range(2):
            w = wins[ch * 2 + half_i]
            for t in range(16):
                pp[half_i * 64:(half_i + 1) * 64, PP_IC + ch * 16 + t] = 1.0 / min(t + 1, w)
    return pp


_NC_CACHE = {}


def kernel(**inputs):
    p = {k: np.asarray(v) for k, v in inputs.items()}
    if "nc" not in _NC_CACHE:
        _NC_CACHE["nc"] = build_program()
    nc = _NC_CACHE["nc"]
    pp = _param_pack(p)
    pw = np.asarray(p["pool_w"], np.float32)
    pwbd = np.zeros((DEPTH, 2, 128, 128), np.float32)
    for l in range(DEPTH):
        for g in range(4):
            ch, o = g // 2, (g % 2) * 64
            pwbd[l, ch, o:o + 64, o:o + 64] = pw[l, g]
    shared = {
        "pp": pp, "pwbd": pwbd,
        "f1g": np.ascontiguousarray(p["ffn1_gate"], np.float32), "f1u": np.ascontiguousarray(p["ffn1_up"], np.float32),
        "f1d": np.ascontiguousarray(p["ffn1_down"], np.float32),
        "f2g": np.ascontiguousarray(p["ffn2_gate"], np.float32), "f2u": np.ascontiguousarray(p["ffn2_up"], np.float32),
        "f2d": np.ascontiguousarray(p["ffn2_down"], np.float32),
        "win": np.ascontiguousarray(p["w_in"], np.float32), "wout": np.ascontiguousarray(p["w_out"], np.float32),
    }
    in_maps = []
    for b in range(8):
        m = dict(shared)
        m["xp"] = np.ascontiguousarray(p["x_prompt"][b], np.float32)
        m["xs"] = np.ascontiguousarray(p["x_sample"][b], np.float32)
        m["ck"] = np.ascontiguousarray(p["cache_k"][:, b].reshape(DEPTH, PAST, 512), np.float32)
        m["cv"] = np.ascontiguousarray(p["cache_v"][:, b].reshape(DEPTH, PAST, 512), np.float32)
        m["stp"] = np.ascontiguousarray(p["state_pool"][:, b], np.float32)
        m["stc"] = np.ascontiguousarray(p["state_conv"][:, b], np.float32)
        in_maps.append(m)
    ncores = int(os.environ.get("KCORES", "8"))
    res = run_bass_kernel_spmd(nc, in_maps[:ncores], core_ids=list(range(ncores)))
    R = list(res.results)
    while len(R) < 8:
        R.append(R[0])

    def gather(name, shape_tail, axis_b):
        arrs = [np.asarray(R[b][name], np.float32) for b in range(8)]
        return np.stack(arrs, axis=axis_b)

    y_prompt = gather("yp", None, 0)
    y_sample = gather("ys", None, 0)
    k_prompt = gather("kp", None, 1).reshape(DEPTH, 8, SEQ, NH, 128)
    v_prompt = gather("vp", None, 1).reshape(DEPTH, 8, SEQ, NH, 128)
    pool_prompt = gather("plp", None, 1)
    conv_prompt = gather("cvp", None, 1)
    k_sample = gather("ks", None, 1).reshape(DEPTH, 8, DEC, NH, 128)
    v_sample = gather("vs", None, 1).reshape(DEPTH, 8, DEC, NH, 128)
    pool_sample = gather("pls", None, 1)
    conv_sample = gather("cvs", None, 1)
    return (y_prompt, y_sample, k_prompt, v_prompt, pool_prompt, conv_prompt,
            k_sample, v_sample, pool_sample, conv_sample)
```
